# Optimizing a Trainium2 kernel written in Bass

```python
import math
import jax, jax.numpy as jnp
from jax import lax
import numpy as np

D_MODEL = 2048
BATCH = 4
SEQ = 2048
DEPTH = 2

GRID_W = 64
CTX_LEN = 256
SSM_WIDTH = D_MODEL // 2
SSM_GROUP = 16
SSM_GROUPS = SSM_WIDTH // SSM_GROUP
SSM_STATE = 64
FFT_WIDTH = D_MODEL - SSM_WIDTH
FFT_GROUPS = 4
FFT_GROUP = FFT_WIDTH // FFT_GROUPS
N_BRANCH = 2
IN_WIDTH = SSM_WIDTH + FFT_WIDTH + N_BRANCH * D_MODEL
D_FF = 4 * D_MODEL
ALPHA = (2 * DEPTH) ** 0.25
BETA = (8 * DEPTH) ** -0.25
LN_EPS = 1e-5
DT_MIN = 1e-3
DT_MAX = 1e-1
POS_BASE = 10000.0

kernel_name = "hybrid_s5_fnet_dit_trunk"


def ln_plain(x):
    xf = x.astype(jnp.float32)
    mu = jnp.mean(xf, axis=-1, keepdims=True)
    xc = xf - mu
    var = jnp.mean(xc * xc, axis=-1, keepdims=True)
    return (xc * lax.rsqrt(var + LN_EPS)).astype(x.dtype)


def ln_affine(x, g, b):
    return ln_plain(x) * g + b


def adaln(cond, w_mod, b_mod):
    m = (jax.nn.silu(cond) @ w_mod + b_mod)[..., None, :]
    return jnp.split(m, 6, axis=-1)


def modulate(h, shift, scale):
    return h * (1.0 + scale) + shift


def grid_sincos(rows, dim):
    quarter = dim // 4
    omega = 1.0 / (POS_BASE ** (jnp.arange(quarter, dtype=jnp.float32) / quarter))
    row = jnp.repeat(jnp.arange(rows, dtype=jnp.float32), GRID_W)
    col = jnp.tile(jnp.arange(GRID_W, dtype=jnp.float32), rows)
    ar = row[:, None] * omega
    ac = col[:, None] * omega
    return jnp.concatenate([jnp.sin(ar), jnp.cos(ar), jnp.sin(ac), jnp.cos(ac)], axis=-1)


def zoh_discretise(lam_re, lam_im, log_step):
    dt = jnp.exp(log_step)[..., None]
    mag = jnp.exp(lam_re * dt)
    ang = lam_im * dt
    abar_re, abar_im = mag * jnp.cos(ang), mag * jnp.sin(ang)
    den = lam_re * lam_re + lam_im * lam_im
    nr, ni = abar_re - 1.0, abar_im
    coef_re = (nr * lam_re + ni * lam_im) / den
    coef_im = (ni * lam_re - nr * lam_im) / den
    return abar_re, abar_im, coef_re, coef_im


def ssm_drive(u, b_re, b_im, coef_re, coef_im):
    bu_re = jnp.einsum('blgc,gnc->blgn', u, b_re)
    bu_im = jnp.einsum('blgc,gnc->blgn', u, b_im)
    return coef_re * bu_re - coef_im * bu_im, coef_re * bu_im + coef_im * bu_re


def complex_linear_scan(abar_re, abar_im, v_re, v_im, h0=None):
    if h0 is not None:
        h0_re, h0_im = h0
        v_re = v_re.at[:, 0].add(abar_re * h0_re - abar_im * h0_im)
        v_im = v_im.at[:, 0].add(abar_re * h0_im + abar_im * h0_re)
    a_re = jnp.broadcast_to(abar_re, v_re.shape)
    a_im = jnp.broadcast_to(abar_im, v_im.shape)

    def combine(e1, e2):
        a1r, a1i, b1r, b1i = e1
        a2r, a2i, b2r, b2i = e2
        return (a2r * a1r - a2i * a1i, a2r * a1i + a2i * a1r,
                a2r * b1r - a2i * b1i + b2r, a2r * b1i + a2i * b1r + b2i)

    _, _, h_re, h_im = lax.associative_scan(combine, (a_re, a_im, v_re, v_im), axis=1)
    return h_re, h_im


def ssm_readout(h_re, h_im, c_re, c_im):
    return jnp.einsum('blgn,gcn->blgc', h_re, c_re) - jnp.einsum('blgn,gcn->blgc', h_im, c_im)


def _flip(t, rev):
    return t[:, ::-1] if rev else t


def bidirectional_s5(u_ctx, u_lat, need_ctx, lam_re, lam_im, log_step, b_re, b_im, c_re, c_im, d_skip):
    f32 = jnp.float32
    dtype = u_lat.dtype
    bsz, n_lat, _ = u_lat.shape
    n_ctx = u_ctx.shape[1]
    uc = u_ctx.astype(f32).reshape(bsz, n_ctx, SSM_GROUPS, SSM_GROUP)
    ul = u_lat.astype(f32).reshape(bsz, n_lat, SSM_GROUPS, SSM_GROUP)
    abar_re, abar_im, coef_re, coef_im = zoh_discretise(lam_re.astype(f32), lam_im.astype(f32), log_step.astype(f32))
    dsk = d_skip.astype(f32)
    y_lat = dsk * u_lat.astype(f32)
    y_ctx = dsk * u_ctx.astype(f32) if need_ctx else None
    for d in range(2):
        rev = d == 1
        br, bi = b_re[d].astype(f32), b_im[d].astype(f32)
        cr, ci = c_re[d].astype(f32), c_im[d].astype(f32)
        hc = complex_linear_scan(abar_re[d], abar_im[d], *ssm_drive(_flip(uc, rev), br, bi, coef_re[d], coef_im[d]))
        hl = complex_linear_scan(abar_re[d], abar_im[d], *ssm_drive(_flip(ul, rev), br, bi, coef_re[d], coef_im[d]),
                                 h0=(hc[0][:, -1], hc[1][:, -1]))
        y_lat = y_lat + _flip(ssm_readout(hl[0], hl[1], cr, ci), rev).reshape(bsz, n_lat, SSM_WIDTH)
        if need_ctx:
            y_ctx = y_ctx + _flip(ssm_readout(hc[0], hc[1], cr, ci), rev).reshape(bsz, n_ctx, SSM_WIDTH)
    return y_lat.astype(dtype), (y_ctx.astype(dtype) if need_ctx else None)


def s5_glu(y, w_glu):
    g = jax.nn.gelu(y)
    return g * jax.nn.sigmoid(g @ w_glu)


def fourier_mix(u):
    bsz, n, _ = u.shape
    g = u.astype(jnp.float32).reshape(bsz, n, FFT_GROUPS, FFT_GROUP)
    f = jnp.fft.fft2(g, axes=(1, 3), norm="ortho").real
    return f.reshape(bsz, n, FFT_WIDTH).astype(u.dtype)


def branch_merge(y_s, y_f, gate_s, gate_f, w_ps, w_pf, w_o):
    merged = jax.nn.sigmoid(gate_s) * (y_s @ w_ps) + jax.nn.sigmoid(gate_f) * (y_f @ w_pf)
    return merged @ w_o


def hybrid_mixer(h_lat, h_ctx, need_ctx, w_in, lam_re, lam_im, log_step, b_re, b_im, c_re, c_im, d_skip,
                 w_glu, w_ps, w_pf, w_o):
    cuts = (SSM_WIDTH, SSM_WIDTH + FFT_WIDTH, SSM_WIDTH + FFT_WIDTH + D_MODEL)
    us_lat, uf_lat, gs_lat, gf_lat = jnp.split(h_lat @ w_in, cuts, axis=-1)
    if need_ctx:
        us_ctx, uf_ctx, gs_ctx, gf_ctx = jnp.split(h_ctx @ w_in, cuts, axis=-1)
    else:
        us_ctx = h_ctx @ w_in[:, :SSM_WIDTH]
    ys_lat, ys_ctx = bidirectional_s5(us_ctx, us_lat, need_ctx, lam_re, lam_im, log_step, b_re, b_im, c_re, c_im, d_skip)
    out_lat = branch_merge(s5_glu(ys_lat, w_glu), fourier_mix(uf_lat), gs_lat, gf_lat, w_ps, w_pf, w_o)
    if not need_ctx:
        return out_lat, None
    out_ctx = branch_merge(s5_glu(ys_ctx, w_glu), fourier_mix(uf_ctx), gs_ctx, gf_ctx, w_ps, w_pf, w_o)
    return out_lat, out_ctx


def sq_relu_mlp(h, w_up, w_down):
    a = jax.nn.relu(h @ w_up)
    return (a * a) @ w_down


def setup_inputs(seed: int = 0) -> dict:
    key = jax.random.key(seed)
    ks = jax.random.split(key, 32)
    L, G, N, C = DEPTH, SSM_GROUPS, SSM_STATE, SSM_GROUP
    f32 = jnp.float32

    def nrm(k, shape, s):
        return jax.random.normal(k, shape, f32) * s

    n_idx = jnp.arange(N, dtype=f32)
    return {
        "x": nrm(ks[0], (BATCH, SEQ, D_MODEL), 1.0),
        "c": nrm(ks[1], (BATCH, D_MODEL), 1.0),
        "ctx": nrm(ks[2], (BATCH, CTX_LEN, D_MODEL), 1.0),
        "c_ctx": nrm(ks[3], (D_MODEL,), 1.0),
        "w_mod": nrm(ks[4], (L, D_MODEL, 6 * D_MODEL), D_MODEL ** -0.5),
        "b_mod": nrm(ks[5], (L, 6 * D_MODEL), 0.02),
        "w_in": nrm(ks[6], (L, D_MODEL, IN_WIDTH), D_MODEL ** -0.5),
        "lam_re": -0.5 + nrm(ks[7], (L, 2, G, N), 0.01),
        "lam_im": math.pi * n_idx + nrm(ks[8], (L, 2, G, N), 0.01),
        "log_step": jax.random.uniform(ks[9], (L, 2, G), f32, math.log(DT_MIN), math.log(DT_MAX)),
        "ssm_b_re": nrm(ks[10], (L, 2, G, N, C), (2 * C) ** -0.5),
        "ssm_b_im": nrm(ks[11], (L, 2, G, N, C), (2 * C) ** -0.5),
        "ssm_c_re": nrm(ks[12], (L, 2, G, C, N), N ** -0.5),
        "ssm_c_im": nrm(ks[13], (L, 2, G, C, N), N ** -0.5),
        "d_skip": nrm(ks[14], (L, SSM_WIDTH), 1.0),
        "w_glu": nrm(ks[15], (L, SSM_WIDTH, SSM_WIDTH), SSM_WIDTH ** -0.5),
        "w_ps": nrm(ks[16], (L, SSM_WIDTH, D_MODEL), SSM_WIDTH ** -0.5),
        "w_pf": nrm(ks[17], (L, FFT_WIDTH, D_MODEL), FFT_WIDTH ** -0.5),
        "w_o": nrm(ks[18], (L, D_MODEL, D_MODEL), BETA * D_MODEL ** -0.5),
        "ln1_g": 1.0 + nrm(ks[19], (L, D_MODEL), 0.02),
        "ln1_b": nrm(ks[20], (L, D_MODEL), 0.02),
        "w_up": nrm(ks[21], (L, D_MODEL, D_FF), D_MODEL ** -0.5),
        "w_down": nrm(ks[22], (L, D_FF, D_MODEL), BETA * D_FF ** -0.5),
        "ln2_g": 1.0 + nrm(ks[23], (L, D_MODEL), 0.02),
        "ln2_b": nrm(ks[24], (L, D_MODEL), 0.02),
    }


def reference(x, c, ctx, c_ctx, w_mod, b_mod, w_in, lam_re, lam_im, log_step, ssm_b_re, ssm_b_im,
              ssm_c_re, ssm_c_im, d_skip, w_glu, w_ps, w_pf, w_o, ln1_g, ln1_b, w_up, w_down, ln2_g, ln2_b):
    n_lat = x.shape[1]
    rows = n_lat // GRID_W
    x = x + grid_sincos(rows, D_MODEL).astype(x.dtype)[None]
    x_ctx = ctx
    for l in range(DEPTH):
        need_ctx = l < DEPTH - 1
        m_lat = adaln(c, w_mod[l], b_mod[l])
        m_ctx = adaln(c_ctx[None], w_mod[l], b_mod[l])
        h_lat = modulate(ln_plain(x), m_lat[0], m_lat[1])
        h_ctx = modulate(ln_plain(x_ctx), m_ctx[0], m_ctx[1])
        mix_lat, mix_ctx = hybrid_mixer(h_lat, h_ctx, need_ctx, w_in[l], lam_re[l], lam_im[l], log_step[l],
                                        ssm_b_re[l], ssm_b_im[l], ssm_c_re[l], ssm_c_im[l], d_skip[l],
                                        w_glu[l], w_ps[l], w_pf[l], w_o[l])
        x = ln_affine(ALPHA * x + m_lat[2] * mix_lat, ln1_g[l], ln1_b[l])
        h2 = modulate(ln_plain(x), m_lat[3], m_lat[4])
        x = ln_affine(ALPHA * x + m_lat[5] * sq_relu_mlp(h2, w_up[l], w_down[l]), ln2_g[l], ln2_b[l])
        if need_ctx:
            x_ctx = ln_affine(ALPHA * x_ctx + m_ctx[2] * mix_ctx, ln1_g[l], ln1_b[l])
            h2c = modulate(ln_plain(x_ctx), m_ctx[3], m_ctx[4])
            x_ctx = ln_affine(ALPHA * x_ctx + m_ctx[5] * sq_relu_mlp(h2c, w_up[l], w_down[l]), ln2_g[l], ln2_b[l])
    return x
```

```python
import contextlib
import math
import numpy as np
import ml_dtypes
import concourse.bass as bass
import concourse.mybir as mybir
from concourse.bass_utils import run_bass_kernel_spmd

F32 = mybir.dt.float32
BF16 = mybir.dt.bfloat16
I32 = mybir.dt.int32
AF = mybir.ActivationFunctionType
ALU = mybir.AluOpType
NPBF = ml_dtypes.bfloat16

D = 2048
NB = 4
SEQ = 2048
CTX = 256
DEPTH = 2
DFF = 8192
ALPHA = (2 * DEPTH) ** 0.25
EPS = 1e-5
NCORES = 8
TWO_PI = 2.0 * math.pi

ENGS = ("pe", "act", "dve", "pool", "sp")


class Buf:
    __slots__ = ("name", "w", "r")

    default_fence = {}

    def __init__(self, name=""):
        self.name = name
        self.w = {}
        self.r = dict(Buf.default_fence)


class Prog:
    def __init__(self, nc, n_dma_sems=8):
        self.nc = nc
        self.ops = {e: [] for e in ENGS}
        self.cnt = {e: 0 for e in ENGS}
        self.known = {e: {} for e in ENGS}
        self.n_dma_sems = n_dma_sems
        self.dma_rr = {e: 0 for e in ("sp", "act", "pool")}
        self.dma_cnt = {}
        self.stack = contextlib.ExitStack()
        self.sems = {}
        self.all_toks = {}
        self._uid = 0
        self.psum_tiles = None
        self.psum_i = 0
        Buf.default_fence = {}
        self.cc_scratch = self.sbuf("ccscr", [128, 8], F32)
        self.cc_sems = [self.stack.enter_context(self.nc.semaphore(f"cc_sem{i}")) for i in range(18)]
        self.cc_n = 0

    def sbuf(self, name, shape, dt, stack=None):
        self._uid += 1
        st = stack if stack is not None else self.stack
        return st.enter_context(self.nc.sbuf_tensor(f"{name}_{self._uid}", list(shape), dt))

    def psum_init(self):
        self.psum_tiles = []
        for i in range(8):
            t = self.stack.enter_context(self.nc.psum_tensor(f"ps{i}", [128, 512], F32))
            self.psum_tiles.append((t, Buf(f"ps{i}")))

    def psum_next(self):
        t = self.psum_tiles[self.psum_i]
        self.psum_i = (self.psum_i + 1) % 8
        return t

    def _sem(self, key):
        if key not in self.sems:
            nm = "s_" + "_".join(str(k) for k in key)
            self.sems[key] = self.stack.enter_context(self.nc.semaphore(nm))
        return self.sems[key]

    def _deps(self, reads, writes):
        deps = []
        for b in reads:
            deps.extend(b.w.items())
        for b in writes:
            deps.extend(b.w.items())
            deps.extend(b.r.items())
        return deps

    def _mark(self, reads, writes, tok):
        for b in reads:
            if b.r.get(tok[0], 0) < tok[1]:
                b.r[tok[0]] = tok[1]
        for b in writes:
            if b.w.get(tok[0], 0) < tok[1]:
                b.w[tok[0]] = tok[1]
            b.r = {}
        if self.all_toks.get(tok[0], 0) < tok[1]:
            self.all_toks[tok[0]] = tok[1]

    def _waits(self, eng, deps, extra=()):
        need = {}
        for (key, val) in list(deps) + list(extra):
            if val <= 0:
                continue
            if need.get(key, 0) < val:
                need[key] = val
        out = []
        kn = self.known[eng]
        for key, val in need.items():
            if kn.get(key, 0) >= val:
                continue
            kn[key] = val
            out.append((key, val))
        return out

    def op(self, eng, fn, reads=(), writes=()):
        deps = self._deps(reads, writes)
        if eng == "pe":
            deps = [d for d in deps if d[0] != ("c", "pe")]
        waits = self._waits(eng, deps)
        self.cnt[eng] += 1
        tok = (("c", eng), self.cnt[eng])
        self.ops[eng].append((waits, fn, tok))
        self._mark(reads, writes, tok)
        return tok

    def dma(self, eng, out_ap, in_ap, reads=(), writes=(), **kw):
        k = self.dma_rr[eng]
        self.dma_rr[eng] = (k + 1) % self.n_dma_sems
        key = ("d", eng, k)
        n = self.dma_cnt.get(key, 0)
        deps = self._deps(reads, writes)
        waits = self._waits(eng, deps, extra=[(key, 16 * n)])
        self.dma_cnt[key] = n + 1
        tok = (key, 16 * (n + 1))

        def fn(e, out_ap=out_ap, in_ap=in_ap, kw=kw):
            return e.dma_start(out=out_ap, in_=in_ap, **kw)
        self.ops[eng].append((waits, fn, tok))
        self._mark(reads, writes, tok)
        return tok

    def collective(self, kind, ins, outs, groups, reads=(), writes=()):
        key = ("cc", self.cc_n)
        self.sems[key] = self.cc_sems[self.cc_n]
        self.cc_n += 1
        deps = self._deps(reads, writes)
        waits = self._waits("pool", deps)

        def fn(e):
            return e.collective_compute(kind, ALU.bypass, replica_groups=groups,
                                        ins=[a.opt() for a in ins], outs=[a.opt() for a in outs])
        self.ops["pool"].append((waits, fn, (key, 1)))
        self.known["pool"][key] = 1
        self.ops["pool"].append(([(key, 1)], None, None))
        scr = self.cc_scratch
        return self.op("pool", lambda e: e.memset(scr[:], 0.0), reads, writes)

    def phase(self):
        Buf.default_fence = dict(self.all_toks)

    def fence(self):
        return dict(self.all_toks)

    def newbuf(self, name="", fence=None):
        b = Buf(name)
        if fence:
            b.r.update(fence)
        return b

    def finish(self):
        waits = self._waits("sp", list(self.all_toks.items()))
        self.ops["sp"].append((waits, None, None))
        nc = self.nc
        for e in ENGS:
            self._sem(("c", e))
        for e in ("sp", "act", "pool"):
            for k in range(self.n_dma_sems):
                self._sem(("d", e, k))
        for e in ENGS:
            for waits_, fn_, tok_ in self.ops[e]:
                if tok_ is not None:
                    self._sem(tok_[0])
        engmap = {"pe": "tensor", "act": "scalar", "dve": "vector", "pool": "gpsimd", "sp": "sync"}
        with nc.Block() as block:
            for e in ENGS:
                ops = self.ops[e]

                def body(engine, ops=ops):
                    for waits, fn, tok in ops:
                        for key, val in waits:
                            engine.wait_ge(self._sem(key), val)
                        if fn is None:
                            continue
                        ins = fn(engine)
                        key, val = tok
                        (ins.then_inc(self._sem(key)) if key[0] == "cc" else ins.then_inc(self._sem(key), 1 if key[0] == "c" else 16))
                getattr(block, engmap[e])(body)
        self.stack.close()


def new_nc():
    return bass.Bass("TRN2", target_bir_lowering=False)


def din(nc, name, shape, dt=F32):
    return nc.dram_tensor(name, list(shape), dt, kind="ExternalInput").ap()


def dout(nc, name, shape, dt=F32):
    return nc.dram_tensor(name, list(shape), dt, kind="ExternalOutput").ap()


def pipeline(jobs, depth):
    handles = {}
    for i in range(min(depth, len(jobs))):
        handles[i] = jobs[i][0]()
    for i in range(len(jobs)):
        jobs[i][1](handles.pop(i))
        nxt = i + depth
        if nxt < len(jobs):
            handles[nxt] = jobs[nxt][0]()


class WStream:
    def __init__(self, P, nbuf, kc=16, ncol=256, stack=None):
        self.P = P
        self.tiles = [(P.sbuf("wslab", [128, kc, ncol], BF16, stack), Buf("wslab")) for _ in range(nbuf)]
        self.i = 0

    def load(self, w2d, r0, nk, c0, ncol):
        t, b = self.tiles[self.i]
        self.i = (self.i + 1) % len(self.tiles)
        src = w2d[r0:r0 + nk * 128, c0:c0 + ncol].rearrange("(kc p) n -> p kc n", p=128)
        self.P.dma("pool", t[:, 0:nk, 0:ncol], src, writes=[b])
        return t, b


def token_tiles(T):
    if T == 1024:
        return [(0, 512, [(0, 512, 0)]), (512, 512, [(512, 512, 0)])]
    assert T == 1152
    return [(0, 384, [(0, 384, 0)]), (384, 384, [(384, 384, 0)]), (768, 384, [(768, 256, 0), (1024, 128, 1)])]


def mm(P, out, lhsT, rhs, start, stop, reads, writes):
    return P.op("pe", lambda e: e.matmul(out, lhsT=lhsT, rhs=rhs, start=start, stop=stop), reads, writes)


def act(P, out, in_, func, reads, writes, bias=None, scale=None):
    kw = {}
    if bias is not None:
        kw["bias"] = bias
    if scale is not None:
        kw["scale"] = scale
    return P.op("act", lambda e: e.activation(out=out, in_=in_, func=func, **kw), reads, writes)


def tt(P, eng, out, in0, in1, op, reads, writes):
    return P.op(eng, lambda e: e.tensor_tensor(out=out, in0=in0, in1=in1, op=op), reads, writes)


def ts(P, eng, out, in0, s1, s2, op0, op1, reads, writes):
    if s2 is None:
        return P.op(eng, lambda e: e.tensor_scalar(out=out, in0=in0, scalar1=s1, scalar2=None, op0=op0), reads, writes)
    return P.op(eng, lambda e: e.tensor_scalar(out=out, in0=in0, scalar1=s1, scalar2=s2, op0=op0, op1=op1), reads, writes)


def stt(P, eng, out, in0, scalar, in1, op0, op1, reads, writes):
    return P.op(eng, lambda e: e.scalar_tensor_tensor(out=out, in0=in0, scalar=scalar, in1=in1, op0=op0, op1=op1),
                reads, writes)


def cp(P, eng, out, in_, reads, writes):
    return P.op(eng, lambda e: e.tensor_copy(out=out, in_=in_), reads, writes)


def emit_ln_stats(P, x, xb, tiles, C):
    T = x.shape[2]
    mean = P.sbuf("mean", [128, T], F32)
    rstd = P.sbuf("rstd", [128, T], F32)
    sq = [(P.sbuf("sq", [128, 384], F32), Buf("sq")) for _ in range(3)]
    tmp = P.sbuf("lntmp", [128, 384], F32)
    tmpb = Buf("lntmp")
    sb = []
    for ti, (t0, tn, _) in enumerate(tiles):
        ps1, pb1 = P.psum_next()
        ps2, pb2 = P.psum_next()
        for c in range(16):
            xc, xcb = S["xc"][c % 3]
            if c % 2:
                cp(P, "dve", xc[:, 0:tn], x[:, c, t0:t0 + tn], [xb[c][ti]], [xcb])
            else:
                act(P, xc[:, 0:tn], x[:, c, t0:t0 + tn], AF.Copy, [xb[c][ti]], [xcb])
            mm(P, ps1[:, 0:tn], C["onesb"][:], xc[:, 0:tn], c == 0, c == 15, [xcb, C["b"]], [pb1])
            s, sbf = sq[c % 3]
            act(P, s[:, 0:tn], x[:, c, t0:t0 + tn], AF.Square, [xb[c][ti]], [sbf])
            mm(P, ps2[:, 0:tn], C["ones"][:], s[:, 0:tn], c == 0, c == 15, [sbf, C["b"]], [pb2])
        mb = Buf("mean")
        ts(P, "dve", mean[:, t0:t0 + tn], ps1[:, 0:tn], 1.0 / D, None, ALU.mult, None, [pb1], [mb])
        tt(P, "dve", tmp[:, 0:tn], mean[:, t0:t0 + tn], mean[:, t0:t0 + tn], ALU.mult, [mb], [tmpb])
        stt(P, "dve", tmp[:, 0:tn], ps2[:, 0:tn], 1.0 / D, tmp[:, 0:tn], ALU.mult, ALU.subtract, [pb2, tmpb], [tmpb])
        ts(P, "dve", tmp[:, 0:tn], tmp[:, 0:tn], EPS, None, ALU.add, None, [tmpb], [tmpb])
        act(P, tmp[:, 0:tn], tmp[:, 0:tn], AF.Ln, [tmpb], [tmpb])
        act(P, rstd[:, t0:t0 + tn], tmp[:, 0:tn], AF.Exp, [tmpb], [mb], scale=-0.5)
        sb.append(mb)
    return mean, rstd, sb


def emit_norm_affine(P, x, xb, out, outb, mean, rstd, sb, tiles, scale_fn, bias_fn, tmps):
    k = 0
    for ti, (t0, tn, segs) in enumerate(tiles):
        for c in range(16):
            tmp, tb = tmps[k % len(tmps)]
            k += 1
            tt(P, "dve", tmp[:, 0:tn], x[:, c, t0:t0 + tn], mean[:, t0:t0 + tn], ALU.subtract, [xb[c][ti], sb[ti]], [tb])
            tt(P, "dve", tmp[:, 0:tn], tmp[:, 0:tn], rstd[:, t0:t0 + tn], ALU.mult, [tb, sb[ti]], [tb])
            act(P, out[:, c, t0:t0 + tn], tmp[:, 0:tn], AF.Identity, [tb], [outb[c][ti]],
                bias=bias_fn(c, mi), scale=scale_fn(c, mi))


def make_consts(P, nc):
    ones = P.sbuf("ones", [128, 128], F32)
    b = Buf("ones")
    P.op("dve", lambda e: e.memset(ones[:], 1.0), writes=[b])
    onesb = P.sbuf("onesb", [128, 128], BF16)
    P.op("dve", lambda e: e.memset(onesb[:], 1.0), writes=[b])
    return {"ones": ones, "onesb": onesb, "b": b}


def grid(n, m, name):
    return [[Buf(f"{name}{i}_{j}") for j in range(m)] for i in range(n)]


def ln_scratch(P, st=None):
    return {
        "mean": P.sbuf("mean", [128, 1152], F32, st), "rstd": P.sbuf("rstd", [128, 1152], F32, st),
        "mbs": [Buf("meanb") for _ in range(3)],
        "sq": [(P.sbuf("sq", [128, 512], BF16, st), Buf("sq")) for _ in range(3)],
        "xc": [(P.sbuf("xc", [128, 512], BF16, st), Buf("xc")) for _ in range(3)],
        "tmp": P.sbuf("lntmp", [128, 512], F32, st), "tmpb": Buf("lntmp"), "mb": Buf("meanb"),
        "tmps": [(P.sbuf("nt", [128, 512], F32, st), Buf()) for _ in range(2)], "k": 0,
    }


def emit_ln(P, S, C, x, xb, out, outb, tiles, scale_fn, bias_fn, extra=()):
    mean, rstd, tmp, tmpb = S["mean"], S["rstd"], S["tmp"], S["tmpb"]
    for ti, (t0, tn, segs) in enumerate(tiles):
        mb = S["mbs"][ti]
        ps1, pb1 = P.psum_next()
        ps2, pb2 = P.psum_next()
        for c in range(16):
            mm(P, ps1[:, 0:tn], C["ones"][:], x[:, c, t0:t0 + tn], c == 0, c == 15, [xb[c][ti], C["b"]], [pb1])
            sq, sbf = S["sq"][c % 3]
            act(P, sq[:, 0:tn], x[:, c, t0:t0 + tn], AF.Square, [xb[c][ti]], [sbf])
            mm(P, ps2[:, 0:tn], C["onesb"][:], sq[:, 0:tn], c == 0, c == 15, [sbf, C["b"]], [pb2])
        ts(P, "dve", mean[:, t0:t0 + tn], ps1[:, 0:tn], 1.0 / D, None, ALU.mult, None, [pb1], [mb])
        tt(P, "dve", tmp[:, 0:tn], mean[:, t0:t0 + tn], mean[:, t0:t0 + tn], ALU.mult, [mb], [tmpb])
        stt(P, "dve", tmp[:, 0:tn], ps2[:, 0:tn], 1.0 / D, tmp[:, 0:tn], ALU.mult, ALU.subtract, [pb2, tmpb], [tmpb])
        ts(P, "dve", tmp[:, 0:tn], tmp[:, 0:tn], EPS, None, ALU.add, None, [tmpb], [tmpb])
        act(P, tmp[:, 0:tn], tmp[:, 0:tn], AF.Ln, [tmpb], [tmpb])
        act(P, rstd[:, t0:t0 + tn], tmp[:, 0:tn], AF.Exp, [tmpb], [mb], scale=-0.5)
    for ti, (t0, tn, segs) in enumerate(tiles):
        mb = S["mbs"][ti]
        for c in range(16):
            t2, tb = S["tmps"][S["k"] % 2]
            S["k"] += 1
            tt(P, "dve", t2[:, 0:tn], x[:, c, t0:t0 + tn], mean[:, t0:t0 + tn], ALU.subtract, [xb[c][ti], mb], [tb])
            tt(P, "dve", t2[:, 0:tn], t2[:, 0:tn], rstd[:, t0:t0 + tn], ALU.mult, [tb, mb], [tb])
            for (s0, sn, mi) in segs:
                act(P, out[:, c, s0:s0 + sn], t2[:, s0 - t0:s0 - t0 + sn], AF.Identity, [tb] + list(extra), [outb[c][ti]],
                    bias=bias_fn(c, mi), scale=scale_fn(c, mi))


NTOK = CTX + SEQ


def fnet_consts():
    ch = np.arange(256)
    ang = 2 * np.pi * np.outer(ch, ch) / 256.0
    dc = np.concatenate([np.cos(ang), np.sin(ang)], axis=1) / 16.0
    dftc = dc.reshape(2, 128, 512).transpose(1, 0, 2)
    t = np.arange(SEQ)
    angl = 2 * np.pi * (np.outer(t, t) % SEQ) / SEQ
    dpl = np.stack([np.cos(angl), -np.sin(angl)], axis=1) / np.sqrt(SEQ)
    tc = np.arange(CTX)
    angc = 2 * np.pi * (np.outer(tc, tc) % CTX) / CTX
    dpc = np.stack([np.cos(angc), -np.sin(angc)], axis=1) / np.sqrt(CTX)
    dpc = dpc.reshape(2, 128, 2, 256).transpose(1, 0, 2, 3)
    return {"dftc": np.ascontiguousarray(dftc).astype(NPBF), "dpl": np.ascontiguousarray(dpl).astype(NPBF),
            "dpc": np.ascontiguousarray(dpc).astype(NPBF)}


NKK = NTOK // 8
NGB = 4
NPB = 2 * NGB


def s5_consts():
    s = np.arange(8, dtype=np.float32)
    ev = np.zeros((3, 2, 8), np.float32)
    ev[0, 0], ev[0, 1] = s + 1, 8 - s
    ev[1, 0], ev[1, 1] = 7 - s, s
    ev[2, 0], ev[2, 1] = -(s + 1), -(8 - s)
    evec = np.broadcast_to(ev[None], (128, 3, 2, 8)).copy()
    pv = np.zeros((128, 4), np.float32)
    pv[:64, 0] = 0.25
    pv[64:, 1] = 0.25
    pv[:64, 2], pv[64:, 2] = 1.0, -1.0
    pv[:64, 3], pv[64:, 3] = -1.0, 1.0
    si = np.arange(128) // 16
    mf = (si[None, :] >= si[:, None]).astype(np.float32)
    mb = (si[:, None] >= si[None, :]).astype(np.float32)
    masks = np.stack([mf, mb], axis=1)
    ident = np.eye(128, dtype=np.float32)
    jv = np.broadcast_to(np.arange(NKK, dtype=np.float32)[None], (128, NKK)).copy()
    return {"evec": evec, "pvec": pv, "masks": np.ascontiguousarray(masks), "ident": ident, "jv": jv}


def host_s5_params(inp, l, p):
    gs = slice(32 * p, 32 * p + 32)

    def dup(a):
        return np.ascontiguousarray(np.concatenate([a, a], axis=0))
    def bm(a):
        sh = a.shape
        return np.ascontiguousarray(a.reshape(sh[0], 2, 32 // NGB, NGB, *sh[2:]).swapaxes(1, 2).reshape(sh))
    out = {}
    out["lre2"] = dup(inp["lam_re"][l][:, gs, :].transpose(2, 0, 1).reshape(64, 64))
    out["lim2"] = dup(inp["lam_im"][l][:, gs, :].transpose(2, 0, 1).reshape(64, 64))
    out["lst2"] = np.ascontiguousarray(np.broadcast_to(inp["log_step"][l][:, gs].reshape(1, 64), (128, 64)))
    out["bb1"] = dup(inp["ssm_b_re"][l][:, gs].transpose(2, 0, 1, 3).reshape(64, 64, 16))
    out["bb2"] = dup(inp["ssm_b_im"][l][:, gs].transpose(2, 0, 1, 3).reshape(64, 64, 16))
    out["cc1"] = dup(inp["ssm_c_re"][l][:, gs].transpose(3, 0, 1, 2).reshape(64, 64, 16))
    out["cc2"] = dup(inp["ssm_c_im"][l][:, gs].transpose(3, 0, 1, 2).reshape(64, 64, 16))
    for n in ("lre2", "lim2", "lst2", "bb1", "bb2", "cc1", "cc2"):
        out[n] = bm(out[n])
    dsk = inp["d_skip"][l].reshape(64, 16)[gs]
    out["dskp"] = np.ascontiguousarray(np.broadcast_to(dsk.T[None], (8, 16, 32)).reshape(128, 32))
    return out


def pos_table():
    quarter = D // 4
    omega = 1.0 / (10000.0 ** (np.arange(quarter, dtype=np.float32) / np.float32(quarter)))
    row = np.repeat(np.arange(SEQ // 64, dtype=np.float32), 64)
    col = np.tile(np.arange(64, dtype=np.float32), SEQ // 64)
    ar = row[:, None] * omega
    ac = col[:, None] * omega
    return np.concatenate([np.sin(ar), np.cos(ar), np.sin(ac), np.cos(ac)], axis=-1).astype(np.float32)


def core_bp(i):
    return i // 2, i % 2


def fm(a):
    t, f = a.shape
    return np.ascontiguousarray(a.reshape(t, f // 128, 128).transpose(2, 1, 0))


def unfm(a):
    p, c, t = a.shape
    return np.ascontiguousarray(a.transpose(2, 1, 0).reshape(t, c * 128))


def run(nc, in_maps):
    res = run_bass_kernel_spmd(nc, in_maps, core_ids=list(range(NCORES)))
    return res.results


def emit_ks(P, SC, U_all, ubufs, prm, ys_dst, ys_buf, pch, bg=None, pre=None):
    st = contextlib.ExitStack()
    P.phase()
    if pre is not None:
        pre()

    def SB(name, shape, dt):
        return P.sbuf(name, shape, dt, st)
    cb = Buf("consts")
    upk = SB("upk", [128, 32, NKK], BF16)
    q = 0
    for s_ in range(8):
        for ph in range(2):
            for (c0, nk, kk0) in ((0, 128, 32 + ph * 128), (128, 16, ph * 16)):
                src = U_all[ph][512 * pch:512 * pch + 512, s_ * 144 + c0: s_ * 144 + c0 + nk].rearrange("(g c) k -> c g k", c=16)
                P.dma(("sp", "act")[q % 2], upk[s_ * 16:(s_ + 1) * 16, :, kk0:kk0 + nk], src, reads=[ubufs[ph]], writes=[cb])
                q += 1
    names2 = ["lre2", "lim2", "lst2"]
    names3 = ["bb1", "bb2", "cc1", "cc2"]
    t2d, t3d = {}, {}
    for n in names2:
        t2d[n] = SB(n, [128, 64], F32)
        P.dma("sp", t2d[n][:], prm[n], writes=[cb])
    for n in names3:
        t3d[n] = SB(n, [128, 64, 16], F32)
        P.dma("act", t3d[n][:], prm[n], writes=[cb])
    dskp = SB("dskp", [128, 32], F32)
    P.dma("sp", dskp[:], prm["dskp"], writes=[cb])
    evec, pvec, masks, ident, jv = SC["evec"], SC["pvec"], SC["masks"], SC["ident"], SC["jv"]
    cb.w.update(SC["b"].w)
    offA, offB, sgnC, sgnB = pvec[:, 0:1], pvec[:, 1:2], pvec[:, 2:3], pvec[:, 3:4]

    NL = NPB * NKK
    t1 = SB("t1", [128, NPB, NKK], F32)
    t2 = SB("t2", [128, NPB, NKK], F32)
    T1 = t1[:].rearrange("p g k -> p (g k)")
    T2 = t2[:].rearrange("p g k -> p (g k)")
    Gs = SB("Gs", [128, NPB, NKK], F32)
    TI = Gs[:].rearrange("p g k -> p (g k)").bitcast(I32)
    sb_ = Buf("tabscratch")
    S1 = SB("S1", [128, 768], F32)
    S2 = SB("S2", [128, 768], F32)
    SI = SB("SI", [128, 768], I32)
    ssb = Buf("smallscratch")

    def finish_table(out, n, elr, shape_fn, small=False):
        A1, A2, AI, bb = (S1, S2, SI, ssb) if small else (T1, T2, TI, sb_)
        a1, a2, ai = A1[:, 0:n], A2[:, 0:n], AI[:, 0:n]
        cp(P, "dve", ai, a1, [bb], [bb])
        cp(P, "dve", a2, ai, [bb], [bb])
        tt(P, "dve", a1, a1, a2, ALU.subtract, [bb], [bb])
        if elr is None:
            act(P, out, shape_fn(a1), AF.Sin, [bb], out_w, scale=TWO_PI)
        else:
            act(P, a1, a1, AF.Sin, [bb], [bb], scale=TWO_PI)
            act(P, shape_fn(a2), elr, AF.Exp, [bb] + out_w, [bb])
            tt(P, "dve", out, shape_fn(a1), shape_fn(a2), ALU.mult, [bb], out_w)

    smc = {n: SB(n, [128, 64], F32) for n in ["dt", "lr", "th", "c1", "s1", "nr", "den", "cr", "ci", "u1", "u2", "p8", "d8"]}
    smi = SB("smi", [128, 64], I32)
    ANG = SB("ANG", [128, 3, 2, NPB, 8], F32)
    ELR = SB("ELR", [128, 3, 2, NPB, 8], F32)
    TAB = SB("TAB", [128, 3, 2, NPB, 8], F32)
    W = {n: SB(n, [128, 2, NPB, 8], F32) for n in ["WA", "WB", "wu1", "wu2"]}
    big = {n: SB(n, [128, NPB, 8, 16], F32) for n in ["tA", "tB", "BT", "BS", "PT", "Q1"]}
    Q1b = SB("Q1b", [128, NPB, 128], BF16)
    Q2b = SB("Q2b", [128, NPB, 128], BF16)
    Bmat = SB("Bmat", [128, NPB, 2, 128], BF16)
    Toep = SB("Toep", [128, NGB, 128], BF16)
    tpa = SB("tpa", [128, NGB, 128], F32)
    tpb = SB("tpb", [128, NGB, 128], F32)
    Ctab = SB("Ctab", [128, NPB, NKK], F32)
    Stab = SB("Stab", [128, NPB, NKK], F32)
    Dtab = SB("Dtab", [128, NPB, NKK], F32)
    ANG2 = SB("ANG2", [128, NPB, NKK], F32)
    GSs = ANG2
    gsb = Buf("Gs")
    hc = SB("hc", [128, NPB, NKK], BF16)
    hs = SB("hs", [128, NPB, NKK], BF16)
    yst = SB("yst", [128, NGB, NKK], BF16)
    pb_ = Buf("prep")
    tabb = Buf("TAB")
    wb_ = Buf("W")
    bigb = {n: Buf(n) for n in big}
    big["Q2"], bigb["Q2"] = big["BT"], bigb["BT"]
    qbb, bmb, toeb, tpab = Buf("Qb"), Buf("Bmat"), Buf("Toep"), Buf("tp")
    l2b, hcb, ystb = Buf("l2tab"), Buf("hc"), Buf("yst")
    t1b = t2b = sb_
    out_w = None

    def v3(t, g0):
        return t[:].rearrange("p (d g) -> p d g", d=2)[:, :, g0:g0 + NGB]

    def flat(t):
        return t[:].rearrange("p d g -> p (d g)")

    pcb, dtb = cb, Buf("Dtab")
    P.op("dve", lambda e: e.memset(Dtab[:, :, 0:1], 0.0), [], [dtb])
    for hz in (hc, hs):
        P.op("dve", lambda e, hz=hz: e.memset(hz[:, 0:NGB, 0:1], 0.0), [], [hcb])
        P.op("dve", lambda e, hz=hz: e.memset(hz[:, NGB:NPB, 31:32], 0.0), [], [hcb])
    lre, lim = t2d["lre2"][:], t2d["lim2"][:]
    M = {n: smc[n][:] for n in smc}
    act(P, M["dt"], t2d["lst2"][:], AF.Exp, [cb], [pb_])
    tt(P, "dve", M["lr"], lre, M["dt"], ALU.mult, [cb, pb_], [pb_])
    tt(P, "dve", M["th"], lim, M["dt"], ALU.mult, [cb, pb_], [pb_])
    for nm, off in (("c1", 0.25), ("s1", 0.0)):
        ts(P, "dve", S1[:, 0:64], M["th"], 1.0 / TWO_PI, off, ALU.mult, ALU.add, [pb_, ssb], [ssb])
        out_w = [pb_]
        finish_table(M[nm], 64, M["lr"], lambda a: a, small=True)
    ts(P, "dve", M["nr"], M["c1"], -1.0, None, ALU.add, None, [pb_], [pb_])
    tt(P, "dve", M["u1"], lre, lre, ALU.mult, [pb_, cb], [pb_])
    tt(P, "dve", M["den"], lim, lim, ALU.mult, [pb_, cb], [pb_])
    tt(P, "dve", M["den"], M["den"], M["u1"], ALU.add, [pb_], [pb_])
    P.op("dve", lambda e: e.reciprocal(out=M["den"], in_=M["den"]), [pb_], [pb_])
    tt(P, "dve", M["u1"], M["nr"], lre, ALU.mult, [pb_, cb], [pb_])
    tt(P, "dve", M["u2"], M["s1"], lim, ALU.mult, [pb_, cb], [pb_])
    tt(P, "dve", M["u1"], M["u1"], M["u2"], ALU.add, [pb_], [pb_])
    tt(P, "dve", M["cr"], M["u1"], M["den"], ALU.mult, [pb_], [pb_])
    tt(P, "dve", M["u1"], M["s1"], lre, ALU.mult, [pb_, cb], [pb_])
    tt(P, "dve", M["u2"], M["nr"], lim, ALU.mult, [pb_, cb], [pb_])
    tt(P, "dve", M["u1"], M["u1"], M["u2"], ALU.subtract, [pb_], [pb_])
    tt(P, "dve", M["ci"], M["u1"], M["den"], ALU.mult, [pb_], [pb_])
    ts(P, "dve", M["p8"], M["th"], 8.0 / TWO_PI, None, ALU.mult, None, [pb_], [pb_])
    cp(P, "dve", smi[:], M["p8"], [pb_], [pb_])
    cp(P, "dve", M["u2"], smi[:], [pb_], [pb_])
    tt(P, "dve", M["p8"], M["p8"], M["u2"], ALU.subtract, [pb_], [pb_])
    act(P, M["d8"], M["lr"], AF.Exp, [pb_], [pb_], scale=8.0)
    p8b = pb_
    for blk in range(32 // NGB):
        g0 = blk * NGB
        sm = {n: smc[n][:, blk * NPB:(blk + 1) * NPB].rearrange("p (d g) -> p d g", d=2) for n in smc}
        pc = {n: t3d[n][:, blk * NPB:(blk + 1) * NPB, :] for n in names3}

        def l2_part1():
            tt(P, "dve", ANG2[:], flat(sm["p8"]).unsqueeze(2).broadcast_to([128, NPB, NKK]),
               jv[:].unsqueeze(1).broadcast_to([128, NPB, NKK]), ALU.mult, [p8b, cb, l2b], [l2b])
            act(P, Dtab[:, :, 1:NKK], flat(sm["d8"]).unsqueeze(2).broadcast_to([128, NPB, NKK - 1]), AF.Copy,
                [p8b, dtb], [dtb])

        def l2_table(tab, off):
            nonlocal out_w
            a2f = ANG2[:].rearrange("p g k -> p (g k)")
            ts(P, "dve", TI[:, 0:NL], a2f, off, None, ALU.add, None, [l2b, sb_], [sb_])
            cp(P, "dve", T2[:, 0:NL], TI[:, 0:NL], [sb_], [sb_])
            stt(P, "dve", T1[:, 0:NL], a2f, off, T2[:, 0:NL], ALU.add, ALU.subtract, [l2b, sb_], [sb_])
            act(P, tab[:].rearrange("p g k -> p (g k)"), T1[:, 0:NL], AF.Sin, [sb_], [l2b], scale=TWO_PI)

        l2_part1()
        for st_ in range(3):
            ev_b = evec[:, st_].unsqueeze(2).broadcast_to([128, 2, NGB, 8])
            for dst, src in ((ANG, sm["th"]), (ELR, sm["lr"])):
                tt(P, "dve", dst[:, st_, 0].rearrange("p (d g) s -> p d g s", d=2),
                   src[:].unsqueeze(3).broadcast_to([128, 2, NGB, 8]), ev_b, ALU.mult, [pb_, cb, tabb], [tabb])
        nA = 3 * 2 * NPB * 8
        nH = 3 * NPB * 8
        t1v = S1[:, 0:nA].rearrange("p (a b n) -> p a b n", a=3, b=2)
        for ab, off in ((0, offA), (1, offB)):
            ts(P, "dve", t1v[:, :, ab, :], ANG[:, :, 0].rearrange("p a g s -> p a (g s)"), 1.0 / TWO_PI, off,
               ALU.mult, ALU.add, [tabb, ssb, cb], [ssb])
        a1, a2, ai = S1[:, 0:nA], S2[:, 0:nA], SI[:, 0:nA]
        cp(P, "dve", ai, a1, [ssb], [ssb])
        cp(P, "dve", a2, ai, [ssb], [ssb])
        tt(P, "dve", a1, a1, a2, ALU.subtract, [ssb], [ssb])
        act(P, a1, a1, AF.Sin, [ssb], [ssb], scale=TWO_PI)
        act(P, S2[:, 0:nH].rearrange("p (a n) -> p a n", a=3), ELR[:, :, 0].rearrange("p a g s -> p a (g s)"), AF.Exp,
            [ssb, tabb], [ssb])
        l2_table(Ctab, 0.25)
        tt(P, "dve", TAB[:].rearrange("p a b g s -> p a b (g s)"), t1v,
           S2[:, 0:nH].rearrange("p (a n) -> p a n", a=3).unsqueeze(2).broadcast_to([128, 3, 2, NPB * 8]), ALU.mult,
           [ssb], [tabb])
        PAs, PBs = TAB[:, 1:3, 0], TAB[:, 1:3, 1]
        cr_b = flat(sm["cr"]).unsqueeze(1).unsqueeze(3).broadcast_to([128, 2, NPB, 8])
        ci_b = flat(sm["ci"]).unsqueeze(1).unsqueeze(3).broadcast_to([128, 2, NPB, 8])
        tt(P, "dve", W["wu1"][:], PAs, cr_b, ALU.mult, [tabb, pb_], [wb_])
        tt(P, "dve", W["wu2"][:], PBs, ci_b, ALU.mult, [tabb, pb_], [wb_])
        stt(P, "dve", W["WA"][:], W["wu2"][:], sgnB, W["wu1"][:], ALU.mult, ALU.add, [wb_, cb], [wb_])
        tt(P, "dve", W["wu1"][:], PBs, cr_b, ALU.mult, [tabb, pb_, wb_], [wb_])
        tt(P, "dve", W["wu2"][:], PAs, ci_b, ALU.mult, [tabb, pb_, wb_], [wb_])
        stt(P, "dve", W["WB"][:], W["wu2"][:], sgnC, W["wu1"][:], ALU.mult, ALU.add, [wb_, cb], [wb_])

        def bc_c(t):
            return t[:].unsqueeze(2).broadcast_to([128, NPB, 8, 16])

        def bc_s(ap):
            return ap.unsqueeze(3).broadcast_to([128, NPB, 8, 16])

        def combo(out_name, a1, a2, b1, b2, sgn, op1, first_scaled):
            tt(P, "dve", big["tA"][:], a1, a2, ALU.mult, [pb_, pcb, wb_, tabb, bigb["tA"]], [bigb["tA"]])
            tt(P, "dve", big["tB"][:], b1, b2, ALU.mult, [pb_, pcb, wb_, tabb, bigb["tB"]], [bigb["tB"]])
            X, Y = (big["tA"], big["tB"]) if first_scaled == "A" else (big["tB"], big["tA"])
            stt(P, "dve", big[out_name][:], X[:], sgn, Y[:], ALU.mult, op1, [bigb["tA"], bigb["tB"], cb], [bigb[out_name]])

        WA_B, WB_B, WA_P, WB_P = W["WA"][:, 0], W["WB"][:, 0], W["WA"][:, 1], W["WB"][:, 1]
        combo("BT", bc_c(pc["bb1"]), bc_s(WA_B), bc_c(pc["bb2"]), bc_s(WB_B), sgnB, ALU.add, "B")
        combo("BS", bc_c(pc["bb1"]), bc_s(WB_B), bc_c(pc["bb2"]), bc_s(WA_B), sgnC, ALU.add, "A")
        for d2_ in range(NPB // 2):
            ps, pbk = P.psum_next()
            for q in range(2):
                dg = d2_ * 2 + q
                for kind, nm in ((0, "BT"), (1, "BS")):
                    o = ps[:, (q * 2 + kind) * 128:(q * 2 + kind + 1) * 128]
                    src = big[nm][:, dg].rearrange("p s c -> p (s c)")
                    P.op("pe", lambda e, o=o, src=src: e.transpose(o, src, ident[:]), [bigb[nm], cb], [pbk])
            cp(P, "dve", Bmat[:, d2_ * 2:d2_ * 2 + 2].rearrange("p g k n -> p (g k n)"), ps[:, :], [pbk], [bmb])
        combo("PT", bc_c(pc["bb1"]), bc_s(WA_P), bc_c(pc["bb2"]), bc_s(WB_P), sgnB, ALU.add, "B")
        PA_Q, PB_Q = TAB[:, 0, 0], TAB[:, 0, 1]
        combo("Q1", bc_c(pc["cc1"]), bc_s(PA_Q), bc_c(pc["cc2"]), bc_s(PB_Q), sgnC, ALU.subtract, "A")
        combo("Q2", bc_c(pc["cc1"]), bc_s(PB_Q), bc_c(pc["cc2"]), bc_s(PA_Q), sgnB, ALU.subtract, "B")
        act(P, Q1b[:], big["Q1"][:].rearrange("p g s c -> p g (s c)"), AF.Copy, [bigb["Q1"]], [qbb])
        act(P, Q2b[:], big["Q2"][:].rearrange("p g s c -> p g (s c)"), AF.Copy, [bigb["Q2"]], [qbb])
        l2_table(Stab, 0.0)
        pst = []
        for d in range(2):
            ps, pbk = P.psum_next()
            for g4 in range(NGB):
                dg = d * NGB + g4
                mm(P, ps[:, g4 * 128:(g4 + 1) * 128], big["PT"][:, dg].rearrange("p s c -> p (s c)"),
                   big["Q1"][:, dg].rearrange("p s c -> p (s c)"), True, True, [bigb["PT"], bigb["Q1"]], [pbk])
            pst.append((ps, pbk))
        for d, (ps, pbk) in enumerate(pst):
            tt(P, "dve", (tpa if d == 0 else tpb)[:], ps[:, 0:NGB * 128].rearrange("p (g n) -> p g n", g=NGB),
               masks[:, d, :].unsqueeze(1).broadcast_to([128, NGB, 128]), ALU.mult, [pbk, cb, tpab], [tpab])
        tt(P, "dve", tpa[:], tpa[:], tpb[:], ALU.add, [tpab], [tpab])
        tt(P, "dve", tpb[:], ident[:].unsqueeze(1).broadcast_to([128, NGB, 128]),
           dskp[:, g0:g0 + NGB].unsqueeze(2).broadcast_to([128, NGB, 128]), ALU.mult, [tpab, cb], [tpab])
        tt(P, "dve", Toep[:], tpa[:], tpb[:], ALU.add, [tpab], [toeb])
        for dg in range(NPB):
            d, g = dg // NGB, g0 + dg % NGB
            psG, pbG = P.psum_next()
            psS, pbS = P.psum_next()
            mm(P, psG[:, 0:NKK], Bmat[:, dg, 0, :], upk[:, g, :], True, True, [bmb, cb], [pbG])
            mm(P, psS[:, 0:NKK], Bmat[:, dg, 1, :], upk[:, g, :], True, True, [bmb, cb], [pbS])
            for (tdst, tb_, tab, ps, pbk) in ((t1, t1b, Ctab, psG, pbG), (t2, t2b, Stab, psS, pbS)):
                if d == 0:
                    tt(P, "dve", tdst[:, dg, :], tab[:, dg, :], ps[:, 0:NKK], ALU.mult, [l2b, pbk, tb_], [tb_])
                else:
                    tt(P, "dve", tdst[:, dg, 0:32], tab[:, dg, 0:32], ps[:, 31::-1], ALU.mult, [l2b, pbk, tb_], [tb_])
                    tt(P, "dve", tdst[:, dg, 32:NKK], tab[:, dg, 32:NKK], ps[:, NKK - 1:31:-1], ALU.mult,
                       [l2b, pbk, tb_], [tb_])
        tt(P, "dve", t1[:], t1[:], t2[:], ALU.add, [t1b, t2b], [t1b])
        P.op("dve", lambda e: e.tensor_tensor_scan(out=t2[:].rearrange("p g k -> p (g k)"),
                                                   data0=Dtab[:].rearrange("p g k -> p (g k)"),
                                                   data1=t1[:].rearrange("p g k -> p (g k)"),
                                                   initial=0.0, op0=ALU.mult, op1=ALU.add), [t1b, t2b, l2b, dtb], [t2b])
        for hdst, tab in ((hc, Ctab), (hs, Stab)):
            tt(P, "dve", hdst[:, 0:NGB, 1:NKK], tab[:, 0:NGB, 0:NKK - 1], t2[:, 0:NGB, 0:NKK - 1], ALU.mult,
               [l2b, t2b, hcb], [hcb])
            tt(P, "dve", hdst[:, NGB:NPB, 0:31], tab[:, NGB:NPB, 30::-1], t2[:, NGB:NPB, 30::-1], ALU.mult,
               [l2b, t2b, hcb], [hcb])
            tt(P, "dve", hdst[:, NGB:NPB, 32:NKK], tab[:, NGB:NPB, NKK - 2:30:-1], t2[:, NGB:NPB, NKK - 2:30:-1], ALU.mult,
               [l2b, t2b, hcb], [hcb])
        for g4 in range(NGB):
            g = g0 + g4
            ps, pbk = P.psum_next()
            mm(P, ps[:, 0:NKK], Toep[:, g4, :], upk[:, g, :], True, False, [toeb, cb], [pbk])
            for d in range(2):
                dg = d * NGB + g4
                mm(P, ps[:, 0:NKK], Q1b[:, dg, :], hc[:, dg, :], False, False, [qbb, hcb], [pbk])
                mm(P, ps[:, 0:NKK], Q2b[:, dg, :], hs[:, dg, :], False, d == 1, [qbb, hcb], [pbk])
            act(P, yst[:, g4, :], ps[:, 0:NKK], AF.Copy, [pbk], [ystb])
        if bg is not None:
            bg.pump(6)
        for s_ in range(8):
            dst = ys_dst.rows(s_ * 512 + g0 * 16, NGB * 16).rearrange("(g c) k -> c g k", c=16)
            P.dma("sp" if s_ % 2 else "act", dst, yst[s_ * 16:(s_ + 1) * 16, :, :], reads=[ystb], writes=[ys_buf])
    st.close()


class Pump:
    def __init__(self, jobs, depth):
        self.jobs, self.depth, self.i, self.h = jobs, depth, 0, {}

    def pump(self, n):
        for _ in range(n):
            if self.i >= len(self.jobs):
                return
            if self.i == 0:
                for k in range(min(self.depth, len(self.jobs))):
                    self.h[k] = self.jobs[k][0]()
            i = self.i
            self.jobs[i][1](self.h.pop(i))
            if i + self.depth < len(self.jobs):
                self.h[i + self.depth] = self.jobs[i + self.depth][0]()
            self.i += 1

    def drain(self):
        self.pump(len(self.jobs))


def k0_setup(P, condT, bmodT):
    ct = P.sbuf("ct", [128, 16, 2], F32)
    cbt = P.sbuf("cbt", [128, 16, 2], BF16)
    bt = P.sbuf("bt", [128, 2, 96], F32)
    bct, bcb, bbt = Buf(), Buf(), Buf()
    P.dma("sp", ct[:], condT, writes=[bct])
    P.dma("sp", bt[:], bmodT, writes=[bbt])
    act(P, cbt[:], ct[:], AF.Silu, [bct], [bcb])
    return {"cbt": cbt, "bcb": bcb, "bt": bt, "bbt": bbt}


def k0_jobs(P, ws, K, wmod_l, mdl, l):
    cbt, bcb, bt, bbt = K["cbt"], K["bcb"], K["bt"], K["bbt"]
    jobs = []
    for s_ in range(48):
        def load(s_=s_):
            return ws.load(wmod_l, 0, 16, s_ * 256, 256)

        def comp(h, s_=s_):
            wt, wb = h
            for jj in range(2):
                q = s_ * 2 + jj
                ps, pb = P.psum_next()
                for kc in range(16):
                    mm(P, ps[:, 0:2], wt[:, kc, jj * 128:(jj + 1) * 128], cbt[:, kc, :], kc == 0, kc == 15, [wb, bcb], [pb])
                act(P, mdl["t"][:, q, :], ps[:, 0:2], AF.Identity, [pb, bbt], [mdl["b"]], bias=bt[:, l, q:q + 1], scale=1.0)
        jobs.append((load, comp))
    return jobs


def k0_finish(P, mdl):
    mv = mdl["t"][:].rearrange("p (k c) w -> p k c w", k=6)
    ts(P, "dve", mdl["one"][:, 0], mv[:, 1], 1.0, None, ALU.add, None, [mdl["b"]], [mdl["b"]])
    ts(P, "dve", mdl["one"][:, 1], mv[:, 4], 1.0, None, ALU.add, None, [mdl["b"]], [mdl["b"]])


def k0_all(P, ws, nc, condT3, wmod, bmodT, oh_d, md, zcol):
    st = contextlib.ExitStack()
    NQ, NW = 24, 3
    NL = DEPTH * NQ * NW
    ct = P.sbuf("ct", [128, 16, NW], F32, st)
    cbt = P.sbuf("cbt", [128, 16, NW], BF16, st)
    bt = P.sbuf("bt", [128, DEPTH, NQ], F32, st)
    oh = P.sbuf("oh", [128, 2], F32, st)
    mloc = P.sbuf("mloc", [128, DEPTH, NQ, NW], F32, st)
    mall = P.sbuf("mall", [128, 4, NL], F32, st)
    bct, bcb, bbt, mlb, dlb, dgb, dgb2, mab = (Buf() for _ in range(8))
    P.dma("sp", ct[:], condT3, writes=[bct])
    P.dma("sp", bt[:], bmodT, writes=[bbt])
    P.dma("sp", oh[:], oh_d, writes=[bbt])
    act(P, cbt[:], ct[:], AF.Silu, [bct], [bcb])
    fs = [(P.sbuf("k0w", [128, 16, 256], F32, st), Buf("k0w")) for _ in range(4)]
    fb = [(P.sbuf("k0b", [128, 16, 256], BF16, st), Buf("k0b")) for _ in range(3)]
    nf = [0]
    ncast = [0]
    jobs = []
    for l in range(DEPTH):
        for s_ in range(NQ // 2):
            j_ = len(jobs)

            def load(s_=s_, l=l, j_=j_):
                if j_ % 3 == 0:
                    return ws.load(wmod[l], 0, 16, s_ * 256, 256) + (cbt,)
                t, b = fs[nf[0] % 4]
                nf[0] += 1
                src = wmod[l][0:2048, s_ * 256:(s_ + 1) * 256].rearrange("(kc p) n -> p kc n", p=128)
                P.dma("sp" if j_ % 3 == 1 else "act", t[:], src, writes=[b])
                return t, b, None

            def comp(h, s_=s_, l=l):
                wt, wb, rhs = h
                if rhs is None:
                    k_ = ncast[0]
                    ncast[0] += 1
                    t2, b2 = fb[k_ % 3]
                    if k_ % 2:
                        act(P, t2[:], wt[:], AF.Copy, [wb], [b2])
                    else:
                        cp(P, "dve", t2[:], wt[:], [wb], [b2])
                    wt, wb, rhs = t2, b2, cbt
                for jj in range(2):
                    q = s_ * 2 + jj
                    ps, pb = P.psum_next()
                    for kc in range(16):
                        mm(P, ps[:, 0:NW], wt[:, kc, jj * 128:(jj + 1) * 128], rhs[:, kc, :], kc == 0, kc == 15, [wb, bcb], [pb])
                    act(P, mloc[:, l, q, :], ps[:, 0:NW], AF.Identity, [pb, bbt], [mlb], bias=bt[:, l, q:q + 1], scale=1.0)
            jobs.append((load, comp))
    pipeline(jobs, 6)
    md_loc = nc.dram_tensor("md_loc", [128, NL], F32).ap()
    md_g1 = nc.dram_tensor("md_g1", [256, NL], F32).ap()
    md_g2 = nc.dram_tensor("md_g2", [512, NL], F32).ap()
    P.dma("sp", md_loc, mloc[:].rearrange("p l q w -> p (l q w)"), reads=[mlb], writes=[dlb])
    P.collective("AllGather", [md_loc], [md_g1], [[0, 1], [2, 3], [4, 5], [6, 7]], reads=[dlb], writes=[dgb])
    P.collective("AllGather", [md_g1], [md_g2], [[0, 4], [1, 5], [2, 6], [3, 7]], reads=[dgb], writes=[dgb2])
    P.dma("sp", mall[:], md_g2.rearrange("(k p) n -> p k n", p=128), reads=[dgb2], writes=[mab])
    for l in range(DEPTH):
        src = mall[:, :, l * NQ * NW:(l + 1) * NQ * NW].rearrange("p k (q w) -> p k q w", w=NW)
        dst = md[l]["t"][:].rearrange("p (k q) w -> p k q w", k=4)
        mb_ = md[l]["b"]
        act(P, dst[:, :, :, 0], src[:, :, :, 0], AF.Identity, [mab, bbt], [mb_], bias=zcol, scale=oh[:, 0:1])
        stt(P, "dve", dst[:, :, :, 0], src[:, :, :, 1], oh[:, 1:2], dst[:, :, :, 0], ALU.mult, ALU.add, [mab, bbt, mb_], [mb_])
        cp(P, "dve", dst[:, :, :, 1], src[:, :, :, 2], [mab], [mb_])
        k0_finish(P, md[l])
    P.phase()
    st.close()


def emit_ka(P, ws, C, x_src, xsrc_buf, pos_src, x0_dst, x0_buf, mdl, w_in, U_dst, u_buf, h_dst, h_buf, T=1152):
    st = contextlib.ExitStack()
    P.phase()
    tiles = token_tiles(T)
    nt = len(tiles)
    md = mdl["t"][:].rearrange("p (k c) w -> p k c w", k=6)
    onep = mdl["one"]
    x = P.sbuf("x", [128, 16, T], F32, st)
    xb = grid(16, nt, "x")
    h = P.sbuf("h", [128, 16, T], BF16, st)
    hb = grid(16, nt, "h")
    for c in range(16):
        P.dma("sp", x[:, c, :], x_src[:, c, :], reads=[xsrc_buf], writes=xb[c])
    if pos_src is not None:
        pos = P.sbuf("pos", [128, 4, 1024], F32, st)
        pbufs = [Buf() for _ in range(4)]
        for c in range(16):
            pbf = pbufs[c % 4]
            P.dma("act", pos[:, c % 4, :], pos_src[:, c, :], writes=[pbf])
            tt(P, "dve", x[:, c, 0:1024], x[:, c, 0:1024], pos[:, c % 4, :], ALU.add, [pbf] + xb[c], xb[c])
            P.dma("sp", x0_dst[:, c, :], x[:, c, :], reads=xb[c], writes=[x0_buf])
    S = ln_scratch(P, st)
    emit_ln(P, S, C, x, xb, h, hb, tiles, lambda c, mi: onep[:, 0, c, mi:mi + 1], lambda c, mi: md[:, 0, c, mi:mi + 1],
            extra=[mdl["b"]])
    for c in range(16):
        P.dma("act", h_dst[:, c, :], h[:, c, :], reads=hb[c], writes=[h_buf])
    stg = [(P.sbuf("stg", [128, T], BF16, st), Buf()) for _ in range(2)]
    jobs = []
    for s_ in range(8):
        def load(s_=s_):
            return ws.load(w_in, 0, 16, s_ * 256, 256)

        def comp(hd, s_=s_):
            wt, wb = hd
            for jj in range(2):
                j = s_ * 2 + jj
                stt_, stb = stg[j % 2]
                for ti, (t0, tn, _) in enumerate(tiles):
                    ps, pb = P.psum_next()
                    for kc in range(16):
                        mm(P, ps[:, 0:tn], wt[:, kc, jj * 128:(jj + 1) * 128], h[:, kc, t0:t0 + tn], kc == 0, kc == 15,
                           [wb, hb[kc][ti]], [pb])
                    if j < 8:
                        o = stt_[:].rearrange("p (s k) -> p s k", s=8)[:, :, t0 // 8:(t0 + tn) // 8]
                        act(P, o, ps[:, 0:tn].rearrange("p (k s) -> p s k", s=8), AF.Copy, [pb], [stb])
                    else:
                        act(P, stt_[:, t0:t0 + tn], ps[:, 0:tn], AF.Copy, [pb], [stb])
                P.dma("sp", U_dst.rows(j * 128, 128), stt_[:], reads=[stb], writes=[u_buf])
        jobs.append((load, comp))
    pipeline(jobs, 3)
    st.close()


def emit_kf(P, FC, U_all, ubufs, yf_dst, yf_buf, pch, need_ctx):
    st = contextlib.ExitStack()
    P.phase()
    uf = P.sbuf("uf", [128, 4, NTOK], BF16, st)
    ufb = Buf()
    for ph in range(2):
        src = U_all[ph][(8 + 4 * pch) * 128:(12 + 4 * pch) * 128, :].rearrange("(c p) t -> p c t", p=128)
        P.dma("sp", uf[:, :, CTX + ph * 1024: CTX + (ph + 1) * 1024], src[:, :, 0:1024], reads=[ubufs[ph]], writes=[ufb])
        P.dma("act", uf[:, :, ph * 128:(ph + 1) * 128], src[:, :, 1024:1152], reads=[ubufs[ph]], writes=[ufb])
    dftc, dpc, cb = FC["dftc"], FC["dpc"], FC["b"]
    dpl_d = FC["dpl_d"]
    V = P.sbuf("V", [128, 18, 2, 512], BF16, st)
    vb = [[Buf() for _ in range(2)] for _ in range(18)]
    yo = P.sbuf("yo", [128, 4, NTOK], BF16, st)
    yob = [Buf() for _ in range(4)]
    k = 0
    for tt_ in range(18):
        for grp in range(2):
            ps, pb = P.psum_next()
            for kc in range(2):
                mm(P, ps[:, :], uf[:, grp * 2 + kc, tt_ * 128:(tt_ + 1) * 128], dftc[:, kc, :], kc == 0, kc == 1, [ufb, cb], [pb])
            k += 1
            if k % 2:
                act(P, V[:, tt_, grp, :], ps[:, :], AF.Copy, [pb], [vb[tt_][grp]])
            else:
                cp(P, "dve", V[:, tt_, grp, :], ps[:, :], [pb], [vb[tt_][grp]])
    if need_ctx:
        for grp in range(2):
            for half in range(2):
                ps, pb = P.psum_next()
                n = 0
                for tt_ in range(2):
                    for cs in range(2):
                        mm(P, ps[:, 0:256], V[:, tt_, grp, cs * 256 + half * 128: cs * 256 + half * 128 + 128],
                           dpc[:, tt_, cs, :], n == 0, n == 3, [vb[tt_][grp], cb], [pb])
                        n += 1
                act(P, yo[:, grp * 2 + half, 0:256], ps[:, 0:256], AF.Copy, [pb], [yob[grp * 2 + half]])
    else:
        for c in range(4):
            P.op("dve", lambda e, c=c: e.memset(yo[:, c, 0:256], 0.0), [], [yob[c]])
    dts = [(P.sbuf("dpl", [128, 16, 2, 512], BF16, st), Buf()) for _ in range(2)]
    jobs = []
    for kb in range(4):
        def load(kb=kb):
            t, b = dts[kb % 2]
            for cs in range(2):
                src = dpl_d[:, cs, kb * 512:(kb + 1) * 512].rearrange("(tt p) k -> p tt k", p=128)
                P.dma("sp", t[:, :, cs, :], src, writes=[b])
            return t, b

        def comp(hd, kb=kb):
            t, b = hd
            for grp in range(2):
                for half in range(2):
                    ps, pb = P.psum_next()
                    n = 0
                    for tt_ in range(16):
                        for cs in range(2):
                            mm(P, ps[:, :], V[:, 2 + tt_, grp, cs * 256 + half * 128: cs * 256 + half * 128 + 128],
                               t[:, tt_, cs, :], n == 0, n == 31, [vb[2 + tt_][grp], b], [pb])
                            n += 1
                    o = yo[:, grp * 2 + half, 256 + kb * 512: 256 + (kb + 1) * 512]
                    if (grp + half) % 2:
                        act(P, o, ps[:, :], AF.Copy, [pb], [yob[grp * 2 + half]])
                    else:
                        cp(P, "dve", o, ps[:, :], [pb], [yob[grp * 2 + half]])
        jobs.append((load, comp))
    pipeline(jobs, 2)
    for c in range(4):
        P.dma("sp", yf_dst.rows(c * 128, 128), yo[:, c, :], reads=[yob[c]], writes=[yf_buf])
    st.close()


def emit_kc(P, ws, C, h_src, h_buf, x_src, xsrc_buf, mdl, lnp, lnpb, Ys_all, ysbufs, Yf_all, yfbufs, W, out_dst, out_buf, ph, T, blend=None):
    stp = contextlib.ExitStack()
    P.phase()
    tiles = token_tiles(T)
    nt = len(tiles)
    KL = T // 8
    md = mdl["t"][:].rearrange("p (k c) w -> p k c w", k=6)
    onep = mdl["one"]
    mdb = mdl["b"]
    h = P.sbuf("h", [128, 16, T], BF16, stp)
    hb = grid(16, nt, "h")
    mg = P.sbuf("mg", [128, 16, T], BF16, stp)
    mgb = grid(16, nt, "mg")
    S = ln_scratch(P, stp)
    sig = [(P.sbuf("sig", [128, 512], F32, stp), Buf()) for _ in range(2)]
    sigk = [0]

    def nsig():
        sigk[0] += 1
        return sig[sigk[0] % 2]

    bl = blend

    def load_h():
        for c in range(16):
            P.dma("sp" if c % 2 else "act", h[:, c, :], h_src[:, c, 0:T], reads=[h_buf], writes=hb[c])

    st2 = contextlib.ExitStack()
    sgl = P.sbuf("bufA", [128, 8, T], BF16, st2)
    g = P.sbuf("bufB", [128, 8, T], BF16, st2)
    yf = P.sbuf("bufC", [128, 8, T], BF16, st2)
    sgt = P.sbuf("sgt", [128, 2, T], BF16, st2)
    t1 = P.sbuf("t1", [128, 2, T], F32, st2)
    ab = [Buf("ysd") for _ in range(8)]
    gb = grid(8, nt, "g")
    sglb = grid(8, nt, "sgl")
    yfb = [Buf("yf") for _ in range(8)]
    sgtb = grid(2, nt, "sgt")
    t1b = grid(2, nt, "t1")
    ysd = sgl[:].rearrange("p j (s k) -> p j s k", s=8)
    ysdB = g[:].rearrange("p j (s k) -> p j s k", s=8)
    for pp in range(2):
        gw = [b_ for c in range(4 * pp, 4 * pp + 4) for b_ in gb[c]]
        for s_ in range(8):
            src = Ys_all.rows(s_ * 512, 512, slot=pp).rearrange("(j p) k -> p j k", p=128)
            P.dma("sp", ysd[:, 4 * pp:4 * pp + 4, s_, 0:128], src[:, :, 32:160], reads=[ysbufs[pp]], writes=ab[4 * pp:4 * pp + 4])
            P.dma("act", ysdB[:, 4 * pp:4 * pp + 4, s_, 0:128], src[:, :, 160:288], reads=[ysbufs[pp]], writes=gw)
            if T > 1024:
                P.dma("sp", ysd[:, 4 * pp:4 * pp + 4, s_, 128:144], src[:, :, 0:16], reads=[ysbufs[pp]], writes=ab[4 * pp:4 * pp + 4])
                P.dma("act", ysdB[:, 4 * pp:4 * pp + 4, s_, 128:144], src[:, :, 16:32], reads=[ysbufs[pp]], writes=gw)
    for pp in range(2):
        for c in range(4):
            cc_ = 4 * pp + c
            sf = Yf_all.rows(c * 128, 128, slot=pp)
            P.dma("sp", yf[:, cc_, 0:1024], sf[:, CTX:CTX + 1024], reads=[yfbufs[pp]], writes=[yfb[cc_]])
            P.dma("act", mg[:, cc_, 0:1024], sf[:, CTX + 1024:CTX + 2048], reads=[yfbufs[pp]], writes=mgb[cc_])
            if T > 1024:
                P.dma("sp", yf[:, cc_, 1024:1152], sf[:, 0:128], reads=[yfbufs[pp]], writes=[yfb[cc_]])
                P.dma("act", mg[:, cc_, 1024:1152], sf[:, 128:256], reads=[yfbufs[pp]], writes=mgb[cc_])
    load_h()
    for c in range(8):
        act(P, g[:, c, :], g[:, c, :], AF.Identity, gb[c] + [bl["b"]], gb[c], bias=bl["z"], scale=bl["m1"])
        stt(P, "dve", sgl[:, c, :], sgl[:, c, :], bl["m0"], g[:, c, :], ALU.mult, ALU.add, gb[c] + [ab[c], bl["b"]], [ab[c]])
    for c in range(8):
        act(P, mg[:, c, 0:T], mg[:, c, 0:T], AF.Identity, mgb[c] + [bl["b"]], mgb[c], bias=bl["z"], scale=bl["m1"])
        stt(P, "dve", yf[:, c, :], yf[:, c, :], bl["m0"], mg[:, c, 0:T], ALU.mult, ALU.add, mgb[c] + [yfb[c], bl["b"]], [yfb[c]])
    for c in range(8):
        for ti, (t0, tn, _) in enumerate(tiles):
            act(P, g[:, c, t0:t0 + tn].rearrange("p (k s) -> p k s", s=8),
                ysd[:, c, :, t0 // 8:(t0 + tn) // 8].rearrange("p s k -> p k s"), AF.Gelu, [ab[c]], [gb[c][ti]])
    for c in range(8):
        for ti in range(nt):
            sglb[c][ti].r.update({k: v for b_ in ab for k, v in b_.r.items()})
    jobs = []
    for s_ in range(4):
        def load(s_=s_):
            return ws.load(W["w_glu"], 0, 8, s_ * 256, 256)

        def comp(hd, s_=s_):
            wt, wb = hd
            for jj in range(2):
                j = s_ * 2 + jj
                for ti, (t0, tn, _) in enumerate(tiles):
                    ps, pb = P.psum_next()
                    for kc in range(8):
                        mm(P, ps[:, 0:tn], wt[:, kc, jj * 128:(jj + 1) * 128], g[:, kc, t0:t0 + tn], kc == 0, kc == 7,
                           [wb, gb[kc][ti]], [pb])
                    sg, sgb = nsig()
                    act(P, sg[:, 0:tn], ps[:, 0:tn], AF.Sigmoid, [pb], [sgb])
                    tt(P, "dve", sgl[:, j, t0:t0 + tn], g[:, j, t0:t0 + tn], sg[:, 0:tn], ALU.mult,
                       [gb[j][ti], sgb], [sglb[j][ti]])
        jobs.append((load, comp))
    for s_ in range(8):
        c0 = s_ * 256

        def mk(kind, s_=s_, c0=c0):
            def load():
                if kind == "gs":
                    return ws.load(W["w_in"], 0, 16, 2048 + c0, 256)
                if kind == "ps":
                    return ws.load(W["w_ps"], 0, 8, c0, 256)
                if kind == "gf":
                    return ws.load(W["w_in"], 0, 16, 4096 + c0, 256)
                return ws.load(W["w_pf"], 0, 8, c0, 256)

            def comp(hd):
                wt, wb = hd
                nk = 16 if kind in ("gs", "gf") else 8
                for jj in range(2):
                    j = s_ * 2 + jj
                    for ti, (t0, tn, _) in enumerate(tiles):
                        ps, pb = P.psum_next()
                        for kc in range(nk):
                            if kind in ("gs", "gf"):
                                rhs, rb = h[:, kc, t0:t0 + tn], hb[kc][ti]
                            elif kind == "ps":
                                rhs, rb = sgl[:, kc, t0:t0 + tn], sglb[kc][ti]
                            else:
                                rhs, rb = yf[:, kc, t0:t0 + tn], yfb[kc]
                            mm(P, ps[:, 0:tn], wt[:, kc, jj * 128:(jj + 1) * 128], rhs, kc == 0, kc == nk - 1, [wb, rb], [pb])
                        if kind in ("gs", "gf"):
                            act(P, sgt[:, jj, t0:t0 + tn], ps[:, 0:tn], AF.Sigmoid, [pb], [sgtb[jj][ti]])
                        elif kind == "ps":
                            tt(P, "dve", t1[:, jj, t0:t0 + tn], sgt[:, jj, t0:t0 + tn], ps[:, 0:tn], ALU.mult,
                               [pb, sgtb[jj][ti]], [t1b[jj][ti]])
                        else:
                            sg, sgb = nsig()
                            tt(P, "dve", sg[:, 0:tn], sgt[:, jj, t0:t0 + tn], ps[:, 0:tn], ALU.mult, [pb, sgtb[jj][ti]], [sgb])
                            tt(P, "dve", mg[:, j, t0:t0 + tn], sg[:, 0:tn], t1[:, jj, t0:t0 + tn], ALU.add,
                               [sgb, t1b[jj][ti]], [mgb[j][ti]])
            return (load, comp)
        for kind in ("gs", "ps", "gf", "pf"):
            jobs.append(mk(kind))
    pipeline(jobs, 3)
    st2.close()
    P.phase()

    x = P.sbuf("x2", [128, 16, T], F32, stp)
    xb = grid(16, nt, "x2")
    for c in range(16):
        P.dma("sp", x[:, c, :], x_src[:, c, 0:T], reads=[xsrc_buf], writes=xb[c])
        for ti, (t0, tn, _) in enumerate(tiles):
            act(P, x[:, c, t0:t0 + tn], x[:, c, t0:t0 + tn], AF.Copy, [xb[c][ti]], [xb[c][ti]], scale=ALPHA)
    jobs = []
    for s_ in range(8):
        def load(s_=s_):
            return ws.load(W["w_o"], 0, 16, s_ * 256, 256)

        def comp(hd, s_=s_):
            wt, wb = hd
            for jj in range(2):
                j = s_ * 2 + jj
                for ti, (t0, tn, segs) in enumerate(tiles):
                    ps, pb = P.psum_next()
                    for kc in range(16):
                        mm(P, ps[:, 0:tn], wt[:, kc, jj * 128:(jj + 1) * 128], mg[:, kc, t0:t0 + tn], kc == 0, kc == 15,
                           [wb, mgb[kc][ti]], [pb])
                    for (s0, sn, mi) in segs:
                        stt(P, "dve", x[:, j, s0:s0 + sn], ps[:, s0 - t0:s0 - t0 + sn], md[:, 2, j, mi:mi + 1],
                            x[:, j, s0:s0 + sn], ALU.mult, ALU.add, [pb, xb[j][ti], mdb], [xb[j][ti]])
        jobs.append((load, comp))
    pipeline(jobs, 3)
    emit_ln(P, S, C, x, xb, x, xb, tiles, lambda c, mi: lnp[:, 0, c:c + 1], lambda c, mi: lnp[:, 1, c:c + 1], extra=[lnpb])
    emit_ln(P, S, C, x, xb, h, hb, tiles, lambda c, mi: onep[:, 1, c, mi:mi + 1], lambda c, mi: md[:, 3, c, mi:mi + 1],
            extra=[mdb])
    for c in range(16):
        for ti, (t0, tn, _) in enumerate(tiles):
            act(P, x[:, c, t0:t0 + tn], x[:, c, t0:t0 + tn], AF.Copy, [xb[c][ti]], [xb[c][ti]], scale=ALPHA)
    jobs = []
    for hblk in range(4):
        for s_ in range(8):
            def load(s_=s_, hblk=hblk):
                return ws.load(W["w_up"], 0, 16, hblk * 2048 + s_ * 256, 256)

            def comp(hd, s_=s_):
                wt, wb = hd
                for jj in range(2):
                    j = s_ * 2 + jj
                    for ti, (t0, tn, _) in enumerate(tiles):
                        ps, pb = P.psum_next()
                        for kc in range(16):
                            mm(P, ps[:, 0:tn], wt[:, kc, jj * 128:(jj + 1) * 128], h[:, kc, t0:t0 + tn], kc == 0, kc == 15,
                               [wb, hb[kc][ti]], [pb])
                        sg, sgb = nsig()
                        act(P, sg[:, 0:tn], ps[:, 0:tn], AF.Relu, [pb], [sgb])
                        tt(P, "dve", mg[:, j, t0:t0 + tn], sg[:, 0:tn], sg[:, 0:tn], ALU.mult, [sgb], [mgb[j][ti]])
            jobs.append((load, comp))
        for s_ in range(8):
            def load(s_=s_, hblk=hblk):
                return ws.load(W["w_down"], hblk * 2048, 16, s_ * 256, 256)

            def comp(hd, s_=s_):
                wt, wb = hd
                for jj in range(2):
                    j = s_ * 2 + jj
                    for ti, (t0, tn, segs) in enumerate(tiles):
                        ps, pb = P.psum_next()
                        for kc in range(16):
                            mm(P, ps[:, 0:tn], wt[:, kc, jj * 128:(jj + 1) * 128], mg[:, kc, t0:t0 + tn], kc == 0, kc == 15,
                               [wb, mgb[kc][ti]], [pb])
                        for (s0, sn, mi) in segs:
                            stt(P, "dve", x[:, j, s0:s0 + sn], ps[:, s0 - t0:s0 - t0 + sn], md[:, 5, j, mi:mi + 1],
                                x[:, j, s0:s0 + sn], ALU.mult, ALU.add, [pb, xb[j][ti], mdb], [xb[j][ti]])
            jobs.append((load, comp))
    pipeline(jobs, 3)
    emit_ln(P, S, C, x, xb, x, xb, tiles, lambda c, mi: lnp[:, 2, c:c + 1], lambda c, mi: lnp[:, 3, c:c + 1], extra=[lnpb])
    for c in range(16):
        P.dma("sp", out_dst[:, c, 0:T], x[:, c, :], reads=xb[c], writes=[out_buf])
    stp.close()


class Chunked:
    def __init__(self, aps, rows_per):
        self.aps, self.rows_per = aps, rows_per

    def rows(self, r0, n, slot=0):
        k, o = r0 // self.rows_per, r0 % self.rows_per
        assert o + n <= self.rows_per
        return self.aps[k][slot * self.rows_per + o: slot * self.rows_per + o + n, :]


def emit_sel(P, BL, items):
    st = contextlib.ExitStack()
    P.phase()
    n = 1152
    ND = len(items)
    ta = [(P.sbuf("sela", [128, n], BF16, st), Buf()) for _ in range(ND)]
    tb = [(P.sbuf("selb", [128, n], BF16, st), Buf()) for _ in range(ND)]
    for k, (dst, srcA, srcB, rbuf, wbuf) in enumerate(items):
        P.dma("sp", ta[k][0][:], srcA, reads=[rbuf], writes=[ta[k][1]])
        P.dma("act", tb[k][0][:], srcB, reads=[rbuf], writes=[tb[k][1]])
    for k, (dst, srcA, srcB, rbuf, wbuf) in enumerate(items):
        a, ab_ = ta[k]
        b, bb_ = tb[k]
        act(P, b[:], b[:], AF.Identity, [bb_, BL["b"]], [bb_], bias=BL["z"], scale=BL["m1"])
        stt(P, "dve", a[:], a[:], BL["m0"], b[:], ALU.mult, ALU.add, [ab_, bb_, BL["b"]], [ab_])
        P.dma("sp", dst, a[:], reads=[ab_], writes=[wbuf])
    st.close()


S5P_SHAPES = {"lre2": [128, 64], "lim2": [128, 64], "lst2": [128, 64], "bb1": [128, 64, 16], "bb2": [128, 64, 16],
              "cc1": [128, 64, 16], "cc2": [128, 64, 16], "dskp": [128, 32]}
WNAMES = {"w_in": [2048, 6144], "w_glu": [1024, 1024], "w_ps": [1024, 2048], "w_pf": [1024, 2048], "w_o": [2048, 2048],
          "w_up": [2048, 8192], "w_down": [8192, 2048]}


def build_fused():
    nc = new_nc()
    condT = din(nc, "condT", [128, 16, 3])
    wmod = din(nc, "w_mod", [DEPTH, 2048, 3072])
    bmodT = din(nc, "bmodT", [128, DEPTH, 24])
    oh_d = din(nc, "onehot", [128, 2])
    xT = din(nc, "xT", [128, 16, 1152])
    posT = din(nc, "posT", [128, 16, 1024])
    lnp_d = din(nc, "lnp", [128, DEPTH, 4, 16])
    Wd = {n: din(nc, n, [DEPTH] + shp) for n, shp in WNAMES.items()}
    s5p = {n: din(nc, "s5_" + n, [DEPTH] + shp) for n, shp in S5P_SHAPES.items()}
    sc_d = {"evec": din(nc, "evec", [128, 3, 2, 8]), "pvec": din(nc, "pvec", [128, 4]), "masks": din(nc, "masks", [128, 2, 128]),
            "ident": din(nc, "ident", [128, 128]), "jv": din(nc, "jv", [128, NKK])}
    dftc_d = din(nc, "dftc", [128, 2, 512], BF16)
    dpl_d = din(nc, "dpl", [SEQ, 2, SEQ], BF16)
    dpc_d = din(nc, "dpc", [128, 2, 2, 256], BF16)
    msk_d = din(nc, "msk", [128, 3])
    outT = dout(nc, "outT", [128, 16, 1024])

    def dram(name, shape, dt=BF16):
        return nc.dram_tensor(name, shape, dt).ap()
    def chunked(name, nchunk, rows_per, cols, mult):
        return Chunked([dram(f"{name}_{k}", [mult * rows_per, cols]) for k in range(nchunk)], rows_per)
    U_loc = [chunked(f"U_loc{l}", 4, 512, 1152, 1) for l in range(DEPTH)]
    Ug = [chunked(f"Ug{l}", 4, 512, 1152, 2) for l in range(DEPTH)]
    Usel = [dram(f"Usel{i}", [2048, 1152]) for i in range(2)]
    Yf_loc = [chunked(f"Yf_loc{l}", 2, 256, NTOK, 1) for l in range(DEPTH)]
    Yfg = [chunked(f"Yfg{l}", 2, 256, NTOK, 2) for l in range(DEPTH)]
    Ys_loc = [chunked(f"Ys_loc{l}", 2, 2048, NKK, 1) for l in range(DEPTH)]
    Ysg = [chunked(f"Ysg{l}", 2, 2048, NKK, 2) for l in range(DEPTH)]

    def gather(loc, g, rb, wb):
        for a, o in zip(loc.aps, g.aps):
            P.collective("AllGather", [a], [o], GROUPS, reads=[rb], writes=[wb])
    X0 = dram("X0", [128, 16, 1152], F32)
    X1 = dram("X1", [128, 16, 1152], F32)
    Hs = dram("Hs", [128, 16, 1152])
    P = Prog(nc)
    P.psum_init()
    C = make_consts(P, nc)
    md = [{"t": P.sbuf("md", [128, 96, 2], F32), "one": P.sbuf("onep", [128, 2, 16, 2], F32), "b": Buf("md")} for _ in range(DEPTH)]
    lnp = P.sbuf("lnp", [128, DEPTH, 4, 16], F32)
    lnpb = Buf("lnp")
    P.dma("sp", lnp[:], lnp_d, writes=[lnpb])
    SC = {"b": Buf("s5c")}
    for n, d_ in sc_d.items():
        SC[n] = P.sbuf(n, list(d_.shape), F32)
        P.dma("act", SC[n][:], d_, writes=[SC["b"]])
    FC = {"b": Buf("fc"), "dpl_d": dpl_d}
    FC["dftc"] = P.sbuf("dftc", [128, 2, 512], BF16)
    FC["dpc"] = P.sbuf("dpc", [128, 2, 2, 256], BF16)
    P.dma("act", FC["dftc"][:], dftc_d, writes=[FC["b"]])
    P.dma("act", FC["dpc"][:], dpc_d, writes=[FC["b"]])
    msk = P.sbuf("msk", [128, 3], F32)
    BL = {"m0": msk[:, 0:1], "m1": msk[:, 1:2], "z": msk[:, 2:3], "b": Buf("msk")}
    P.dma("sp", msk[:], msk_d, writes=[BL["b"]])
    ext = Buf("ext")
    ws = WStream(P, 3)
    k0_all(P, ws, nc, condT, wmod, bmodT, oh_d, md, BL["z"])
    bH, bX0, bX1, bout = Buf("H"), Buf("X0"), Buf("X1"), Buf("out")
    bU = [Buf("Usel0"), Buf("Usel1")]
    GROUPS = [[0, 1], [2, 3], [4, 5], [6, 7]]
    for l in range(DEPTH):
        last = l == DEPTH - 1
        W = {n: Wd[n][l] for n in WNAMES}
        bUl, bUg, bYfl, bYfg, bYsl, bYsg = (Buf(f"{n}{l}") for n in ("Ul", "Ug", "Yfl", "Yfg", "Ysl", "Ysg"))
        if l == 0:
            emit_ka(P, ws, C, xT, ext, posT, X0, bX0, md[l], W["w_in"], U_loc[l], bUl, Hs, bH)
        else:
            emit_ka(P, ws, C, X1, bX1, None, None, None, md[l], W["w_in"], U_loc[l], bUl, Hs, bH)
        gather(U_loc[l], Ug[l], bUl, bUg)
        items = []
        for ph in range(2):
            for base in (0, 1024):
                for j in range(4):
                    r0 = base + j * 128
                    items.append((Usel[ph][r0:r0 + 128, :], Ug[l].rows(r0, 128, slot=ph), Ug[l].rows(r0 + 512, 128, slot=ph),
                                  bUg, bU[ph]))
        emit_sel(P, BL, items)
        emit_kf(P, FC, Usel, bU, Yf_loc[l], bYfl, 0, not last)
        emit_ks(P, SC, Usel, bU, {n: s5p[n][l] for n in S5P_SHAPES}, Ys_loc[l], bYsl, 0, None,
                pre=lambda l=l, bYfl=bYfl, bYfg=bYfg: gather(Yf_loc[l], Yfg[l], bYfl, bYfg))
        gather(Ys_loc[l], Ysg[l], bYsl, bYsg)
        Yf_all, Ys_all = Yfg[l], Ysg[l]
        if not last:
            emit_kc(P, ws, C, Hs, bH, X0, bX0, md[l], lnp[:, l], lnpb, Ys_all, [bYsg, bYsg], Yf_all, [bYfg, bYfg], W,
                    X1, bX1, None, 1152, blend=BL)
        else:
            emit_kc(P, ws, C, Hs, bH, X1, bX1, md[l], lnp[:, l], lnpb, Ys_all, [bYsg, bYsg], Yf_all, [bYfg, bYfg], W,
                    outT, bout, None, 1024, blend=BL)
    P.finish()
    return nc


_NC = {}


def kernel(**inp):
    inp = {k: np.asarray(v) for k, v in inp.items()}
    x, ctx = inp["x"], inp["ctx"]
    pos = pos_table()
    fc = fnet_consts()
    sc = s5_consts()
    if "nc" not in _NC:
        _NC["nc"] = build_fused()
    posT = [fm(pos[p * 1024:(p + 1) * 1024]) for p in range(2)]
    lnp = np.stack([np.stack([inp["ln1_g"][l], inp["ln1_b"][l], inp["ln2_g"][l], inp["ln2_b"][l]]) for l in range(DEPTH)])
    lnp = np.ascontiguousarray(lnp.reshape(DEPTH, 4, 16, 128).transpose(3, 0, 1, 2))
    bmodT = np.ascontiguousarray(inp["b_mod"].reshape(DEPTH, 96, 128).transpose(2, 0, 1))
    s5 = []
    for p in range(2):
        hp = [host_s5_params(inp, l, p) for l in range(DEPTH)]
        s5.append({"s5_" + n: np.ascontiguousarray(np.stack([hp[l][n] for l in range(DEPTH)])) for n in S5P_SHAPES})
    shared = {"lnp": lnp, **{n: inp[n] for n in WNAMES}, **sc, **fc}
    maps = []
    for i in range(NCORES):
        b, r = i // 2, i % 2
        k, bl_ = (i // 4) * 2 + r, (i % 4) // 2
        onehot = np.zeros((128, 2), np.float32)
        onehot[:, i // 4] = 1.0
        cond = np.stack([inp["c"][bl_], inp["c"][bl_ + 2], inp["c_ctx"]])
        condT = np.ascontiguousarray(cond.reshape(3, 16, 128).transpose(2, 1, 0))
        wm = np.ascontiguousarray(inp["w_mod"][:, :, 3072 * k:3072 * (k + 1)])
        bm = np.ascontiguousarray(bmodT[:, :, 24 * k:24 * (k + 1)])
        xt = fm(np.concatenate([x[b, r * 1024:(r + 1) * 1024], ctx[b, r * 128:(r + 1) * 128]], axis=0))
        msk = np.ascontiguousarray(np.broadcast_to(np.array([1.0 - r, float(r), 0.0], np.float32)[None], (128, 3)))
        maps.append({"w_mod": wm, "bmodT": bm, "onehot": onehot, "condT": condT, "xT": xt, "posT": posT[r], "msk": msk, **s5[r], **shared})
    res = run(_NC["nc"], maps)
    out = np.zeros((NB, SEQ, D), np.float32)
    for i in range(NCORES):
        b, p = i // 2, i % 2
        out[b, p * 1024:(p + 1) * 1024] = unfm(res[i]["outT"])
    return out
```

```python
import contextlib
import math
import numpy as np
import ml_dtypes
import concourse.bass as bass
import concourse.mybir as mybir
from concourse.bass_utils import run_bass_kernel_spmd

F32 = mybir.dt.float32
BF16 = mybir.dt.bfloat16
I32 = mybir.dt.int32
AF = mybir.ActivationFunctionType
ALU = mybir.AluOpType
NPBF = ml_dtypes.bfloat16

D = 2048
NB = 4
SEQ = 2048
CTX = 256
DEPTH = 2
DFF = 8192
ALPHA = (2 * DEPTH) ** 0.25
EPS = 1e-5
NCORES = 8
TWO_PI = 2.0 * math.pi

ENGS = ("pe", "act", "dve", "pool", "sp")


class Buf:
    __slots__ = ("name", "w", "r")

    default_fence = {}

    def __init__(self, name=""):
        self.name = name
        self.w = {}
        self.r = dict(Buf.default_fence)


class Prog:
    def __init__(self, nc, n_dma_sems=8):
        self.nc = nc
        self.ops = {e: [] for e in ENGS}
        self.cnt = {e: 0 for e in ENGS}
        self.known = {e: {} for e in ENGS}
        self.n_dma_sems = n_dma_sems
        self.dma_rr = {e: 0 for e in ("sp", "act", "pool")}
        self.dma_cnt = {}
        self.stack = contextlib.ExitStack()
        self.sems = {}
        self.all_toks = {}
        self._uid = 0
        self.psum_tiles = None
        self.psum_i = 0
        Buf.default_fence = {}
        self.cc_scratch = self.sbuf("ccscr", [128, 8], F32)
        self.cc_sems = [self.stack.enter_context(self.nc.semaphore(f"cc_sem{i}")) for i in range(18)]
        self.cc_n = 0

    def sbuf(self, name, shape, dt, stack=None):
        self._uid += 1
        st = stack if stack is not None else self.stack
        return st.enter_context(self.nc.sbuf_tensor(f"{name}_{self._uid}", list(shape), dt))

    def psum_init(self):
        self.psum_tiles = []
        for i in range(8):
            t = self.stack.enter_context(self.nc.psum_tensor(f"ps{i}", [128, 512], F32))
            self.psum_tiles.append((t, Buf(f"ps{i}")))

    def psum_next(self):
        t = self.psum_tiles[self.psum_i]
        self.psum_i = (self.psum_i + 1) % 8
        return t

    def _sem(self, key):
        if key not in self.sems:
            nm = "s_" + "_".join(str(k) for k in key)
            self.sems[key] = self.stack.enter_context(self.nc.semaphore(nm))
        return self.sems[key]

    def _deps(self, reads, writes):
        deps = []
        for b in reads:
            deps.extend(b.w.items())
        for b in writes:
            deps.extend(b.w.items())
            deps.extend(b.r.items())
        return deps

    def _mark(self, reads, writes, tok):
        for b in reads:
            if b.r.get(tok[0], 0) < tok[1]:
                b.r[tok[0]] = tok[1]
        for b in writes:
            if b.w.get(tok[0], 0) < tok[1]:
                b.w[tok[0]] = tok[1]
            b.r = {}
        if self.all_toks.get(tok[0], 0) < tok[1]:
            self.all_toks[tok[0]] = tok[1]

    def _waits(self, eng, deps, extra=()):
        need = {}
        for (key, val) in list(deps) + list(extra):
            if val <= 0:
                continue
            if need.get(key, 0) < val:
                need[key] = val
        out = []
        kn = self.known[eng]
        for key, val in need.items():
            if kn.get(key, 0) >= val:
                continue
            kn[key] = val
            out.append((key, val))
        return out

    def op(self, eng, fn, reads=(), writes=()):
        deps = self._deps(reads, writes)
        if eng == "pe":
            deps = [d for d in deps if d[0] != ("c", "pe")]
        waits = self._waits(eng, deps)
        self.cnt[eng] += 1
        tok = (("c", eng), self.cnt[eng])
        self.ops[eng].append((waits, fn, tok))
        self._mark(reads, writes, tok)
        return tok

    def dma(self, eng, out_ap, in_ap, reads=(), writes=(), **kw):
        k = self.dma_rr[eng]
        self.dma_rr[eng] = (k + 1) % self.n_dma_sems
        key = ("d", eng, k)
        n = self.dma_cnt.get(key, 0)
        deps = self._deps(reads, writes)
        waits = self._waits(eng, deps, extra=[(key, 16 * n)])
        self.dma_cnt[key] = n + 1
        tok = (key, 16 * (n + 1))

        def fn(e, out_ap=out_ap, in_ap=in_ap, kw=kw):
            return e.dma_start(out=out_ap, in_=in_ap, **kw)
        self.ops[eng].append((waits, fn, tok))
        self._mark(reads, writes, tok)
        return tok

    def collective(self, kind, ins, outs, groups, reads=(), writes=()):
        key = ("cc", self.cc_n)
        self.sems[key] = self.cc_sems[self.cc_n]
        self.cc_n += 1
        deps = self._deps(reads, writes)
        waits = self._waits("pool", deps)

        def fn(e):
            return e.collective_compute(kind, ALU.bypass, replica_groups=groups,
                                        ins=[a.opt() for a in ins], outs=[a.opt() for a in outs])
        self.ops["pool"].append((waits, fn, (key, 1)))
        self.known["pool"][key] = 1
        self.ops["pool"].append(([(key, 1)], None, None))
        scr = self.cc_scratch
        return self.op("pool", lambda e: e.memset(scr[:], 0.0), reads, writes)

    def phase(self):
        Buf.default_fence = dict(self.all_toks)

    def fence(self):
        return dict(self.all_toks)

    def newbuf(self, name="", fence=None):
        b = Buf(name)
        if fence:
            b.r.update(fence)
        return b

    def finish(self):
        waits = self._waits("sp", list(self.all_toks.items()))
        self.ops["sp"].append((waits, None, None))
        nc = self.nc
        for e in ENGS:
            self._sem(("c", e))
        for e in ("sp", "act", "pool"):
            for k in range(self.n_dma_sems):
                self._sem(("d", e, k))
        for e in ENGS:
            for waits_, fn_, tok_ in self.ops[e]:
                if tok_ is not None:
                    self._sem(tok_[0])
        engmap = {"pe": "tensor", "act": "scalar", "dve": "vector", "pool": "gpsimd", "sp": "sync"}
        with nc.Block() as block:
            for e in ENGS:
                ops = self.ops[e]

                def body(engine, ops=ops):
                    for waits, fn, tok in ops:
                        for key, val in waits:
                            engine.wait_ge(self._sem(key), val)
                        if fn is None:
                            continue
                        ins = fn(engine)
                        key, val = tok
                        (ins.then_inc(self._sem(key)) if key[0] == "cc" else ins.then_inc(self._sem(key), 1 if key[0] == "c" else 16))
                getattr(block, engmap[e])(body)
        self.stack.close()


def new_nc():
    return bass.Bass("TRN2", target_bir_lowering=False)


def din(nc, name, shape, dt=F32):
    return nc.dram_tensor(name, list(shape), dt, kind="ExternalInput").ap()


def dout(nc, name, shape, dt=F32):
    return nc.dram_tensor(name, list(shape), dt, kind="ExternalOutput").ap()


def pipeline(jobs, depth):
    handles = {}
    for i in range(min(depth, len(jobs))):
        handles[i] = jobs[i][0]()
    for i in range(len(jobs)):
        jobs[i][1](handles.pop(i))
        nxt = i + depth
        if nxt < len(jobs):
            handles[nxt] = jobs[nxt][0]()


class WStream:
    def __init__(self, P, nbuf, kc=16, ncol=256, stack=None):
        self.P = P
        self.tiles = [(P.sbuf("wslab", [128, kc, ncol], BF16, stack), Buf("wslab")) for _ in range(nbuf)]
        self.i = 0

    def load(self, w2d, r0, nk, c0, ncol):
        t, b = self.tiles[self.i]
        self.i = (self.i + 1) % len(self.tiles)
        src = w2d[r0:r0 + nk * 128, c0:c0 + ncol].rearrange("(kc p) n -> p kc n", p=128)
        self.P.dma("pool", t[:, 0:nk, 0:ncol], src, writes=[b])
        return t, b


def token_tiles(T):
    if T == 1024:
        return [(0, 512, [(0, 512, 0)]), (512, 512, [(512, 512, 0)])]
    assert T == 1152
    return [(0, 384, [(0, 384, 0)]), (384, 384, [(384, 384, 0)]), (768, 384, [(768, 256, 0), (1024, 128, 1)])]


def mm(P, out, lhsT, rhs, start, stop, reads, writes):
    return P.op("pe", lambda e: e.matmul(out, lhsT=lhsT, rhs=rhs, start=start, stop=stop), reads, writes)


def act(P, out, in_, func, reads, writes, bias=None, scale=None):
    kw = {}
    if bias is not None:
        kw["bias"] = bias
    if scale is not None:
        kw["scale"] = scale
    return P.op("act", lambda e: e.activation(out=out, in_=in_, func=func, **kw), reads, writes)


def tt(P, eng, out, in0, in1, op, reads, writes):
    return P.op(eng, lambda e: e.tensor_tensor(out=out, in0=in0, in1=in1, op=op), reads, writes)


def ts(P, eng, out, in0, s1, s2, op0, op1, reads, writes):
    if s2 is None:
        return P.op(eng, lambda e: e.tensor_scalar(out=out, in0=in0, scalar1=s1, scalar2=None, op0=op0), reads, writes)
    return P.op(eng, lambda e: e.tensor_scalar(out=out, in0=in0, scalar1=s1, scalar2=s2, op0=op0, op1=op1), reads, writes)


def stt(P, eng, out, in0, scalar, in1, op0, op1, reads, writes):
    return P.op(eng, lambda e: e.scalar_tensor_tensor(out=out, in0=in0, scalar=scalar, in1=in1, op0=op0, op1=op1),
                reads, writes)


def cp(P, eng, out, in_, reads, writes):
    return P.op(eng, lambda e: e.tensor_copy(out=out, in_=in_), reads, writes)


def emit_ln_stats(P, x, xb, tiles, C):
    T = x.shape[2]
    mean = P.sbuf("mean", [128, T], F32)
    rstd = P.sbuf("rstd", [128, T], F32)
    sq = [(P.sbuf("sq", [128, 384], F32), Buf("sq")) for _ in range(3)]
    tmp = P.sbuf("lntmp", [128, 384], F32)
    tmpb = Buf("lntmp")
    sb = []
    for ti, (t0, tn, _) in enumerate(tiles):
        ps1, pb1 = P.psum_next()
        ps2, pb2 = P.psum_next()
        for c in range(16):
            xc, xcb = S["xc"][c % 3]
            if c % 2:
                cp(P, "dve", xc[:, 0:tn], x[:, c, t0:t0 + tn], [xb[c][ti]], [xcb])
            else:
                act(P, xc[:, 0:tn], x[:, c, t0:t0 + tn], AF.Copy, [xb[c][ti]], [xcb])
            mm(P, ps1[:, 0:tn], C["onesb"][:], xc[:, 0:tn], c == 0, c == 15, [xcb, C["b"]], [pb1])
            s, sbf = sq[c % 3]
            act(P, s[:, 0:tn], x[:, c, t0:t0 + tn], AF.Square, [xb[c][ti]], [sbf])
            mm(P, ps2[:, 0:tn], C["ones"][:], s[:, 0:tn], c == 0, c == 15, [sbf, C["b"]], [pb2])
        mb = Buf("mean")
        ts(P, "dve", mean[:, t0:t0 + tn], ps1[:, 0:tn], 1.0 / D, None, ALU.mult, None, [pb1], [mb])
        tt(P, "dve", tmp[:, 0:tn], mean[:, t0:t0 + tn], mean[:, t0:t0 + tn], ALU.mult, [mb], [tmpb])
        stt(P, "dve", tmp[:, 0:tn], ps2[:, 0:tn], 1.0 / D, tmp[:, 0:tn], ALU.mult, ALU.subtract, [pb2, tmpb], [tmpb])
        ts(P, "dve", tmp[:, 0:tn], tmp[:, 0:tn], EPS, None, ALU.add, None, [tmpb], [tmpb])
        act(P, tmp[:, 0:tn], tmp[:, 0:tn], AF.Ln, [tmpb], [tmpb])
        act(P, rstd[:, t0:t0 + tn], tmp[:, 0:tn], AF.Exp, [tmpb], [mb], scale=-0.5)
        sb.append(mb)
    return mean, rstd, sb


def emit_norm_affine(P, x, xb, out, outb, mean, rstd, sb, tiles, scale_fn, bias_fn, tmps):
    k = 0
    for ti, (t0, tn, segs) in enumerate(tiles):
        for c in range(16):
            tmp, tb = tmps[k % len(tmps)]
            k += 1
            tt(P, "dve", tmp[:, 0:tn], x[:, c, t0:t0 + tn], mean[:, t0:t0 + tn], ALU.subtract, [xb[c][ti], sb[ti]], [tb])
            tt(P, "dve", tmp[:, 0:tn], tmp[:, 0:tn], rstd[:, t0:t0 + tn], ALU.mult, [tb, sb[ti]], [tb])
            act(P, out[:, c, t0:t0 + tn], tmp[:, 0:tn], AF.Identity, [tb], [outb[c][ti]],
                bias=bias_fn(c, mi), scale=scale_fn(c, mi))


def make_consts(P, nc):
    ones = P.sbuf("ones", [128, 128], F32)
    b = Buf("ones")
    P.op("dve", lambda e: e.memset(ones[:], 1.0), writes=[b])
    onesb = P.sbuf("onesb", [128, 128], BF16)
    P.op("dve", lambda e: e.memset(onesb[:], 1.0), writes=[b])
    return {"ones": ones, "onesb": onesb, "b": b}


def grid(n, m, name):
    return [[Buf(f"{name}{i}_{j}") for j in range(m)] for i in range(n)]


def ln_scratch(P, st=None):
    return {
        "mean": P.sbuf("mean", [128, 1152], F32, st), "rstd": P.sbuf("rstd", [128, 1152], F32, st),
        "mbs": [Buf("meanb") for _ in range(3)],
        "sq": [(P.sbuf("sq", [128, 512], BF16, st), Buf("sq")) for _ in range(3)],
        "xc": [(P.sbuf("xc", [128, 512], BF16, st), Buf("xc")) for _ in range(3)],
        "tmp": P.sbuf("lntmp", [128, 512], F32, st), "tmpb": Buf("lntmp"), "mb": Buf("meanb"),
        "tmps": [(P.sbuf("nt", [128, 512], F32, st), Buf()) for _ in range(3)], "k": 0,
    }


def emit_ln(P, S, C, x, xb, out, outb, tiles, scale_fn, bias_fn, extra=()):
    mean, rstd, tmp, tmpb = S["mean"], S["rstd"], S["tmp"], S["tmpb"]
    for ti, (t0, tn, segs) in enumerate(tiles):
        mb = S["mbs"][ti]
        ps1, pb1 = P.psum_next()
        ps2, pb2 = P.psum_next()
        for c in range(16):
            mm(P, ps1[:, 0:tn], C["ones"][:], x[:, c, t0:t0 + tn], c == 0, c == 15, [xb[c][ti], C["b"]], [pb1])
            sq, sbf = S["sq"][c % 3]
            act(P, sq[:, 0:tn], x[:, c, t0:t0 + tn], AF.Square, [xb[c][ti]], [sbf])
            mm(P, ps2[:, 0:tn], C["onesb"][:], sq[:, 0:tn], c == 0, c == 15, [sbf, C["b"]], [pb2])
        ts(P, "dve", mean[:, t0:t0 + tn], ps1[:, 0:tn], 1.0 / D, None, ALU.mult, None, [pb1], [mb])
        tt(P, "dve", tmp[:, 0:tn], mean[:, t0:t0 + tn], mean[:, t0:t0 + tn], ALU.mult, [mb], [tmpb])
        stt(P, "dve", tmp[:, 0:tn], ps2[:, 0:tn], 1.0 / D, tmp[:, 0:tn], ALU.mult, ALU.subtract, [pb2, tmpb], [tmpb])
        ts(P, "dve", tmp[:, 0:tn], tmp[:, 0:tn], EPS, None, ALU.add, None, [tmpb], [tmpb])
        act(P, tmp[:, 0:tn], tmp[:, 0:tn], AF.Ln, [tmpb], [tmpb])
        act(P, rstd[:, t0:t0 + tn], tmp[:, 0:tn], AF.Exp, [tmpb], [mb], scale=-0.5)
    for ti, (t0, tn, segs) in enumerate(tiles):
        mb = S["mbs"][ti]
        for c in range(16):
            eng = "pool" if c % 5 == 4 else "dve"
            t2, tb = S["tmps"][2 if eng == "pool" else S["k"] % 2]
            S["k"] += 1
            tt(P, eng, t2[:, 0:tn], x[:, c, t0:t0 + tn], mean[:, t0:t0 + tn], ALU.subtract, [xb[c][ti], mb], [tb])
            tt(P, eng, t2[:, 0:tn], t2[:, 0:tn], rstd[:, t0:t0 + tn], ALU.mult, [tb, mb], [tb])
            for (s0, sn, mi) in segs:
                act(P, out[:, c, s0:s0 + sn], t2[:, s0 - t0:s0 - t0 + sn], AF.Identity, [tb] + list(extra), [outb[c][ti]],
                    bias=bias_fn(c, mi), scale=scale_fn(c, mi))


NTOK = CTX + SEQ


def fnet_consts():
    ch = np.arange(256)
    ang = 2 * np.pi * np.outer(ch, ch) / 256.0
    dc = np.concatenate([np.cos(ang), np.sin(ang)], axis=1) / 16.0
    dftc = dc.reshape(2, 128, 512).transpose(1, 0, 2)
    t = np.arange(SEQ)
    angl = 2 * np.pi * (np.outer(t, t) % SEQ) / SEQ
    dpl = np.stack([np.cos(angl), -np.sin(angl)], axis=1) / np.sqrt(SEQ)
    tc = np.arange(CTX)
    angc = 2 * np.pi * (np.outer(tc, tc) % CTX) / CTX
    dpc = np.stack([np.cos(angc), -np.sin(angc)], axis=1) / np.sqrt(CTX)
    dpc = dpc.reshape(2, 128, 2, 256).transpose(1, 0, 2, 3)
    return {"dftc": np.ascontiguousarray(dftc).astype(NPBF), "dpl": np.ascontiguousarray(dpl).astype(NPBF),
            "dpc": np.ascontiguousarray(dpc).astype(NPBF)}


NKK = NTOK // 8
NGB = 4
NPB = 2 * NGB


def s5_consts():
    s = np.arange(8, dtype=np.float32)
    ev = np.zeros((3, 2, 8), np.float32)
    ev[0, 0], ev[0, 1] = s + 1, 8 - s
    ev[1, 0], ev[1, 1] = 7 - s, s
    ev[2, 0], ev[2, 1] = -(s + 1), -(8 - s)
    evec = np.broadcast_to(ev[None], (128, 3, 2, 8)).copy()
    pv = np.zeros((128, 4), np.float32)
    pv[:64, 0] = 0.25
    pv[64:, 1] = 0.25
    pv[:64, 2], pv[64:, 2] = 1.0, -1.0
    pv[:64, 3], pv[64:, 3] = -1.0, 1.0
    si = np.arange(128) // 16
    mf = (si[None, :] >= si[:, None]).astype(np.float32)
    mb = (si[:, None] >= si[None, :]).astype(np.float32)
    masks = np.stack([mf, mb], axis=1)
    ident = np.eye(128, dtype=np.float32)
    jv = np.broadcast_to(np.arange(NKK, dtype=np.float32)[None], (128, NKK)).copy()
    return {"evec": evec, "pvec": pv, "masks": np.ascontiguousarray(masks), "ident": ident, "jv": jv}


def host_s5_params(inp, l, p):
    gs = slice(32 * p, 32 * p + 32)

    def dup(a):
        return np.ascontiguousarray(np.concatenate([a, a], axis=0))
    def bm(a):
        sh = a.shape
        return np.ascontiguousarray(a.reshape(sh[0], 2, 32 // NGB, NGB, *sh[2:]).swapaxes(1, 2).reshape(sh))
    out = {}
    out["lre2"] = dup(inp["lam_re"][l][:, gs, :].transpose(2, 0, 1).reshape(64, 64))
    out["lim2"] = dup(inp["lam_im"][l][:, gs, :].transpose(2, 0, 1).reshape(64, 64))
    out["lst2"] = np.ascontiguousarray(np.broadcast_to(inp["log_step"][l][:, gs].reshape(1, 64), (128, 64)))
    out["bb1"] = dup(inp["ssm_b_re"][l][:, gs].transpose(2, 0, 1, 3).reshape(64, 64, 16))
    out["bb2"] = dup(inp["ssm_b_im"][l][:, gs].transpose(2, 0, 1, 3).reshape(64, 64, 16))
    out["cc1"] = dup(inp["ssm_c_re"][l][:, gs].transpose(3, 0, 1, 2).reshape(64, 64, 16))
    out["cc2"] = dup(inp["ssm_c_im"][l][:, gs].transpose(3, 0, 1, 2).reshape(64, 64, 16))
    for n in ("lre2", "lim2", "lst2", "bb1", "bb2", "cc1", "cc2"):
        out[n] = bm(out[n])
    dsk = inp["d_skip"][l].reshape(64, 16)[gs]
    out["dskp"] = np.ascontiguousarray(np.broadcast_to(dsk.T[None], (8, 16, 32)).reshape(128, 32))
    return out


def pos_table():
    quarter = D // 4
    omega = 1.0 / (10000.0 ** (np.arange(quarter, dtype=np.float32) / np.float32(quarter)))
    row = np.repeat(np.arange(SEQ // 64, dtype=np.float32), 64)
    col = np.tile(np.arange(64, dtype=np.float32), SEQ // 64)
    ar = row[:, None] * omega
    ac = col[:, None] * omega
    return np.concatenate([np.sin(ar), np.cos(ar), np.sin(ac), np.cos(ac)], axis=-1).astype(np.float32)


def core_bp(i):
    return i // 2, i % 2


def fm(a):
    t, f = a.shape
    return np.ascontiguousarray(a.reshape(t, f // 128, 128).transpose(2, 1, 0))


def unfm(a):
    p, c, t = a.shape
    return np.ascontiguousarray(a.transpose(2, 1, 0).reshape(t, c * 128))


def run(nc, in_maps):
    res = run_bass_kernel_spmd(nc, in_maps, core_ids=list(range(NCORES)))
    return res.results


def emit_ks(P, SC, U_all, ubufs, prm, ys_dst, ys_buf, pch, bg=None, pre=None):
    st = contextlib.ExitStack()
    P.phase()
    if pre is not None:
        pre()

    def SB(name, shape, dt):
        return P.sbuf(name, shape, dt, st)
    cb = Buf("consts")
    upk = SB("upk", [128, 32, NKK], BF16)
    q = 0
    for s_ in range(8):
        for ph in range(2):
            for (c0, nk, kk0) in ((0, 128, 32 + ph * 128), (128, 16, ph * 16)):
                src = U_all[ph][512 * pch:512 * pch + 512, s_ * 144 + c0: s_ * 144 + c0 + nk].rearrange("(g c) k -> c g k", c=16)
                P.dma(("sp", "act")[q % 2], upk[s_ * 16:(s_ + 1) * 16, :, kk0:kk0 + nk], src, reads=[ubufs[ph]], writes=[cb])
                q += 1
    names2 = ["lre2", "lim2", "lst2"]
    names3 = ["bb1", "bb2", "cc1", "cc2"]
    t2d, t3d = {}, {}
    for n in names2:
        t2d[n] = SB(n, [128, 64], F32)
        P.dma("sp", t2d[n][:], prm[n], writes=[cb])
    for n in names3:
        t3d[n] = SB(n, [128, 64, 16], F32)
        P.dma("act", t3d[n][:], prm[n], writes=[cb])
    dskp = SB("dskp", [128, 32], F32)
    P.dma("sp", dskp[:], prm["dskp"], writes=[cb])
    evec, pvec, masks, ident, jv = SC["evec"], SC["pvec"], SC["masks"], SC["ident"], SC["jv"]
    cb.w.update(SC["b"].w)
    offA, offB, sgnC, sgnB = pvec[:, 0:1], pvec[:, 1:2], pvec[:, 2:3], pvec[:, 3:4]

    NL = NPB * NKK
    t1 = SB("t1", [128, NPB, NKK], F32)
    t2 = SB("t2", [128, NPB, NKK], F32)
    T1 = t1[:].rearrange("p g k -> p (g k)")
    T2 = t2[:].rearrange("p g k -> p (g k)")
    Gs = SB("Gs", [128, NPB, NKK], F32)
    TI = Gs[:].rearrange("p g k -> p (g k)").bitcast(I32)
    sb_ = Buf("tabscratch")
    S1 = SB("S1", [128, 768], F32)
    S2 = SB("S2", [128, 768], F32)
    SI = SB("SI", [128, 768], I32)
    ssb = Buf("smallscratch")

    def finish_table(out, n, elr, shape_fn, small=False):
        A1, A2, AI, bb = (S1, S2, SI, ssb) if small else (T1, T2, TI, sb_)
        a1, a2, ai = A1[:, 0:n], A2[:, 0:n], AI[:, 0:n]
        cp(P, "dve", ai, a1, [bb], [bb])
        cp(P, "dve", a2, ai, [bb], [bb])
        tt(P, "dve", a1, a1, a2, ALU.subtract, [bb], [bb])
        if elr is None:
            act(P, out, shape_fn(a1), AF.Sin, [bb], out_w, scale=TWO_PI)
        else:
            act(P, a1, a1, AF.Sin, [bb], [bb], scale=TWO_PI)
            act(P, shape_fn(a2), elr, AF.Exp, [bb] + out_w, [bb])
            tt(P, "dve", out, shape_fn(a1), shape_fn(a2), ALU.mult, [bb], out_w)

    smc = {n: SB(n, [128, 64], F32) for n in ["dt", "lr", "th", "c1", "s1", "nr", "den", "cr", "ci", "u1", "u2", "p8", "d8"]}
    smi = SB("smi", [128, 64], I32)
    ANG = SB("ANG", [128, 3, 2, NPB, 8], F32)
    ELR = SB("ELR", [128, 3, 2, NPB, 8], F32)
    TAB = SB("TAB", [128, 3, 2, NPB, 8], F32)
    W = {n: SB(n, [128, 2, NPB, 8], F32) for n in ["WA", "WB", "wu1", "wu2"]}
    big = {n: SB(n, [128, NPB, 8, 16], F32) for n in ["tA", "tB", "BT", "BS", "PT", "Q1"]}
    Q1b = SB("Q1b", [128, NPB, 128], BF16)
    Q2b = SB("Q2b", [128, NPB, 128], BF16)
    Bmat = SB("Bmat", [128, NPB, 2, 128], BF16)
    Toep = SB("Toep", [128, NGB, 128], BF16)
    tpa = SB("tpa", [128, NGB, 128], F32)
    tpb = SB("tpb", [128, NGB, 128], F32)
    Ctab = SB("Ctab", [128, NPB, NKK], F32)
    Stab = SB("Stab", [128, NPB, NKK], F32)
    Dtab = SB("Dtab", [128, NPB, NKK], F32)
    ANG2 = SB("ANG2", [128, NPB, NKK], F32)
    GSs = ANG2
    gsb = Buf("Gs")
    hc = SB("hc", [128, NPB, NKK], BF16)
    hs = SB("hs", [128, NPB, NKK], BF16)
    yst = SB("yst", [128, NGB, NKK], BF16)
    pb_ = Buf("prep")
    tabb = Buf("TAB")
    wb_ = Buf("W")
    bigb = {n: Buf(n) for n in big}
    big["Q2"], bigb["Q2"] = big["BT"], bigb["BT"]
    qbb, bmb, toeb, tpab = Buf("Qb"), Buf("Bmat"), Buf("Toep"), Buf("tp")
    l2b, hcb, ystb = Buf("l2tab"), Buf("hc"), Buf("yst")
    t1b = t2b = sb_
    out_w = None

    def v3(t, g0):
        return t[:].rearrange("p (d g) -> p d g", d=2)[:, :, g0:g0 + NGB]

    def flat(t):
        return t[:].rearrange("p d g -> p (d g)")

    pcb, dtb = cb, Buf("Dtab")
    P.op("dve", lambda e: e.memset(Dtab[:, :, 0:1], 0.0), [], [dtb])
    for hz in (hc, hs):
        P.op("dve", lambda e, hz=hz: e.memset(hz[:, 0:NGB, 0:1], 0.0), [], [hcb])
        P.op("dve", lambda e, hz=hz: e.memset(hz[:, NGB:NPB, 31:32], 0.0), [], [hcb])
    lre, lim = t2d["lre2"][:], t2d["lim2"][:]
    M = {n: smc[n][:] for n in smc}
    act(P, M["dt"], t2d["lst2"][:], AF.Exp, [cb], [pb_])
    tt(P, "dve", M["lr"], lre, M["dt"], ALU.mult, [cb, pb_], [pb_])
    tt(P, "dve", M["th"], lim, M["dt"], ALU.mult, [cb, pb_], [pb_])
    for nm, off in (("c1", 0.25), ("s1", 0.0)):
        ts(P, "dve", S1[:, 0:64], M["th"], 1.0 / TWO_PI, off, ALU.mult, ALU.add, [pb_, ssb], [ssb])
        out_w = [pb_]
        finish_table(M[nm], 64, M["lr"], lambda a: a, small=True)
    ts(P, "dve", M["nr"], M["c1"], -1.0, None, ALU.add, None, [pb_], [pb_])
    tt(P, "dve", M["u1"], lre, lre, ALU.mult, [pb_, cb], [pb_])
    tt(P, "dve", M["den"], lim, lim, ALU.mult, [pb_, cb], [pb_])
    tt(P, "dve", M["den"], M["den"], M["u1"], ALU.add, [pb_], [pb_])
    P.op("dve", lambda e: e.reciprocal(out=M["den"], in_=M["den"]), [pb_], [pb_])
    tt(P, "dve", M["u1"], M["nr"], lre, ALU.mult, [pb_, cb], [pb_])
    tt(P, "dve", M["u2"], M["s1"], lim, ALU.mult, [pb_, cb], [pb_])
    tt(P, "dve", M["u1"], M["u1"], M["u2"], ALU.add, [pb_], [pb_])
    tt(P, "dve", M["cr"], M["u1"], M["den"], ALU.mult, [pb_], [pb_])
    tt(P, "dve", M["u1"], M["s1"], lre, ALU.mult, [pb_, cb], [pb_])
    tt(P, "dve", M["u2"], M["nr"], lim, ALU.mult, [pb_, cb], [pb_])
    tt(P, "dve", M["u1"], M["u1"], M["u2"], ALU.subtract, [pb_], [pb_])
    tt(P, "dve", M["ci"], M["u1"], M["den"], ALU.mult, [pb_], [pb_])
    ts(P, "dve", M["p8"], M["th"], 8.0 / TWO_PI, None, ALU.mult, None, [pb_], [pb_])
    cp(P, "dve", smi[:], M["p8"], [pb_], [pb_])
    cp(P, "dve", M["u2"], smi[:], [pb_], [pb_])
    tt(P, "dve", M["p8"], M["p8"], M["u2"], ALU.subtract, [pb_], [pb_])
    act(P, M["d8"], M["lr"], AF.Exp, [pb_], [pb_], scale=8.0)
    p8b = pb_
    for blk in range(32 // NGB):
        g0 = blk * NGB
        sm = {n: smc[n][:, blk * NPB:(blk + 1) * NPB].rearrange("p (d g) -> p d g", d=2) for n in smc}
        pc = {n: t3d[n][:, blk * NPB:(blk + 1) * NPB, :] for n in names3}

        def l2_part1():
            tt(P, "dve", ANG2[:], flat(sm["p8"]).unsqueeze(2).broadcast_to([128, NPB, NKK]),
               jv[:].unsqueeze(1).broadcast_to([128, NPB, NKK]), ALU.mult, [p8b, cb, l2b], [l2b])
            act(P, Dtab[:, :, 1:NKK], flat(sm["d8"]).unsqueeze(2).broadcast_to([128, NPB, NKK - 1]), AF.Copy,
                [p8b, dtb], [dtb])

        def l2_table(tab, off):
            nonlocal out_w
            a2f = ANG2[:].rearrange("p g k -> p (g k)")
            ts(P, "dve", TI[:, 0:NL], a2f, off, None, ALU.add, None, [l2b, sb_], [sb_])
            cp(P, "dve", T2[:, 0:NL], TI[:, 0:NL], [sb_], [sb_])
            stt(P, "dve", T1[:, 0:NL], a2f, off, T2[:, 0:NL], ALU.add, ALU.subtract, [l2b, sb_], [sb_])
            act(P, tab[:].rearrange("p g k -> p (g k)"), T1[:, 0:NL], AF.Sin, [sb_], [l2b], scale=TWO_PI)

        l2_part1()
        for st_ in range(3):
            ev_b = evec[:, st_].unsqueeze(2).broadcast_to([128, 2, NGB, 8])
            for dst, src in ((ANG, sm["th"]), (ELR, sm["lr"])):
                tt(P, "dve", dst[:, st_, 0].rearrange("p (d g) s -> p d g s", d=2),
                   src[:].unsqueeze(3).broadcast_to([128, 2, NGB, 8]), ev_b, ALU.mult, [pb_, cb, tabb], [tabb])
        nA = 3 * 2 * NPB * 8
        nH = 3 * NPB * 8
        t1v = S1[:, 0:nA].rearrange("p (a b n) -> p a b n", a=3, b=2)
        for ab, off in ((0, offA), (1, offB)):
            ts(P, "dve", t1v[:, :, ab, :], ANG[:, :, 0].rearrange("p a g s -> p a (g s)"), 1.0 / TWO_PI, off,
               ALU.mult, ALU.add, [tabb, ssb, cb], [ssb])
        a1, a2, ai = S1[:, 0:nA], S2[:, 0:nA], SI[:, 0:nA]
        cp(P, "dve", ai, a1, [ssb], [ssb])
        cp(P, "dve", a2, ai, [ssb], [ssb])
        tt(P, "dve", a1, a1, a2, ALU.subtract, [ssb], [ssb])
        act(P, a1, a1, AF.Sin, [ssb], [ssb], scale=TWO_PI)
        act(P, S2[:, 0:nH].rearrange("p (a n) -> p a n", a=3), ELR[:, :, 0].rearrange("p a g s -> p a (g s)"), AF.Exp,
            [ssb, tabb], [ssb])
        l2_table(Ctab, 0.25)
        tt(P, "dve", TAB[:].rearrange("p a b g s -> p a b (g s)"), t1v,
           S2[:, 0:nH].rearrange("p (a n) -> p a n", a=3).unsqueeze(2).broadcast_to([128, 3, 2, NPB * 8]), ALU.mult,
           [ssb], [tabb])
        PAs, PBs = TAB[:, 1:3, 0], TAB[:, 1:3, 1]
        cr_b = flat(sm["cr"]).unsqueeze(1).unsqueeze(3).broadcast_to([128, 2, NPB, 8])
        ci_b = flat(sm["ci"]).unsqueeze(1).unsqueeze(3).broadcast_to([128, 2, NPB, 8])
        tt(P, "dve", W["wu1"][:], PAs, cr_b, ALU.mult, [tabb, pb_], [wb_])
        tt(P, "dve", W["wu2"][:], PBs, ci_b, ALU.mult, [tabb, pb_], [wb_])
        stt(P, "dve", W["WA"][:], W["wu2"][:], sgnB, W["wu1"][:], ALU.mult, ALU.add, [wb_, cb], [wb_])
        tt(P, "dve", W["wu1"][:], PBs, cr_b, ALU.mult, [tabb, pb_, wb_], [wb_])
        tt(P, "dve", W["wu2"][:], PAs, ci_b, ALU.mult, [tabb, pb_, wb_], [wb_])
        stt(P, "dve", W["WB"][:], W["wu2"][:], sgnC, W["wu1"][:], ALU.mult, ALU.add, [wb_, cb], [wb_])

        def bc_c(t):
            return t[:].unsqueeze(2).broadcast_to([128, NPB, 8, 16])

        def bc_s(ap):
            return ap.unsqueeze(3).broadcast_to([128, NPB, 8, 16])

        def combo(out_name, a1, a2, b1, b2, sgn, op1, first_scaled):
            tt(P, "dve", big["tA"][:], a1, a2, ALU.mult, [pb_, pcb, wb_, tabb, bigb["tA"]], [bigb["tA"]])
            tt(P, "dve", big["tB"][:], b1, b2, ALU.mult, [pb_, pcb, wb_, tabb, bigb["tB"]], [bigb["tB"]])
            X, Y = (big["tA"], big["tB"]) if first_scaled == "A" else (big["tB"], big["tA"])
            stt(P, "dve", big[out_name][:], X[:], sgn, Y[:], ALU.mult, op1, [bigb["tA"], bigb["tB"], cb], [bigb[out_name]])

        WA_B, WB_B, WA_P, WB_P = W["WA"][:, 0], W["WB"][:, 0], W["WA"][:, 1], W["WB"][:, 1]
        combo("BT", bc_c(pc["bb1"]), bc_s(WA_B), bc_c(pc["bb2"]), bc_s(WB_B), sgnB, ALU.add, "B")
        combo("BS", bc_c(pc["bb1"]), bc_s(WB_B), bc_c(pc["bb2"]), bc_s(WA_B), sgnC, ALU.add, "A")
        for d2_ in range(NPB // 2):
            ps, pbk = P.psum_next()
            for q in range(2):
                dg = d2_ * 2 + q
                for kind, nm in ((0, "BT"), (1, "BS")):
                    o = ps[:, (q * 2 + kind) * 128:(q * 2 + kind + 1) * 128]
                    src = big[nm][:, dg].rearrange("p s c -> p (s c)")
                    P.op("pe", lambda e, o=o, src=src: e.transpose(o, src, ident[:]), [bigb[nm], cb], [pbk])
            cp(P, "dve", Bmat[:, d2_ * 2:d2_ * 2 + 2].rearrange("p g k n -> p (g k n)"), ps[:, :], [pbk], [bmb])
        combo("PT", bc_c(pc["bb1"]), bc_s(WA_P), bc_c(pc["bb2"]), bc_s(WB_P), sgnB, ALU.add, "B")
        PA_Q, PB_Q = TAB[:, 0, 0], TAB[:, 0, 1]
        combo("Q1", bc_c(pc["cc1"]), bc_s(PA_Q), bc_c(pc["cc2"]), bc_s(PB_Q), sgnC, ALU.subtract, "A")
        combo("Q2", bc_c(pc["cc1"]), bc_s(PB_Q), bc_c(pc["cc2"]), bc_s(PA_Q), sgnB, ALU.subtract, "B")
        act(P, Q1b[:], big["Q1"][:].rearrange("p g s c -> p g (s c)"), AF.Copy, [bigb["Q1"]], [qbb])
        act(P, Q2b[:], big["Q2"][:].rearrange("p g s c -> p g (s c)"), AF.Copy, [bigb["Q2"]], [qbb])
        l2_table(Stab, 0.0)
        pst = []
        for d in range(2):
            ps, pbk = P.psum_next()
            for g4 in range(NGB):
                dg = d * NGB + g4
                mm(P, ps[:, g4 * 128:(g4 + 1) * 128], big["PT"][:, dg].rearrange("p s c -> p (s c)"),
                   big["Q1"][:, dg].rearrange("p s c -> p (s c)"), True, True, [bigb["PT"], bigb["Q1"]], [pbk])
            pst.append((ps, pbk))
        for d, (ps, pbk) in enumerate(pst):
            tt(P, "dve", (tpa if d == 0 else tpb)[:], ps[:, 0:NGB * 128].rearrange("p (g n) -> p g n", g=NGB),
               masks[:, d, :].unsqueeze(1).broadcast_to([128, NGB, 128]), ALU.mult, [pbk, cb, tpab], [tpab])
        tt(P, "dve", tpa[:], tpa[:], tpb[:], ALU.add, [tpab], [tpab])
        tt(P, "dve", tpb[:], ident[:].unsqueeze(1).broadcast_to([128, NGB, 128]),
           dskp[:, g0:g0 + NGB].unsqueeze(2).broadcast_to([128, NGB, 128]), ALU.mult, [tpab, cb], [tpab])
        tt(P, "dve", Toep[:], tpa[:], tpb[:], ALU.add, [tpab], [toeb])
        for dg in range(NPB):
            d, g = dg // NGB, g0 + dg % NGB
            psG, pbG = P.psum_next()
            psS, pbS = P.psum_next()
            mm(P, psG[:, 0:NKK], Bmat[:, dg, 0, :], upk[:, g, :], True, True, [bmb, cb], [pbG])
            mm(P, psS[:, 0:NKK], Bmat[:, dg, 1, :], upk[:, g, :], True, True, [bmb, cb], [pbS])
            for (tdst, tb_, tab, ps, pbk) in ((t1, t1b, Ctab, psG, pbG), (t2, t2b, Stab, psS, pbS)):
                if d == 0:
                    tt(P, "dve", tdst[:, dg, :], tab[:, dg, :], ps[:, 0:NKK], ALU.mult, [l2b, pbk, tb_], [tb_])
                else:
                    tt(P, "dve", tdst[:, dg, 0:32], tab[:, dg, 0:32], ps[:, 31::-1], ALU.mult, [l2b, pbk, tb_], [tb_])
                    tt(P, "dve", tdst[:, dg, 32:NKK], tab[:, dg, 32:NKK], ps[:, NKK - 1:31:-1], ALU.mult,
                       [l2b, pbk, tb_], [tb_])
        tt(P, "dve", t1[:], t1[:], t2[:], ALU.add, [t1b, t2b], [t1b])
        P.op("dve", lambda e: e.tensor_tensor_scan(out=t2[:].rearrange("p g k -> p (g k)"),
                                                   data0=Dtab[:].rearrange("p g k -> p (g k)"),
                                                   data1=t1[:].rearrange("p g k -> p (g k)"),
                                                   initial=0.0, op0=ALU.mult, op1=ALU.add), [t1b, t2b, l2b, dtb], [t2b])
        for hdst, tab in ((hc, Ctab), (hs, Stab)):
            tt(P, "dve", hdst[:, 0:NGB, 1:NKK], tab[:, 0:NGB, 0:NKK - 1], t2[:, 0:NGB, 0:NKK - 1], ALU.mult,
               [l2b, t2b, hcb], [hcb])
            tt(P, "dve", hdst[:, NGB:NPB, 0:31], tab[:, NGB:NPB, 30::-1], t2[:, NGB:NPB, 30::-1], ALU.mult,
               [l2b, t2b, hcb], [hcb])
            tt(P, "dve", hdst[:, NGB:NPB, 32:NKK], tab[:, NGB:NPB, NKK - 2:30:-1], t2[:, NGB:NPB, NKK - 2:30:-1], ALU.mult,
               [l2b, t2b, hcb], [hcb])
        for g4 in range(NGB):
            g = g0 + g4
            ps, pbk = P.psum_next()
            mm(P, ps[:, 0:NKK], Toep[:, g4, :], upk[:, g, :], True, False, [toeb, cb], [pbk])
            for d in range(2):
                dg = d * NGB + g4
                mm(P, ps[:, 0:NKK], Q1b[:, dg, :], hc[:, dg, :], False, False, [qbb, hcb], [pbk])
                mm(P, ps[:, 0:NKK], Q2b[:, dg, :], hs[:, dg, :], False, d == 1, [qbb, hcb], [pbk])
            act(P, yst[:, g4, :], ps[:, 0:NKK], AF.Copy, [pbk], [ystb])
        if bg is not None:
            bg.pump(6)
        for s_ in range(8):
            dst = ys_dst.rows(s_ * 512 + g0 * 16, NGB * 16).rearrange("(g c) k -> c g k", c=16)
            P.dma("sp" if s_ % 2 else "act", dst, yst[s_ * 16:(s_ + 1) * 16, :, :], reads=[ystb], writes=[ys_buf])
    st.close()


class Pump:
    def __init__(self, jobs, depth):
        self.jobs, self.depth, self.i, self.h = jobs, depth, 0, {}

    def pump(self, n):
        for _ in range(n):
            if self.i >= len(self.jobs):
                return
            if self.i == 0:
                for k in range(min(self.depth, len(self.jobs))):
                    self.h[k] = self.jobs[k][0]()
            i = self.i
            self.jobs[i][1](self.h.pop(i))
            if i + self.depth < len(self.jobs):
                self.h[i + self.depth] = self.jobs[i + self.depth][0]()
            self.i += 1

    def drain(self):
        self.pump(len(self.jobs))


def k0_setup(P, condT, bmodT):
    ct = P.sbuf("ct", [128, 16, 2], F32)
    cbt = P.sbuf("cbt", [128, 16, 2], BF16)
    bt = P.sbuf("bt", [128, 2, 96], F32)
    bct, bcb, bbt = Buf(), Buf(), Buf()
    P.dma("sp", ct[:], condT, writes=[bct])
    P.dma("sp", bt[:], bmodT, writes=[bbt])
    act(P, cbt[:], ct[:], AF.Silu, [bct], [bcb])
    return {"cbt": cbt, "bcb": bcb, "bt": bt, "bbt": bbt}


def k0_jobs(P, ws, K, wmod_l, mdl, l):
    cbt, bcb, bt, bbt = K["cbt"], K["bcb"], K["bt"], K["bbt"]
    jobs = []
    for s_ in range(48):
        def load(s_=s_):
            return ws.load(wmod_l, 0, 16, s_ * 256, 256)

        def comp(h, s_=s_):
            wt, wb = h
            for jj in range(2):
                q = s_ * 2 + jj
                ps, pb = P.psum_next()
                for kc in range(16):
                    mm(P, ps[:, 0:2], wt[:, kc, jj * 128:(jj + 1) * 128], cbt[:, kc, :], kc == 0, kc == 15, [wb, bcb], [pb])
                act(P, mdl["t"][:, q, :], ps[:, 0:2], AF.Identity, [pb, bbt], [mdl["b"]], bias=bt[:, l, q:q + 1], scale=1.0)
        jobs.append((load, comp))
    return jobs


def k0_finish(P, mdl):
    mv = mdl["t"][:].rearrange("p (k c) w -> p k c w", k=6)
    ts(P, "dve", mdl["one"][:, 0], mv[:, 1], 1.0, None, ALU.add, None, [mdl["b"]], [mdl["b"]])
    ts(P, "dve", mdl["one"][:, 1], mv[:, 4], 1.0, None, ALU.add, None, [mdl["b"]], [mdl["b"]])


def k0_all(P, ws, nc, condT3, wmod, bmodT, oh_d, md, zcol):
    st = contextlib.ExitStack()
    NQ, NW = 24, 3
    NL = DEPTH * NQ * NW
    ct = P.sbuf("ct", [128, 16, NW], F32, st)
    cbt = P.sbuf("cbt", [128, 16, NW], BF16, st)
    bt = P.sbuf("bt", [128, DEPTH, NQ], F32, st)
    oh = P.sbuf("oh", [128, 2], F32, st)
    mloc = P.sbuf("mloc", [128, DEPTH, NQ, NW], F32, st)
    mall = P.sbuf("mall", [128, 4, NL], F32, st)
    bct, bcb, bbt, mlb, dlb, dgb, dgb2, mab = (Buf() for _ in range(8))
    P.dma("sp", ct[:], condT3, writes=[bct])
    P.dma("sp", bt[:], bmodT, writes=[bbt])
    P.dma("sp", oh[:], oh_d, writes=[bbt])
    act(P, cbt[:], ct[:], AF.Silu, [bct], [bcb])
    jobs = []
    for l in range(DEPTH):
        for s_ in range(NQ // 2):
            def load(s_=s_, l=l):
                return ws.load(wmod[l], 0, 16, s_ * 256, 256)

            def comp(h, s_=s_, l=l):
                wt, wb = h
                for jj in range(2):
                    q = s_ * 2 + jj
                    ps, pb = P.psum_next()
                    for kc in range(16):
                        mm(P, ps[:, 0:NW], wt[:, kc, jj * 128:(jj + 1) * 128], cbt[:, kc, :], kc == 0, kc == 15, [wb, bcb], [pb])
                    act(P, mloc[:, l, q, :], ps[:, 0:NW], AF.Identity, [pb, bbt], [mlb], bias=bt[:, l, q:q + 1], scale=1.0)
            jobs.append((load, comp))
    pipeline(jobs, 3)
    md_loc = nc.dram_tensor("md_loc", [128, NL], F32).ap()
    md_g1 = nc.dram_tensor("md_g1", [256, NL], F32).ap()
    md_g2 = nc.dram_tensor("md_g2", [512, NL], F32).ap()
    P.dma("sp", md_loc, mloc[:].rearrange("p l q w -> p (l q w)"), reads=[mlb], writes=[dlb])
    P.collective("AllGather", [md_loc], [md_g1], [[0, 1], [2, 3], [4, 5], [6, 7]], reads=[dlb], writes=[dgb])
    P.collective("AllGather", [md_g1], [md_g2], [[0, 4], [1, 5], [2, 6], [3, 7]], reads=[dgb], writes=[dgb2])
    P.dma("sp", mall[:], md_g2.rearrange("(k p) n -> p k n", p=128), reads=[dgb2], writes=[mab])
    for l in range(DEPTH):
        src = mall[:, :, l * NQ * NW:(l + 1) * NQ * NW].rearrange("p k (q w) -> p k q w", w=NW)
        dst = md[l]["t"][:].rearrange("p (k q) w -> p k q w", k=4)
        mb_ = md[l]["b"]
        act(P, dst[:, :, :, 0], src[:, :, :, 0], AF.Identity, [mab, bbt], [mb_], bias=zcol, scale=oh[:, 0:1])
        stt(P, "dve", dst[:, :, :, 0], src[:, :, :, 1], oh[:, 1:2], dst[:, :, :, 0], ALU.mult, ALU.add, [mab, bbt, mb_], [mb_])
        cp(P, "dve", dst[:, :, :, 1], src[:, :, :, 2], [mab], [mb_])
        k0_finish(P, md[l])
    P.phase()
    st.close()


def emit_ka(P, ws, C, x_src, xsrc_buf, pos_src, x0_dst, x0_buf, mdl, w_in, U_dst, u_buf, h_dst, h_buf, T=1152):
    st = contextlib.ExitStack()
    P.phase()
    tiles = token_tiles(T)
    nt = len(tiles)
    md = mdl["t"][:].rearrange("p (k c) w -> p k c w", k=6)
    onep = mdl["one"]
    x = P.sbuf("x", [128, 16, T], F32, st)
    xb = grid(16, nt, "x")
    h = P.sbuf("h", [128, 16, T], BF16, st)
    hb = grid(16, nt, "h")
    for c in range(16):
        P.dma("sp", x[:, c, :], x_src[:, c, :], reads=[xsrc_buf], writes=xb[c])
    if pos_src is not None:
        pos = P.sbuf("pos", [128, 4, 1024], F32, st)
        pbufs = [Buf() for _ in range(4)]
        for c in range(16):
            pbf = pbufs[c % 4]
            P.dma("act", pos[:, c % 4, :], pos_src[:, c, :], writes=[pbf])
            tt(P, "dve", x[:, c, 0:1024], x[:, c, 0:1024], pos[:, c % 4, :], ALU.add, [pbf] + xb[c], xb[c])
            P.dma("sp", x0_dst[:, c, :], x[:, c, :], reads=xb[c], writes=[x0_buf])
    S = ln_scratch(P, st)
    emit_ln(P, S, C, x, xb, h, hb, tiles, lambda c, mi: onep[:, 0, c, mi:mi + 1], lambda c, mi: md[:, 0, c, mi:mi + 1],
            extra=[mdl["b"]])
    for c in range(16):
        P.dma("act", h_dst[:, c, :], h[:, c, :], reads=hb[c], writes=[h_buf])
    stg = [(P.sbuf("stg", [128, T], BF16, st), Buf()) for _ in range(2)]
    jobs = []
    for s_ in range(8):
        def load(s_=s_):
            return ws.load(w_in, 0, 16, s_ * 256, 256)

        def comp(hd, s_=s_):
            wt, wb = hd
            for jj in range(2):
                j = s_ * 2 + jj
                stt_, stb = stg[j % 2]
                for ti, (t0, tn, _) in enumerate(tiles):
                    ps, pb = P.psum_next()
                    for kc in range(16):
                        mm(P, ps[:, 0:tn], wt[:, kc, jj * 128:(jj + 1) * 128], h[:, kc, t0:t0 + tn], kc == 0, kc == 15,
                           [wb, hb[kc][ti]], [pb])
                    if j < 8:
                        o = stt_[:].rearrange("p (s k) -> p s k", s=8)[:, :, t0 // 8:(t0 + tn) // 8]
                        act(P, o, ps[:, 0:tn].rearrange("p (k s) -> p s k", s=8), AF.Copy, [pb], [stb])
                    else:
                        act(P, stt_[:, t0:t0 + tn], ps[:, 0:tn], AF.Copy, [pb], [stb])
                P.dma("sp", U_dst.rows(j * 128, 128), stt_[:], reads=[stb], writes=[u_buf])
        jobs.append((load, comp))
    pipeline(jobs, 3)
    st.close()


def emit_kf(P, FC, U_all, ubufs, yf_dst, yf_buf, pch, need_ctx):
    st = contextlib.ExitStack()
    P.phase()
    uf = P.sbuf("uf", [128, 4, NTOK], BF16, st)
    ufb = Buf()
    for ph in range(2):
        src = U_all[ph][(8 + 4 * pch) * 128:(12 + 4 * pch) * 128, :].rearrange("(c p) t -> p c t", p=128)
        P.dma("sp", uf[:, :, CTX + ph * 1024: CTX + (ph + 1) * 1024], src[:, :, 0:1024], reads=[ubufs[ph]], writes=[ufb])
        P.dma("act", uf[:, :, ph * 128:(ph + 1) * 128], src[:, :, 1024:1152], reads=[ubufs[ph]], writes=[ufb])
    dftc, dpc, cb = FC["dftc"], FC["dpc"], FC["b"]
    dpl_d = FC["dpl_d"]
    V = P.sbuf("V", [128, 18, 2, 512], BF16, st)
    vb = [[Buf() for _ in range(2)] for _ in range(18)]
    yo = P.sbuf("yo", [128, 4, NTOK], BF16, st)
    yob = [Buf() for _ in range(4)]
    k = 0
    for tt_ in range(18):
        for grp in range(2):
            ps, pb = P.psum_next()
            for kc in range(2):
                mm(P, ps[:, :], uf[:, grp * 2 + kc, tt_ * 128:(tt_ + 1) * 128], dftc[:, kc, :], kc == 0, kc == 1, [ufb, cb], [pb])
            k += 1
            if k % 2:
                act(P, V[:, tt_, grp, :], ps[:, :], AF.Copy, [pb], [vb[tt_][grp]])
            else:
                cp(P, "dve", V[:, tt_, grp, :], ps[:, :], [pb], [vb[tt_][grp]])
    if need_ctx:
        for grp in range(2):
            for half in range(2):
                ps, pb = P.psum_next()
                n = 0
                for tt_ in range(2):
                    for cs in range(2):
                        mm(P, ps[:, 0:256], V[:, tt_, grp, cs * 256 + half * 128: cs * 256 + half * 128 + 128],
                           dpc[:, tt_, cs, :], n == 0, n == 3, [vb[tt_][grp], cb], [pb])
                        n += 1
                act(P, yo[:, grp * 2 + half, 0:256], ps[:, 0:256], AF.Copy, [pb], [yob[grp * 2 + half]])
    else:
        for c in range(4):
            P.op("dve", lambda e, c=c: e.memset(yo[:, c, 0:256], 0.0), [], [yob[c]])
    dts = [(P.sbuf("dpl", [128, 16, 2, 512], BF16, st), Buf()) for _ in range(2)]
    jobs = []
    for kb in range(4):
        def load(kb=kb):
            t, b = dts[kb % 2]
            for cs in range(2):
                src = dpl_d[:, cs, kb * 512:(kb + 1) * 512].rearrange("(tt p) k -> p tt k", p=128)
                P.dma("sp", t[:, :, cs, :], src, writes=[b])
            return t, b

        def comp(hd, kb=kb):
            t, b = hd
            for grp in range(2):
                for half in range(2):
                    ps, pb = P.psum_next()
                    n = 0
                    for tt_ in range(16):
                        for cs in range(2):
                            mm(P, ps[:, :], V[:, 2 + tt_, grp, cs * 256 + half * 128: cs * 256 + half * 128 + 128],
                               t[:, tt_, cs, :], n == 0, n == 31, [vb[2 + tt_][grp], b], [pb])
                            n += 1
                    o = yo[:, grp * 2 + half, 256 + kb * 512: 256 + (kb + 1) * 512]
                    if (grp + half) % 2:
                        act(P, o, ps[:, :], AF.Copy, [pb], [yob[grp * 2 + half]])
                    else:
                        cp(P, "dve", o, ps[:, :], [pb], [yob[grp * 2 + half]])
        jobs.append((load, comp))
    pipeline(jobs, 2)
    for c in range(4):
        P.dma("sp", yf_dst.rows(c * 128, 128), yo[:, c, :], reads=[yob[c]], writes=[yf_buf])
    st.close()


def emit_kc(P, ws, C, h_src, h_buf, x_src, xsrc_buf, mdl, lnp, lnpb, Ys_all, ysbufs, Yf_all, yfbufs, W, out_dst, out_buf, ph, T, blend=None):
    stp = contextlib.ExitStack()
    P.phase()
    tiles = token_tiles(T)
    nt = len(tiles)
    KL = T // 8
    md = mdl["t"][:].rearrange("p (k c) w -> p k c w", k=6)
    onep = mdl["one"]
    mdb = mdl["b"]
    h = P.sbuf("h", [128, 16, T], BF16, stp)
    hb = grid(16, nt, "h")
    mg = P.sbuf("mg", [128, 16, T], BF16, stp)
    mgb = grid(16, nt, "mg")
    S = ln_scratch(P, stp)
    sig = [(P.sbuf("sig", [128, 512], F32, stp), Buf()) for _ in range(2)]
    sigk = [0]

    def nsig():
        sigk[0] += 1
        return sig[sigk[0] % 2]

    bl = blend

    def load_h():
        for c in range(16):
            P.dma("sp" if c % 2 else "act", h[:, c, :], h_src[:, c, 0:T], reads=[h_buf], writes=hb[c])

    st2 = contextlib.ExitStack()
    sgl = P.sbuf("bufA", [128, 8, T], BF16, st2)
    g = P.sbuf("bufB", [128, 8, T], BF16, st2)
    yf = P.sbuf("bufC", [128, 8, T], BF16, st2)
    sgt = P.sbuf("sgt", [128, 2, T], BF16, st2)
    t1 = P.sbuf("t1", [128, 2, T], F32, st2)
    ab = [Buf("ysd") for _ in range(8)]
    gb = grid(8, nt, "g")
    sglb = grid(8, nt, "sgl")
    yfb = [Buf("yf") for _ in range(8)]
    sgtb = grid(2, nt, "sgt")
    t1b = grid(2, nt, "t1")
    ysd = sgl[:].rearrange("p j (s k) -> p j s k", s=8)
    ysdB = g[:].rearrange("p j (s k) -> p j s k", s=8)
    for pp in range(2):
        gw = [b_ for c in range(4 * pp, 4 * pp + 4) for b_ in gb[c]]
        for s_ in range(8):
            src = Ys_all.rows(s_ * 512, 512, slot=pp).rearrange("(j p) k -> p j k", p=128)
            P.dma("sp", ysd[:, 4 * pp:4 * pp + 4, s_, 0:128], src[:, :, 32:160], reads=[ysbufs[pp]], writes=ab[4 * pp:4 * pp + 4])
            P.dma("act", ysdB[:, 4 * pp:4 * pp + 4, s_, 0:128], src[:, :, 160:288], reads=[ysbufs[pp]], writes=gw)
            if T > 1024:
                P.dma("sp", ysd[:, 4 * pp:4 * pp + 4, s_, 128:144], src[:, :, 0:16], reads=[ysbufs[pp]], writes=ab[4 * pp:4 * pp + 4])
                P.dma("act", ysdB[:, 4 * pp:4 * pp + 4, s_, 128:144], src[:, :, 16:32], reads=[ysbufs[pp]], writes=gw)
    for pp in range(2):
        for c in range(4):
            cc_ = 4 * pp + c
            sf = Yf_all.rows(c * 128, 128, slot=pp)
            P.dma("sp", yf[:, cc_, 0:1024], sf[:, CTX:CTX + 1024], reads=[yfbufs[pp]], writes=[yfb[cc_]])
            P.dma("act", mg[:, cc_, 0:1024], sf[:, CTX + 1024:CTX + 2048], reads=[yfbufs[pp]], writes=mgb[cc_])
            if T > 1024:
                P.dma("sp", yf[:, cc_, 1024:1152], sf[:, 0:128], reads=[yfbufs[pp]], writes=[yfb[cc_]])
                P.dma("act", mg[:, cc_, 1024:1152], sf[:, 128:256], reads=[yfbufs[pp]], writes=mgb[cc_])
    load_h()
    for c in range(8):
        act(P, g[:, c, :], g[:, c, :], AF.Identity, gb[c] + [bl["b"]], gb[c], bias=bl["z"], scale=bl["m1"])
        stt(P, "dve", sgl[:, c, :], sgl[:, c, :], bl["m0"], g[:, c, :], ALU.mult, ALU.add, gb[c] + [ab[c], bl["b"]], [ab[c]])
    for c in range(8):
        act(P, mg[:, c, 0:T], mg[:, c, 0:T], AF.Identity, mgb[c] + [bl["b"]], mgb[c], bias=bl["z"], scale=bl["m1"])
        stt(P, "dve", yf[:, c, :], yf[:, c, :], bl["m0"], mg[:, c, 0:T], ALU.mult, ALU.add, mgb[c] + [yfb[c], bl["b"]], [yfb[c]])
    for c in range(8):
        for ti, (t0, tn, _) in enumerate(tiles):
            act(P, g[:, c, t0:t0 + tn].rearrange("p (k s) -> p k s", s=8),
                ysd[:, c, :, t0 // 8:(t0 + tn) // 8].rearrange("p s k -> p k s"), AF.Gelu, [ab[c]], [gb[c][ti]])
    for c in range(8):
        for ti in range(nt):
            sglb[c][ti].r.update({k: v for b_ in ab for k, v in b_.r.items()})
    jobs = []
    for s_ in range(4):
        def load(s_=s_):
            return ws.load(W["w_glu"], 0, 8, s_ * 256, 256)

        def comp(hd, s_=s_):
            wt, wb = hd
            for jj in range(2):
                j = s_ * 2 + jj
                for ti, (t0, tn, _) in enumerate(tiles):
                    ps, pb = P.psum_next()
                    for kc in range(8):
                        mm(P, ps[:, 0:tn], wt[:, kc, jj * 128:(jj + 1) * 128], g[:, kc, t0:t0 + tn], kc == 0, kc == 7,
                           [wb, gb[kc][ti]], [pb])
                    sg, sgb = nsig()
                    act(P, sg[:, 0:tn], ps[:, 0:tn], AF.Sigmoid, [pb], [sgb])
                    tt(P, "dve", sgl[:, j, t0:t0 + tn], g[:, j, t0:t0 + tn], sg[:, 0:tn], ALU.mult,
                       [gb[j][ti], sgb], [sglb[j][ti]])
        jobs.append((load, comp))
    for s_ in range(8):
        c0 = s_ * 256

        def mk(kind, s_=s_, c0=c0):
            def load():
                if kind == "gs":
                    return ws.load(W["w_in"], 0, 16, 2048 + c0, 256)
                if kind == "ps":
                    return ws.load(W["w_ps"], 0, 8, c0, 256)
                if kind == "gf":
                    return ws.load(W["w_in"], 0, 16, 4096 + c0, 256)
                return ws.load(W["w_pf"], 0, 8, c0, 256)

            def comp(hd):
                wt, wb = hd
                nk = 16 if kind in ("gs", "gf") else 8
                for jj in range(2):
                    j = s_ * 2 + jj
                    for ti, (t0, tn, _) in enumerate(tiles):
                        ps, pb = P.psum_next()
                        for kc in range(nk):
                            if kind in ("gs", "gf"):
                                rhs, rb = h[:, kc, t0:t0 + tn], hb[kc][ti]
                            elif kind == "ps":
                                rhs, rb = sgl[:, kc, t0:t0 + tn], sglb[kc][ti]
                            else:
                                rhs, rb = yf[:, kc, t0:t0 + tn], yfb[kc]
                            mm(P, ps[:, 0:tn], wt[:, kc, jj * 128:(jj + 1) * 128], rhs, kc == 0, kc == nk - 1, [wb, rb], [pb])
                        if kind in ("gs", "gf"):
                            act(P, sgt[:, jj, t0:t0 + tn], ps[:, 0:tn], AF.Sigmoid, [pb], [sgtb[jj][ti]])
                        elif kind == "ps":
                            tt(P, "dve", t1[:, jj, t0:t0 + tn], sgt[:, jj, t0:t0 + tn], ps[:, 0:tn], ALU.mult,
                               [pb, sgtb[jj][ti]], [t1b[jj][ti]])
                        else:
                            sg, sgb = nsig()
                            tt(P, "dve", sg[:, 0:tn], sgt[:, jj, t0:t0 + tn], ps[:, 0:tn], ALU.mult, [pb, sgtb[jj][ti]], [sgb])
                            tt(P, "dve", mg[:, j, t0:t0 + tn], sg[:, 0:tn], t1[:, jj, t0:t0 + tn], ALU.add,
                               [sgb, t1b[jj][ti]], [mgb[j][ti]])
            return (load, comp)
        for kind in ("gs", "ps", "gf", "pf"):
            jobs.append(mk(kind))
    pipeline(jobs, 3)
    st2.close()
    P.phase()

    x = P.sbuf("x2", [128, 16, T], F32, stp)
    xb = grid(16, nt, "x2")
    for c in range(16):
        P.dma("sp", x[:, c, :], x_src[:, c, 0:T], reads=[xsrc_buf], writes=xb[c])
        for ti, (t0, tn, _) in enumerate(tiles):
            act(P, x[:, c, t0:t0 + tn], x[:, c, t0:t0 + tn], AF.Copy, [xb[c][ti]], [xb[c][ti]], scale=ALPHA)
    jobs = []
    for s_ in range(8):
        def load(s_=s_):
            return ws.load(W["w_o"], 0, 16, s_ * 256, 256)

        def comp(hd, s_=s_):
            wt, wb = hd
            for jj in range(2):
                j = s_ * 2 + jj
                for ti, (t0, tn, segs) in enumerate(tiles):
                    ps, pb = P.psum_next()
                    for kc in range(16):
                        mm(P, ps[:, 0:tn], wt[:, kc, jj * 128:(jj + 1) * 128], mg[:, kc, t0:t0 + tn], kc == 0, kc == 15,
                           [wb, mgb[kc][ti]], [pb])
                    for (s0, sn, mi) in segs:
                        stt(P, "dve", x[:, j, s0:s0 + sn], ps[:, s0 - t0:s0 - t0 + sn], md[:, 2, j, mi:mi + 1],
                            x[:, j, s0:s0 + sn], ALU.mult, ALU.add, [pb, xb[j][ti], mdb], [xb[j][ti]])
        jobs.append((load, comp))
    pipeline(jobs, 3)
    emit_ln(P, S, C, x, xb, x, xb, tiles, lambda c, mi: lnp[:, 0, c:c + 1], lambda c, mi: lnp[:, 1, c:c + 1], extra=[lnpb])
    emit_ln(P, S, C, x, xb, h, hb, tiles, lambda c, mi: onep[:, 1, c, mi:mi + 1], lambda c, mi: md[:, 3, c, mi:mi + 1],
            extra=[mdb])
    for c in range(16):
        for ti, (t0, tn, _) in enumerate(tiles):
            act(P, x[:, c, t0:t0 + tn], x[:, c, t0:t0 + tn], AF.Copy, [xb[c][ti]], [xb[c][ti]], scale=ALPHA)
    jobs = []
    for hblk in range(4):
        for s_ in range(8):
            def load(s_=s_, hblk=hblk):
                return ws.load(W["w_up"], 0, 16, hblk * 2048 + s_ * 256, 256)

            def comp(hd, s_=s_):
                wt, wb = hd
                for jj in range(2):
                    j = s_ * 2 + jj
                    for ti, (t0, tn, _) in enumerate(tiles):
                        ps, pb = P.psum_next()
                        for kc in range(16):
                            mm(P, ps[:, 0:tn], wt[:, kc, jj * 128:(jj + 1) * 128], h[:, kc, t0:t0 + tn], kc == 0, kc == 15,
                               [wb, hb[kc][ti]], [pb])
                        sg, sgb = nsig()
                        act(P, sg[:, 0:tn], ps[:, 0:tn], AF.Relu, [pb], [sgb])
                        tt(P, "dve", mg[:, j, t0:t0 + tn], sg[:, 0:tn], sg[:, 0:tn], ALU.mult, [sgb], [mgb[j][ti]])
            jobs.append((load, comp))
        for s_ in range(8):
            def load(s_=s_, hblk=hblk):
                return ws.load(W["w_down"], hblk * 2048, 16, s_ * 256, 256)

            def comp(hd, s_=s_):
                wt, wb = hd
                for jj in range(2):
                    j = s_ * 2 + jj
                    for ti, (t0, tn, segs) in enumerate(tiles):
                        ps, pb = P.psum_next()
                        for kc in range(16):
                            mm(P, ps[:, 0:tn], wt[:, kc, jj * 128:(jj + 1) * 128], mg[:, kc, t0:t0 + tn], kc == 0, kc == 15,
                               [wb, mgb[kc][ti]], [pb])
                        for (s0, sn, mi) in segs:
                            stt(P, "dve", x[:, j, s0:s0 + sn], ps[:, s0 - t0:s0 - t0 + sn], md[:, 5, j, mi:mi + 1],
                                x[:, j, s0:s0 + sn], ALU.mult, ALU.add, [pb, xb[j][ti], mdb], [xb[j][ti]])
            jobs.append((load, comp))
    pipeline(jobs, 3)
    emit_ln(P, S, C, x, xb, x, xb, tiles, lambda c, mi: lnp[:, 2, c:c + 1], lambda c, mi: lnp[:, 3, c:c + 1], extra=[lnpb])
    for c in range(16):
        P.dma("sp", out_dst[:, c, 0:T], x[:, c, :], reads=xb[c], writes=[out_buf])
    stp.close()


class Chunked:
    def __init__(self, aps, rows_per):
        self.aps, self.rows_per = aps, rows_per

    def rows(self, r0, n, slot=0):
        k, o = r0 // self.rows_per, r0 % self.rows_per
        assert o + n <= self.rows_per
        return self.aps[k][slot * self.rows_per + o: slot * self.rows_per + o + n, :]


def emit_sel(P, BL, items):
    st = contextlib.ExitStack()
    P.phase()
    n = 1152
    ND = len(items)
    ta = [(P.sbuf("sela", [128, n], BF16, st), Buf()) for _ in range(ND)]
    tb = [(P.sbuf("selb", [128, n], BF16, st), Buf()) for _ in range(ND)]
    for k, (dst, srcA, srcB, rbuf, wbuf) in enumerate(items):
        P.dma("sp", ta[k][0][:], srcA, reads=[rbuf], writes=[ta[k][1]])
        P.dma("act", tb[k][0][:], srcB, reads=[rbuf], writes=[tb[k][1]])
    for k, (dst, srcA, srcB, rbuf, wbuf) in enumerate(items):
        a, ab_ = ta[k]
        b, bb_ = tb[k]
        act(P, b[:], b[:], AF.Identity, [bb_, BL["b"]], [bb_], bias=BL["z"], scale=BL["m1"])
        stt(P, "dve", a[:], a[:], BL["m0"], b[:], ALU.mult, ALU.add, [ab_, bb_, BL["b"]], [ab_])
        P.dma("sp", dst, a[:], reads=[ab_], writes=[wbuf])
    st.close()


S5P_SHAPES = {"lre2": [128, 64], "lim2": [128, 64], "lst2": [128, 64], "bb1": [128, 64, 16], "bb2": [128, 64, 16],
              "cc1": [128, 64, 16], "cc2": [128, 64, 16], "dskp": [128, 32]}
WNAMES = {"w_in": [2048, 6144], "w_glu": [1024, 1024], "w_ps": [1024, 2048], "w_pf": [1024, 2048], "w_o": [2048, 2048],
          "w_up": [2048, 8192], "w_down": [8192, 2048]}


def build_fused():
    nc = new_nc()
    condT = din(nc, "condT", [128, 16, 3])
    wmod = din(nc, "w_mod", [DEPTH, 2048, 3072])
    bmodT = din(nc, "bmodT", [128, DEPTH, 24])
    oh_d = din(nc, "onehot", [128, 2])
    xT = din(nc, "xT", [128, 16, 1152])
    posT = din(nc, "posT", [128, 16, 1024])
    lnp_d = din(nc, "lnp", [128, DEPTH, 4, 16])
    Wd = {n: din(nc, n, [DEPTH] + shp) for n, shp in WNAMES.items()}
    s5p = {n: din(nc, "s5_" + n, [DEPTH] + shp) for n, shp in S5P_SHAPES.items()}
    sc_d = {"evec": din(nc, "evec", [128, 3, 2, 8]), "pvec": din(nc, "pvec", [128, 4]), "masks": din(nc, "masks", [128, 2, 128]),
            "ident": din(nc, "ident", [128, 128]), "jv": din(nc, "jv", [128, NKK])}
    dftc_d = din(nc, "dftc", [128, 2, 512], BF16)
    dpl_d = din(nc, "dpl", [SEQ, 2, SEQ], BF16)
    dpc_d = din(nc, "dpc", [128, 2, 2, 256], BF16)
    msk_d = din(nc, "msk", [128, 3])
    outT = dout(nc, "outT", [128, 16, 1024])

    def dram(name, shape, dt=BF16):
        return nc.dram_tensor(name, shape, dt).ap()
    def chunked(name, nchunk, rows_per, cols, mult):
        return Chunked([dram(f"{name}_{k}", [mult * rows_per, cols]) for k in range(nchunk)], rows_per)
    U_loc = [chunked(f"U_loc{l}", 4, 512, 1152, 1) for l in range(DEPTH)]
    Ug = [chunked(f"Ug{l}", 4, 512, 1152, 2) for l in range(DEPTH)]
    Usel = [dram(f"Usel{i}", [2048, 1152]) for i in range(2)]
    Yf_loc = [chunked(f"Yf_loc{l}", 2, 256, NTOK, 1) for l in range(DEPTH)]
    Yfg = [chunked(f"Yfg{l}", 2, 256, NTOK, 2) for l in range(DEPTH)]
    Ys_loc = [chunked(f"Ys_loc{l}", 2, 2048, NKK, 1) for l in range(DEPTH)]
    Ysg = [chunked(f"Ysg{l}", 2, 2048, NKK, 2) for l in range(DEPTH)]

    def gather(loc, g, rb, wb):
        for a, o in zip(loc.aps, g.aps):
            P.collective("AllGather", [a], [o], GROUPS, reads=[rb], writes=[wb])
    X0 = dram("X0", [128, 16, 1152], F32)
    X1 = dram("X1", [128, 16, 1152], F32)
    Hs = dram("Hs", [128, 16, 1152])
    P = Prog(nc)
    P.psum_init()
    C = make_consts(P, nc)
    md = [{"t": P.sbuf("md", [128, 96, 2], F32), "one": P.sbuf("onep", [128, 2, 16, 2], F32), "b": Buf("md")} for _ in range(DEPTH)]
    lnp = P.sbuf("lnp", [128, DEPTH, 4, 16], F32)
    lnpb = Buf("lnp")
    P.dma("sp", lnp[:], lnp_d, writes=[lnpb])
    SC = {"b": Buf("s5c")}
    for n, d_ in sc_d.items():
        SC[n] = P.sbuf(n, list(d_.shape), F32)
        P.dma("act", SC[n][:], d_, writes=[SC["b"]])
    FC = {"b": Buf("fc"), "dpl_d": dpl_d}
    FC["dftc"] = P.sbuf("dftc", [128, 2, 512], BF16)
    FC["dpc"] = P.sbuf("dpc", [128, 2, 2, 256], BF16)
    P.dma("act", FC["dftc"][:], dftc_d, writes=[FC["b"]])
    P.dma("act", FC["dpc"][:], dpc_d, writes=[FC["b"]])
    msk = P.sbuf("msk", [128, 3], F32)
    BL = {"m0": msk[:, 0:1], "m1": msk[:, 1:2], "z": msk[:, 2:3], "b": Buf("msk")}
    P.dma("sp", msk[:], msk_d, writes=[BL["b"]])
    ext = Buf("ext")
    ws = WStream(P, 3)
    k0_all(P, ws, nc, condT, wmod, bmodT, oh_d, md, BL["z"])
    bH, bX0, bX1, bout = Buf("H"), Buf("X0"), Buf("X1"), Buf("out")
    bU = [Buf("Usel0"), Buf("Usel1")]
    GROUPS = [[0, 1], [2, 3], [4, 5], [6, 7]]
    for l in range(DEPTH):
        last = l == DEPTH - 1
        W = {n: Wd[n][l] for n in WNAMES}
        bUl, bUg, bYfl, bYfg, bYsl, bYsg = (Buf(f"{n}{l}") for n in ("Ul", "Ug", "Yfl", "Yfg", "Ysl", "Ysg"))
        if l == 0:
            emit_ka(P, ws, C, xT, ext, posT, X0, bX0, md[l], W["w_in"], U_loc[l], bUl, Hs, bH)
        else:
            emit_ka(P, ws, C, X1, bX1, None, None, None, md[l], W["w_in"], U_loc[l], bUl, Hs, bH)
        gather(U_loc[l], Ug[l], bUl, bUg)
        items = []
        for ph in range(2):
            for base in (0, 1024):
                for j in range(4):
                    r0 = base + j * 128
                    items.append((Usel[ph][r0:r0 + 128, :], Ug[l].rows(r0, 128, slot=ph), Ug[l].rows(r0 + 512, 128, slot=ph),
                                  bUg, bU[ph]))
        emit_sel(P, BL, items)
        emit_kf(P, FC, Usel, bU, Yf_loc[l], bYfl, 0, not last)
        emit_ks(P, SC, Usel, bU, {n: s5p[n][l] for n in S5P_SHAPES}, Ys_loc[l], bYsl, 0, None,
                pre=lambda l=l, bYfl=bYfl, bYfg=bYfg: gather(Yf_loc[l], Yfg[l], bYfl, bYfg))
        gather(Ys_loc[l], Ysg[l], bYsl, bYsg)
        Yf_all, Ys_all = Yfg[l], Ysg[l]
        if not last:
            emit_kc(P, ws, C, Hs, bH, X0, bX0, md[l], lnp[:, l], lnpb, Ys_all, [bYsg, bYsg], Yf_all, [bYfg, bYfg], W,
                    X1, bX1, None, 1152, blend=BL)
        else:
            emit_kc(P, ws, C, Hs, bH, X1, bX1, md[l], lnp[:, l], lnpb, Ys_all, [bYsg, bYsg], Yf_all, [bYfg, bYfg], W,
                    outT, bout, None, 1024, blend=BL)
    P.finish()
    return nc


_NC = {}


def kernel(**inp):
    inp = {k: np.asarray(v) for k, v in inp.items()}
    x, ctx = inp["x"], inp["ctx"]
    pos = pos_table()
    fc = fnet_consts()
    sc = s5_consts()
    if "nc" not in _NC:
        _NC["nc"] = build_fused()
    posT = [fm(pos[p * 1024:(p + 1) * 1024]) for p in range(2)]
    lnp = np.stack([np.stack([inp["ln1_g"][l], inp["ln1_b"][l], inp["ln2_g"][l], inp["ln2_b"][l]]) for l in range(DEPTH)])
    lnp = np.ascontiguousarray(lnp.reshape(DEPTH, 4, 16, 128).transpose(3, 0, 1, 2))
    bmodT = np.ascontiguousarray(inp["b_mod"].reshape(DEPTH, 96, 128).transpose(2, 0, 1))
    s5 = []
    for p in range(2):
        hp = [host_s5_params(inp, l, p) for l in range(DEPTH)]
        s5.append({"s5_" + n: np.ascontiguousarray(np.stack([hp[l][n] for l in range(DEPTH)])) for n in S5P_SHAPES})
    shared = {"lnp": lnp, **{n: inp[n] for n in WNAMES}, **sc, **fc}
    maps = []
    for i in range(NCORES):
        b, r = i // 2, i % 2
        k, bl_ = (i // 4) * 2 + r, (i % 4) // 2
        onehot = np.zeros((128, 2), np.float32)
        onehot[:, i // 4] = 1.0
        cond = np.stack([inp["c"][bl_], inp["c"][bl_ + 2], inp["c_ctx"]])
        condT = np.ascontiguousarray(cond.reshape(3, 16, 128).transpose(2, 1, 0))
        wm = np.ascontiguousarray(inp["w_mod"][:, :, 3072 * k:3072 * (k + 1)])
        bm = np.ascontiguousarray(bmodT[:, :, 24 * k:24 * (k + 1)])
        xt = fm(np.concatenate([x[b, r * 1024:(r + 1) * 1024], ctx[b, r * 128:(r + 1) * 128]], axis=0))
        msk = np.ascontiguousarray(np.broadcast_to(np.array([1.0 - r, float(r), 0.0], np.float32)[None], (128, 3)))
        maps.append({"w_mod": wm, "bmodT": bm, "onehot": onehot, "condT": condT, "xT": xt, "posT": posT[r], "msk": msk, **s5[r], **shared})
    res = run(_NC["nc"], maps)
    out = np.zeros((NB, SEQ, D), np.float32)
    for i in range(NCORES):
        b, p = i // 2, i % 2
        out[b, p * 1024:(p + 1) * 1024] = unfm(res[i]["outT"])
    return out
```

```python
import contextlib
import math
import numpy as np
import ml_dtypes
import concourse.bass as bass
import concourse.mybir as mybir
from concourse.bass_utils import run_bass_kernel_spmd

F32 = mybir.dt.float32
BF16 = mybir.dt.bfloat16
I32 = mybir.dt.int32
AF = mybir.ActivationFunctionType
ALU = mybir.AluOpType
NPBF = ml_dtypes.bfloat16

D = 2048
NB = 4
SEQ = 2048
CTX = 256
DEPTH = 2
DFF = 8192
ALPHA = (2 * DEPTH) ** 0.25
EPS = 1e-5
NCORES = 8
TWO_PI = 2.0 * math.pi

ENGS = ("pe", "act", "dve", "pool", "sp")


class Buf:
    __slots__ = ("name", "w", "r")

    default_fence = {}

    def __init__(self, name=""):
        self.name = name
        self.w = {}
        self.r = dict(Buf.default_fence)


class Prog:
    def __init__(self, nc, n_dma_sems=8):
        self.nc = nc
        self.ops = {e: [] for e in ENGS}
        self.cnt = {e: 0 for e in ENGS}
        self.known = {e: {} for e in ENGS}
        self.n_dma_sems = n_dma_sems
        self.dma_rr = {e: 0 for e in ("sp", "act", "pool")}
        self.dma_cnt = {}
        self.stack = contextlib.ExitStack()
        self.sems = {}
        self.all_toks = {}
        self._uid = 0
        self.psum_tiles = None
        self.psum_i = 0
        Buf.default_fence = {}
        self.cc_scratch = self.sbuf("ccscr", [128, 8], F32)
        self.cc_sems = [self.stack.enter_context(self.nc.semaphore(f"cc_sem{i}")) for i in range(18)]
        self.cc_n = 0

    def sbuf(self, name, shape, dt, stack=None):
        self._uid += 1
        st = stack if stack is not None else self.stack
        return st.enter_context(self.nc.sbuf_tensor(f"{name}_{self._uid}", list(shape), dt))

    def psum_init(self):
        self.psum_tiles = []
        for i in range(8):
            t = self.stack.enter_context(self.nc.psum_tensor(f"ps{i}", [128, 512], F32))
            self.psum_tiles.append((t, Buf(f"ps{i}")))

    def psum_next(self):
        t = self.psum_tiles[self.psum_i]
        self.psum_i = (self.psum_i + 1) % 8
        return t

    def _sem(self, key):
        if key not in self.sems:
            nm = "s_" + "_".join(str(k) for k in key)
            self.sems[key] = self.stack.enter_context(self.nc.semaphore(nm))
        return self.sems[key]

    def _deps(self, reads, writes):
        deps = []
        for b in reads:
            deps.extend(b.w.items())
        for b in writes:
            deps.extend(b.w.items())
            deps.extend(b.r.items())
        return deps

    def _mark(self, reads, writes, tok):
        for b in reads:
            if b.r.get(tok[0], 0) < tok[1]:
                b.r[tok[0]] = tok[1]
        for b in writes:
            if b.w.get(tok[0], 0) < tok[1]:
                b.w[tok[0]] = tok[1]
            b.r = {}
        if self.all_toks.get(tok[0], 0) < tok[1]:
            self.all_toks[tok[0]] = tok[1]

    def _waits(self, eng, deps, extra=()):
        need = {}
        for (key, val) in list(deps) + list(extra):
            if val <= 0:
                continue
            if need.get(key, 0) < val:
                need[key] = val
        out = []
        kn = self.known[eng]
        for key, val in need.items():
            if kn.get(key, 0) >= val:
                continue
            kn[key] = val
            out.append((key, val))
        return out

    def op(self, eng, fn, reads=(), writes=()):
        deps = self._deps(reads, writes)
        if eng == "pe":
            deps = [d for d in deps if d[0] != ("c", "pe")]
        waits = self._waits(eng, deps)
        self.cnt[eng] += 1
        tok = (("c", eng), self.cnt[eng])
        self.ops[eng].append((waits, fn, tok))
        self._mark(reads, writes, tok)
        return tok

    def dma(self, eng, out_ap, in_ap, reads=(), writes=(), **kw):
        k = self.dma_rr[eng]
        self.dma_rr[eng] = (k + 1) % self.n_dma_sems
        key = ("d", eng, k)
        n = self.dma_cnt.get(key, 0)
        deps = self._deps(reads, writes)
        waits = self._waits(eng, deps, extra=[(key, 16 * n)])
        self.dma_cnt[key] = n + 1
        tok = (key, 16 * (n + 1))

        def fn(e, out_ap=out_ap, in_ap=in_ap, kw=kw):
            return e.dma_start(out=out_ap, in_=in_ap, **kw)
        self.ops[eng].append((waits, fn, tok))
        self._mark(reads, writes, tok)
        return tok

    def collective(self, kind, ins, outs, groups, reads=(), writes=()):
        key = ("cc", self.cc_n)
        self.sems[key] = self.cc_sems[self.cc_n]
        self.cc_n += 1
        deps = self._deps(reads, writes)
        waits = self._waits("pool", deps)

        def fn(e):
            return e.collective_compute(kind, ALU.bypass, replica_groups=groups,
                                        ins=[a.opt() for a in ins], outs=[a.opt() for a in outs])
        self.ops["pool"].append((waits, fn, (key, 1)))
        self.known["pool"][key] = 1
        self.ops["pool"].append(([(key, 1)], None, None))
        scr = self.cc_scratch
        return self.op("pool", lambda e: e.memset(scr[:], 0.0), reads, writes)

    def phase(self):
        Buf.default_fence = dict(self.all_toks)

    def fence(self):
        return dict(self.all_toks)

    def newbuf(self, name="", fence=None):
        b = Buf(name)
        if fence:
            b.r.update(fence)
        return b

    def finish(self):
        waits = self._waits("sp", list(self.all_toks.items()))
        self.ops["sp"].append((waits, None, None))
        nc = self.nc
        for e in ENGS:
            self._sem(("c", e))
        for e in ("sp", "act", "pool"):
            for k in range(self.n_dma_sems):
                self._sem(("d", e, k))
        for e in ENGS:
            for waits_, fn_, tok_ in self.ops[e]:
                if tok_ is not None:
                    self._sem(tok_[0])
        engmap = {"pe": "tensor", "act": "scalar", "dve": "vector", "pool": "gpsimd", "sp": "sync"}
        with nc.Block() as block:
            for e in ENGS:
                ops = self.ops[e]

                def body(engine, ops=ops):
                    for waits, fn, tok in ops:
                        for key, val in waits:
                            engine.wait_ge(self._sem(key), val)
                        if fn is None:
                            continue
                        ins = fn(engine)
                        key, val = tok
                        (ins.then_inc(self._sem(key)) if key[0] == "cc" else ins.then_inc(self._sem(key), 1 if key[0] == "c" else 16))
                getattr(block, engmap[e])(body)
        self.stack.close()


def new_nc():
    return bass.Bass("TRN2", target_bir_lowering=False)


def din(nc, name, shape, dt=F32):
    return nc.dram_tensor(name, list(shape), dt, kind="ExternalInput").ap()


def dout(nc, name, shape, dt=F32):
    return nc.dram_tensor(name, list(shape), dt, kind="ExternalOutput").ap()


def pipeline(jobs, depth):
    handles = {}
    for i in range(min(depth, len(jobs))):
        handles[i] = jobs[i][0]()
    for i in range(len(jobs)):
        jobs[i][1](handles.pop(i))
        nxt = i + depth
        if nxt < len(jobs):
            handles[nxt] = jobs[nxt][0]()


class WStream:
    def __init__(self, P, nbuf, kc=16, ncol=256, stack=None):
        self.P = P
        self.tiles = [(P.sbuf("wslab", [128, kc, ncol], BF16, stack), Buf("wslab")) for _ in range(nbuf)]
        self.i = 0

    def load(self, w2d, r0, nk, c0, ncol):
        t, b = self.tiles[self.i]
        self.i = (self.i + 1) % len(self.tiles)
        src = w2d[r0:r0 + nk * 128, c0:c0 + ncol].rearrange("(kc p) n -> p kc n", p=128)
        self.P.dma("pool", t[:, 0:nk, 0:ncol], src, writes=[b])
        return t, b


def token_tiles(T):
    if T == 1024:
        return [(0, 512, [(0, 512, 0)]), (512, 512, [(512, 512, 0)])]
    assert T == 1152
    return [(0, 384, [(0, 384, 0)]), (384, 384, [(384, 384, 0)]), (768, 384, [(768, 256, 0), (1024, 128, 1)])]


def mm(P, out, lhsT, rhs, start, stop, reads, writes):
    return P.op("pe", lambda e: e.matmul(out, lhsT=lhsT, rhs=rhs, start=start, stop=stop), reads, writes)


def act(P, out, in_, func, reads, writes, bias=None, scale=None):
    kw = {}
    if bias is not None:
        kw["bias"] = bias
    if scale is not None:
        kw["scale"] = scale
    return P.op("act", lambda e: e.activation(out=out, in_=in_, func=func, **kw), reads, writes)


def tt(P, eng, out, in0, in1, op, reads, writes):
    return P.op(eng, lambda e: e.tensor_tensor(out=out, in0=in0, in1=in1, op=op), reads, writes)


def ts(P, eng, out, in0, s1, s2, op0, op1, reads, writes):
    if s2 is None:
        return P.op(eng, lambda e: e.tensor_scalar(out=out, in0=in0, scalar1=s1, scalar2=None, op0=op0), reads, writes)
    return P.op(eng, lambda e: e.tensor_scalar(out=out, in0=in0, scalar1=s1, scalar2=s2, op0=op0, op1=op1), reads, writes)


def stt(P, eng, out, in0, scalar, in1, op0, op1, reads, writes):
    return P.op(eng, lambda e: e.scalar_tensor_tensor(out=out, in0=in0, scalar=scalar, in1=in1, op0=op0, op1=op1),
                reads, writes)


def cp(P, eng, out, in_, reads, writes):
    return P.op(eng, lambda e: e.tensor_copy(out=out, in_=in_), reads, writes)


def emit_ln_stats(P, x, xb, tiles, C):
    T = x.shape[2]
    mean = P.sbuf("mean", [128, T], F32)
    rstd = P.sbuf("rstd", [128, T], F32)
    sq = [(P.sbuf("sq", [128, 384], F32), Buf("sq")) for _ in range(3)]
    tmp = P.sbuf("lntmp", [128, 384], F32)
    tmpb = Buf("lntmp")
    sb = []
    for ti, (t0, tn, _) in enumerate(tiles):
        ps1, pb1 = P.psum_next()
        ps2, pb2 = P.psum_next()
        for c in range(16):
            xc, xcb = S["xc"][c % 3]
            if c % 2:
                cp(P, "dve", xc[:, 0:tn], x[:, c, t0:t0 + tn], [xb[c][ti]], [xcb])
            else:
                act(P, xc[:, 0:tn], x[:, c, t0:t0 + tn], AF.Copy, [xb[c][ti]], [xcb])
            mm(P, ps1[:, 0:tn], C["onesb"][:], xc[:, 0:tn], c == 0, c == 15, [xcb, C["b"]], [pb1])
            s, sbf = sq[c % 3]
            act(P, s[:, 0:tn], x[:, c, t0:t0 + tn], AF.Square, [xb[c][ti]], [sbf])
            mm(P, ps2[:, 0:tn], C["ones"][:], s[:, 0:tn], c == 0, c == 15, [sbf, C["b"]], [pb2])
        mb = Buf("mean")
        ts(P, "dve", mean[:, t0:t0 + tn], ps1[:, 0:tn], 1.0 / D, None, ALU.mult, None, [pb1], [mb])
        tt(P, "dve", tmp[:, 0:tn], mean[:, t0:t0 + tn], mean[:, t0:t0 + tn], ALU.mult, [mb], [tmpb])
        stt(P, "dve", tmp[:, 0:tn], ps2[:, 0:tn], 1.0 / D, tmp[:, 0:tn], ALU.mult, ALU.subtract, [pb2, tmpb], [tmpb])
        ts(P, "dve", tmp[:, 0:tn], tmp[:, 0:tn], EPS, None, ALU.add, None, [tmpb], [tmpb])
        act(P, tmp[:, 0:tn], tmp[:, 0:tn], AF.Ln, [tmpb], [tmpb])
        act(P, rstd[:, t0:t0 + tn], tmp[:, 0:tn], AF.Exp, [tmpb], [mb], scale=-0.5)
        sb.append(mb)
    return mean, rstd, sb


def emit_norm_affine(P, x, xb, out, outb, mean, rstd, sb, tiles, scale_fn, bias_fn, tmps):
    k = 0
    for ti, (t0, tn, segs) in enumerate(tiles):
        for c in range(16):
            tmp, tb = tmps[k % len(tmps)]
            k += 1
            tt(P, "dve", tmp[:, 0:tn], x[:, c, t0:t0 + tn], mean[:, t0:t0 + tn], ALU.subtract, [xb[c][ti], sb[ti]], [tb])
            tt(P, "dve", tmp[:, 0:tn], tmp[:, 0:tn], rstd[:, t0:t0 + tn], ALU.mult, [tb, sb[ti]], [tb])
            act(P, out[:, c, t0:t0 + tn], tmp[:, 0:tn], AF.Identity, [tb], [outb[c][ti]],
                bias=bias_fn(c, mi), scale=scale_fn(c, mi))


def make_consts(P, nc):
    ones = P.sbuf("ones", [128, 128], F32)
    b = Buf("ones")
    P.op("dve", lambda e: e.memset(ones[:], 1.0), writes=[b])
    onesb = P.sbuf("onesb", [128, 128], BF16)
    P.op("dve", lambda e: e.memset(onesb[:], 1.0), writes=[b])
    return {"ones": ones, "onesb": onesb, "b": b}


def grid(n, m, name):
    return [[Buf(f"{name}{i}_{j}") for j in range(m)] for i in range(n)]


def ln_scratch(P, st=None):
    return {
        "mean": P.sbuf("mean", [128, 1152], F32, st), "rstd": P.sbuf("rstd", [128, 1152], F32, st),
        "mbs": [Buf("meanb") for _ in range(3)],
        "sq": [(P.sbuf("sq", [128, 512], BF16, st), Buf("sq")) for _ in range(3)],
        "xc": [(P.sbuf("xc", [128, 512], BF16, st), Buf("xc")) for _ in range(3)],
        "tmp": P.sbuf("lntmp", [128, 512], F32, st), "tmpb": Buf("lntmp"), "mb": Buf("meanb"),
        "tmps": [(P.sbuf("nt", [128, 512], F32, st), Buf()) for _ in range(2)], "k": 0,
    }


def emit_ln(P, S, C, x, xb, out, outb, tiles, scale_fn, bias_fn, extra=()):
    mean, rstd, tmp, tmpb = S["mean"], S["rstd"], S["tmp"], S["tmpb"]
    for ti, (t0, tn, segs) in enumerate(tiles):
        mb = S["mbs"][ti]
        ps1, pb1 = P.psum_next()
        ps2, pb2 = P.psum_next()
        for c in range(16):
            mm(P, ps1[:, 0:tn], C["ones"][:], x[:, c, t0:t0 + tn], c == 0, c == 15, [xb[c][ti], C["b"]], [pb1])
            sq, sbf = S["sq"][c % 3]
            act(P, sq[:, 0:tn], x[:, c, t0:t0 + tn], AF.Square, [xb[c][ti]], [sbf])
            mm(P, ps2[:, 0:tn], C["onesb"][:], sq[:, 0:tn], c == 0, c == 15, [sbf, C["b"]], [pb2])
        ts(P, "dve", mean[:, t0:t0 + tn], ps1[:, 0:tn], 1.0 / D, None, ALU.mult, None, [pb1], [mb])
        tt(P, "dve", tmp[:, 0:tn], mean[:, t0:t0 + tn], mean[:, t0:t0 + tn], ALU.mult, [mb], [tmpb])
        stt(P, "dve", tmp[:, 0:tn], ps2[:, 0:tn], 1.0 / D, tmp[:, 0:tn], ALU.mult, ALU.subtract, [pb2, tmpb], [tmpb])
        ts(P, "dve", tmp[:, 0:tn], tmp[:, 0:tn], EPS, None, ALU.add, None, [tmpb], [tmpb])
        act(P, tmp[:, 0:tn], tmp[:, 0:tn], AF.Ln, [tmpb], [tmpb])
        act(P, rstd[:, t0:t0 + tn], tmp[:, 0:tn], AF.Exp, [tmpb], [mb], scale=-0.5)
    for ti, (t0, tn, segs) in enumerate(tiles):
        mb = S["mbs"][ti]
        for c in range(16):
            t2, tb = S["tmps"][S["k"] % 2]
            S["k"] += 1
            tt(P, "dve", t2[:, 0:tn], x[:, c, t0:t0 + tn], mean[:, t0:t0 + tn], ALU.subtract, [xb[c][ti], mb], [tb])
            tt(P, "dve", t2[:, 0:tn], t2[:, 0:tn], rstd[:, t0:t0 + tn], ALU.mult, [tb, mb], [tb])
            for (s0, sn, mi) in segs:
                act(P, out[:, c, s0:s0 + sn], t2[:, s0 - t0:s0 - t0 + sn], AF.Identity, [tb] + list(extra), [outb[c][ti]],
                    bias=bias_fn(c, mi), scale=scale_fn(c, mi))


NTOK = CTX + SEQ


def fnet_consts():
    ch = np.arange(256)
    ang = 2 * np.pi * np.outer(ch, ch) / 256.0
    dc = np.concatenate([np.cos(ang), np.sin(ang)], axis=1) / 16.0
    dftc = dc.reshape(2, 128, 512).transpose(1, 0, 2)
    t = np.arange(SEQ)
    angl = 2 * np.pi * (np.outer(t, t) % SEQ) / SEQ
    dpl = np.stack([np.cos(angl), -np.sin(angl)], axis=1) / np.sqrt(SEQ)
    tc = np.arange(CTX)
    angc = 2 * np.pi * (np.outer(tc, tc) % CTX) / CTX
    dpc = np.stack([np.cos(angc), -np.sin(angc)], axis=1) / np.sqrt(CTX)
    dpc = dpc.reshape(2, 128, 2, 256).transpose(1, 0, 2, 3)
    return {"dftc": np.ascontiguousarray(dftc).astype(NPBF), "dpl": np.ascontiguousarray(dpl).astype(NPBF),
            "dpc": np.ascontiguousarray(dpc).astype(NPBF)}


NKK = NTOK // 8
NGB = 4
NPB = 2 * NGB


def s5_consts():
    s = np.arange(8, dtype=np.float32)
    ev = np.zeros((3, 2, 8), np.float32)
    ev[0, 0], ev[0, 1] = s + 1, 8 - s
    ev[1, 0], ev[1, 1] = 7 - s, s
    ev[2, 0], ev[2, 1] = -(s + 1), -(8 - s)
    evec = np.broadcast_to(ev[None], (128, 3, 2, 8)).copy()
    pv = np.zeros((128, 4), np.float32)
    pv[:64, 0] = 0.25
    pv[64:, 1] = 0.25
    pv[:64, 2], pv[64:, 2] = 1.0, -1.0
    pv[:64, 3], pv[64:, 3] = -1.0, 1.0
    si = np.arange(128) // 16
    mf = (si[None, :] >= si[:, None]).astype(np.float32)
    mb = (si[:, None] >= si[None, :]).astype(np.float32)
    masks = np.stack([mf, mb], axis=1)
    ident = np.eye(128, dtype=np.float32)
    jv = np.broadcast_to(np.arange(NKK, dtype=np.float32)[None], (128, NKK)).copy()
    return {"evec": evec, "pvec": pv, "masks": np.ascontiguousarray(masks), "ident": ident, "jv": jv}


def host_s5_params(inp, l, p):
    gs = slice(32 * p, 32 * p + 32)

    def dup(a):
        return np.ascontiguousarray(np.concatenate([a, a], axis=0))
    def bm(a):
        sh = a.shape
        return np.ascontiguousarray(a.reshape(sh[0], 2, 32 // NGB, NGB, *sh[2:]).swapaxes(1, 2).reshape(sh))
    out = {}
    out["lre2"] = dup(inp["lam_re"][l][:, gs, :].transpose(2, 0, 1).reshape(64, 64))
    out["lim2"] = dup(inp["lam_im"][l][:, gs, :].transpose(2, 0, 1).reshape(64, 64))
    out["lst2"] = np.ascontiguousarray(np.broadcast_to(inp["log_step"][l][:, gs].reshape(1, 64), (128, 64)))
    out["bb1"] = dup(inp["ssm_b_re"][l][:, gs].transpose(2, 0, 1, 3).reshape(64, 64, 16))
    out["bb2"] = dup(inp["ssm_b_im"][l][:, gs].transpose(2, 0, 1, 3).reshape(64, 64, 16))
    out["cc1"] = dup(inp["ssm_c_re"][l][:, gs].transpose(3, 0, 1, 2).reshape(64, 64, 16))
    out["cc2"] = dup(inp["ssm_c_im"][l][:, gs].transpose(3, 0, 1, 2).reshape(64, 64, 16))
    for n in ("lre2", "lim2", "lst2", "bb1", "bb2", "cc1", "cc2"):
        out[n] = bm(out[n])
    dsk = inp["d_skip"][l].reshape(64, 16)[gs]
    out["dskp"] = np.ascontiguousarray(np.broadcast_to(dsk.T[None], (8, 16, 32)).reshape(128, 32))
    return out


def pos_table():
    quarter = D // 4
    omega = 1.0 / (10000.0 ** (np.arange(quarter, dtype=np.float32) / np.float32(quarter)))
    row = np.repeat(np.arange(SEQ // 64, dtype=np.float32), 64)
    col = np.tile(np.arange(64, dtype=np.float32), SEQ // 64)
    ar = row[:, None] * omega
    ac = col[:, None] * omega
    return np.concatenate([np.sin(ar), np.cos(ar), np.sin(ac), np.cos(ac)], axis=-1).astype(np.float32)


def core_bp(i):
    return i // 2, i % 2


def fm(a):
    t, f = a.shape
    return np.ascontiguousarray(a.reshape(t, f // 128, 128).transpose(2, 1, 0))


def unfm(a):
    p, c, t = a.shape
    return np.ascontiguousarray(a.transpose(2, 1, 0).reshape(t, c * 128))


def run(nc, in_maps):
    res = run_bass_kernel_spmd(nc, in_maps, core_ids=list(range(NCORES)))
    return res.results


def emit_ks(P, SC, U_all, ubufs, prm, ys_dst, ys_buf, pch, bg=None, pre=None):
    st = contextlib.ExitStack()
    P.phase()
    if pre is not None:
        pre()

    def SB(name, shape, dt):
        return P.sbuf(name, shape, dt, st)
    cb = Buf("consts")
    upk = SB("upk", [128, 32, NKK], BF16)
    q = 0
    for s_ in range(8):
        for ph in range(2):
            for (c0, nk, kk0) in ((0, 128, 32 + ph * 128), (128, 16, ph * 16)):
                src = U_all[ph][512 * pch:512 * pch + 512, s_ * 144 + c0: s_ * 144 + c0 + nk].rearrange("(g c) k -> c g k", c=16)
                P.dma(("sp", "act")[q % 2], upk[s_ * 16:(s_ + 1) * 16, :, kk0:kk0 + nk], src, reads=[ubufs[ph]], writes=[cb])
                q += 1
    names2 = ["lre2", "lim2", "lst2"]
    names3 = ["bb1", "bb2", "cc1", "cc2"]
    t2d, t3d = {}, {}
    for n in names2:
        t2d[n] = SB(n, [128, 64], F32)
        P.dma("sp", t2d[n][:], prm[n], writes=[cb])
    for n in names3:
        t3d[n] = SB(n, [128, 64, 16], F32)
        P.dma("act", t3d[n][:], prm[n], writes=[cb])
    dskp = SB("dskp", [128, 32], F32)
    P.dma("sp", dskp[:], prm["dskp"], writes=[cb])
    evec, pvec, masks, ident, jv = SC["evec"], SC["pvec"], SC["masks"], SC["ident"], SC["jv"]
    cb.w.update(SC["b"].w)
    offA, offB, sgnC, sgnB = pvec[:, 0:1], pvec[:, 1:2], pvec[:, 2:3], pvec[:, 3:4]

    NL = NPB * NKK
    t1 = SB("t1", [128, NPB, NKK], F32)
    t2 = SB("t2", [128, NPB, NKK], F32)
    T1 = t1[:].rearrange("p g k -> p (g k)")
    T2 = t2[:].rearrange("p g k -> p (g k)")
    Gs = SB("Gs", [128, NPB, NKK], F32)
    TI = Gs[:].rearrange("p g k -> p (g k)").bitcast(I32)
    sb_ = Buf("tabscratch")
    S1 = SB("S1", [128, 768], F32)
    S2 = SB("S2", [128, 768], F32)
    SI = SB("SI", [128, 768], I32)
    ssb = Buf("smallscratch")

    def finish_table(out, n, elr, shape_fn, small=False):
        A1, A2, AI, bb = (S1, S2, SI, ssb) if small else (T1, T2, TI, sb_)
        a1, a2, ai = A1[:, 0:n], A2[:, 0:n], AI[:, 0:n]
        cp(P, "dve", ai, a1, [bb], [bb])
        cp(P, "dve", a2, ai, [bb], [bb])
        tt(P, "dve", a1, a1, a2, ALU.subtract, [bb], [bb])
        if elr is None:
            act(P, out, shape_fn(a1), AF.Sin, [bb], out_w, scale=TWO_PI)
        else:
            act(P, a1, a1, AF.Sin, [bb], [bb], scale=TWO_PI)
            act(P, shape_fn(a2), elr, AF.Exp, [bb] + out_w, [bb])
            tt(P, "dve", out, shape_fn(a1), shape_fn(a2), ALU.mult, [bb], out_w)

    smc = {n: SB(n, [128, 64], F32) for n in ["dt", "lr", "th", "c1", "s1", "nr", "den", "cr", "ci", "u1", "u2", "p8", "d8"]}
    smi = SB("smi", [128, 64], I32)
    ANG = SB("ANG", [128, 3, 2, NPB, 8], F32)
    ELR = SB("ELR", [128, 3, 2, NPB, 8], F32)
    TAB = SB("TAB", [128, 3, 2, NPB, 8], F32)
    W = {n: SB(n, [128, 2, NPB, 8], F32) for n in ["WA", "WB", "wu1", "wu2"]}
    big = {n: SB(n, [128, NPB, 8, 16], F32) for n in ["tA", "tB", "BT", "BS", "PT", "Q1"]}
    Q1b = SB("Q1b", [128, NPB, 128], BF16)
    Q2b = SB("Q2b", [128, NPB, 128], BF16)
    Bmat = SB("Bmat", [128, NPB, 2, 128], BF16)
    Toep = SB("Toep", [128, NGB, 128], BF16)
    tpa = SB("tpa", [128, NGB, 128], F32)
    tpb = SB("tpb", [128, NGB, 128], F32)
    Ctab = SB("Ctab", [128, NPB, NKK], F32)
    Stab = SB("Stab", [128, NPB, NKK], F32)
    Dtab = SB("Dtab", [128, NPB, NKK], F32)
    ANG2 = SB("ANG2", [128, NPB, NKK], F32)
    GSs = ANG2
    gsb = Buf("Gs")
    hc = SB("hc", [128, NPB, NKK], BF16)
    hs = SB("hs", [128, NPB, NKK], BF16)
    yst = SB("yst", [128, NGB, NKK], BF16)
    pb_ = Buf("prep")
    tabb = Buf("TAB")
    wb_ = Buf("W")
    bigb = {n: Buf(n) for n in big}
    big["Q2"], bigb["Q2"] = big["BT"], bigb["BT"]
    qbb, bmb, toeb, tpab = Buf("Qb"), Buf("Bmat"), Buf("Toep"), Buf("tp")
    l2b, hcb, ystb = Buf("l2tab"), Buf("hc"), Buf("yst")
    t1b = t2b = sb_
    out_w = None

    def v3(t, g0):
        return t[:].rearrange("p (d g) -> p d g", d=2)[:, :, g0:g0 + NGB]

    def flat(t):
        return t[:].rearrange("p d g -> p (d g)")

    pcb, dtb = cb, Buf("Dtab")
    P.op("dve", lambda e: e.memset(Dtab[:, :, 0:1], 0.0), [], [dtb])
    for hz in (hc, hs):
        P.op("dve", lambda e, hz=hz: e.memset(hz[:, 0:NGB, 0:1], 0.0), [], [hcb])
        P.op("dve", lambda e, hz=hz: e.memset(hz[:, NGB:NPB, 31:32], 0.0), [], [hcb])
    lre, lim = t2d["lre2"][:], t2d["lim2"][:]
    M = {n: smc[n][:] for n in smc}
    act(P, M["dt"], t2d["lst2"][:], AF.Exp, [cb], [pb_])
    tt(P, "dve", M["lr"], lre, M["dt"], ALU.mult, [cb, pb_], [pb_])
    tt(P, "dve", M["th"], lim, M["dt"], ALU.mult, [cb, pb_], [pb_])
    for nm, off in (("c1", 0.25), ("s1", 0.0)):
        ts(P, "dve", S1[:, 0:64], M["th"], 1.0 / TWO_PI, off, ALU.mult, ALU.add, [pb_, ssb], [ssb])
        out_w = [pb_]
        finish_table(M[nm], 64, M["lr"], lambda a: a, small=True)
    ts(P, "dve", M["nr"], M["c1"], -1.0, None, ALU.add, None, [pb_], [pb_])
    tt(P, "dve", M["u1"], lre, lre, ALU.mult, [pb_, cb], [pb_])
    tt(P, "dve", M["den"], lim, lim, ALU.mult, [pb_, cb], [pb_])
    tt(P, "dve", M["den"], M["den"], M["u1"], ALU.add, [pb_], [pb_])
    P.op("dve", lambda e: e.reciprocal(out=M["den"], in_=M["den"]), [pb_], [pb_])
    tt(P, "dve", M["u1"], M["nr"], lre, ALU.mult, [pb_, cb], [pb_])
    tt(P, "dve", M["u2"], M["s1"], lim, ALU.mult, [pb_, cb], [pb_])
    tt(P, "dve", M["u1"], M["u1"], M["u2"], ALU.add, [pb_], [pb_])
    tt(P, "dve", M["cr"], M["u1"], M["den"], ALU.mult, [pb_], [pb_])
    tt(P, "dve", M["u1"], M["s1"], lre, ALU.mult, [pb_, cb], [pb_])
    tt(P, "dve", M["u2"], M["nr"], lim, ALU.mult, [pb_, cb], [pb_])
    tt(P, "dve", M["u1"], M["u1"], M["u2"], ALU.subtract, [pb_], [pb_])
    tt(P, "dve", M["ci"], M["u1"], M["den"], ALU.mult, [pb_], [pb_])
    ts(P, "dve", M["p8"], M["th"], 8.0 / TWO_PI, None, ALU.mult, None, [pb_], [pb_])
    cp(P, "dve", smi[:], M["p8"], [pb_], [pb_])
    cp(P, "dve", M["u2"], smi[:], [pb_], [pb_])
    tt(P, "dve", M["p8"], M["p8"], M["u2"], ALU.subtract, [pb_], [pb_])
    act(P, M["d8"], M["lr"], AF.Exp, [pb_], [pb_], scale=8.0)
    p8b = pb_
    for blk in range(32 // NGB):
        g0 = blk * NGB
        sm = {n: smc[n][:, blk * NPB:(blk + 1) * NPB].rearrange("p (d g) -> p d g", d=2) for n in smc}
        pc = {n: t3d[n][:, blk * NPB:(blk + 1) * NPB, :] for n in names3}

        def l2_part1():
            tt(P, "dve", ANG2[:], flat(sm["p8"]).unsqueeze(2).broadcast_to([128, NPB, NKK]),
               jv[:].unsqueeze(1).broadcast_to([128, NPB, NKK]), ALU.mult, [p8b, cb, l2b], [l2b])
            act(P, Dtab[:, :, 1:NKK], flat(sm["d8"]).unsqueeze(2).broadcast_to([128, NPB, NKK - 1]), AF.Copy,
                [p8b, dtb], [dtb])

        def l2_table(tab, off):
            nonlocal out_w
            a2f = ANG2[:].rearrange("p g k -> p (g k)")
            ts(P, "dve", TI[:, 0:NL], a2f, off, None, ALU.add, None, [l2b, sb_], [sb_])
            cp(P, "dve", T2[:, 0:NL], TI[:, 0:NL], [sb_], [sb_])
            stt(P, "dve", T1[:, 0:NL], a2f, off, T2[:, 0:NL], ALU.add, ALU.subtract, [l2b, sb_], [sb_])
            act(P, tab[:].rearrange("p g k -> p (g k)"), T1[:, 0:NL], AF.Sin, [sb_], [l2b], scale=TWO_PI)

        l2_part1()
        for st_ in range(3):
            ev_b = evec[:, st_].unsqueeze(2).broadcast_to([128, 2, NGB, 8])
            for dst, src in ((ANG, sm["th"]), (ELR, sm["lr"])):
                tt(P, "dve", dst[:, st_, 0].rearrange("p (d g) s -> p d g s", d=2),
                   src[:].unsqueeze(3).broadcast_to([128, 2, NGB, 8]), ev_b, ALU.mult, [pb_, cb, tabb], [tabb])
        nA = 3 * 2 * NPB * 8
        nH = 3 * NPB * 8
        t1v = S1[:, 0:nA].rearrange("p (a b n) -> p a b n", a=3, b=2)
        for ab, off in ((0, offA), (1, offB)):
            ts(P, "dve", t1v[:, :, ab, :], ANG[:, :, 0].rearrange("p a g s -> p a (g s)"), 1.0 / TWO_PI, off,
               ALU.mult, ALU.add, [tabb, ssb, cb], [ssb])
        a1, a2, ai = S1[:, 0:nA], S2[:, 0:nA], SI[:, 0:nA]
        cp(P, "dve", ai, a1, [ssb], [ssb])
        cp(P, "dve", a2, ai, [ssb], [ssb])
        tt(P, "dve", a1, a1, a2, ALU.subtract, [ssb], [ssb])
        act(P, a1, a1, AF.Sin, [ssb], [ssb], scale=TWO_PI)
        act(P, S2[:, 0:nH].rearrange("p (a n) -> p a n", a=3), ELR[:, :, 0].rearrange("p a g s -> p a (g s)"), AF.Exp,
            [ssb, tabb], [ssb])
        l2_table(Ctab, 0.25)
        tt(P, "dve", TAB[:].rearrange("p a b g s -> p a b (g s)"), t1v,
           S2[:, 0:nH].rearrange("p (a n) -> p a n", a=3).unsqueeze(2).broadcast_to([128, 3, 2, NPB * 8]), ALU.mult,
           [ssb], [tabb])
        PAs, PBs = TAB[:, 1:3, 0], TAB[:, 1:3, 1]
        cr_b = flat(sm["cr"]).unsqueeze(1).unsqueeze(3).broadcast_to([128, 2, NPB, 8])
        ci_b = flat(sm["ci"]).unsqueeze(1).unsqueeze(3).broadcast_to([128, 2, NPB, 8])
        tt(P, "dve", W["wu1"][:], PAs, cr_b, ALU.mult, [tabb, pb_], [wb_])
        tt(P, "dve", W["wu2"][:], PBs, ci_b, ALU.mult, [tabb, pb_], [wb_])
        stt(P, "dve", W["WA"][:], W["wu2"][:], sgnB, W["wu1"][:], ALU.mult, ALU.add, [wb_, cb], [wb_])
        tt(P, "dve", W["wu1"][:], PBs, cr_b, ALU.mult, [tabb, pb_, wb_], [wb_])
        tt(P, "dve", W["wu2"][:], PAs, ci_b, ALU.mult, [tabb, pb_, wb_], [wb_])
        stt(P, "dve", W["WB"][:], W["wu2"][:], sgnC, W["wu1"][:], ALU.mult, ALU.add, [wb_, cb], [wb_])

        def bc_c(t):
            return t[:].unsqueeze(2).broadcast_to([128, NPB, 8, 16])

        def bc_s(ap):
            return ap.unsqueeze(3).broadcast_to([128, NPB, 8, 16])

        def combo(out_name, a1, a2, b1, b2, sgn, op1, first_scaled):
            tt(P, "dve", big["tA"][:], a1, a2, ALU.mult, [pb_, pcb, wb_, tabb, bigb["tA"]], [bigb["tA"]])
            tt(P, "dve", big["tB"][:], b1, b2, ALU.mult, [pb_, pcb, wb_, tabb, bigb["tB"]], [bigb["tB"]])
            X, Y = (big["tA"], big["tB"]) if first_scaled == "A" else (big["tB"], big["tA"])
            stt(P, "dve", big[out_name][:], X[:], sgn, Y[:], ALU.mult, op1, [bigb["tA"], bigb["tB"], cb], [bigb[out_name]])

        WA_B, WB_B, WA_P, WB_P = W["WA"][:, 0], W["WB"][:, 0], W["WA"][:, 1], W["WB"][:, 1]
        combo("BT", bc_c(pc["bb1"]), bc_s(WA_B), bc_c(pc["bb2"]), bc_s(WB_B), sgnB, ALU.add, "B")
        combo("BS", bc_c(pc["bb1"]), bc_s(WB_B), bc_c(pc["bb2"]), bc_s(WA_B), sgnC, ALU.add, "A")
        for d2_ in range(NPB // 2):
            ps, pbk = P.psum_next()
            for q in range(2):
                dg = d2_ * 2 + q
                for kind, nm in ((0, "BT"), (1, "BS")):
                    o = ps[:, (q * 2 + kind) * 128:(q * 2 + kind + 1) * 128]
                    src = big[nm][:, dg].rearrange("p s c -> p (s c)")
                    P.op("pe", lambda e, o=o, src=src: e.transpose(o, src, ident[:]), [bigb[nm], cb], [pbk])
            cp(P, "dve", Bmat[:, d2_ * 2:d2_ * 2 + 2].rearrange("p g k n -> p (g k n)"), ps[:, :], [pbk], [bmb])
        combo("PT", bc_c(pc["bb1"]), bc_s(WA_P), bc_c(pc["bb2"]), bc_s(WB_P), sgnB, ALU.add, "B")
        PA_Q, PB_Q = TAB[:, 0, 0], TAB[:, 0, 1]
        combo("Q1", bc_c(pc["cc1"]), bc_s(PA_Q), bc_c(pc["cc2"]), bc_s(PB_Q), sgnC, ALU.subtract, "A")
        combo("Q2", bc_c(pc["cc1"]), bc_s(PB_Q), bc_c(pc["cc2"]), bc_s(PA_Q), sgnB, ALU.subtract, "B")
        act(P, Q1b[:], big["Q1"][:].rearrange("p g s c -> p g (s c)"), AF.Copy, [bigb["Q1"]], [qbb])
        act(P, Q2b[:], big["Q2"][:].rearrange("p g s c -> p g (s c)"), AF.Copy, [bigb["Q2"]], [qbb])
        l2_table(Stab, 0.0)
        pst = []
        for d in range(2):
            ps, pbk = P.psum_next()
            for g4 in range(NGB):
                dg = d * NGB + g4
                mm(P, ps[:, g4 * 128:(g4 + 1) * 128], big["PT"][:, dg].rearrange("p s c -> p (s c)"),
                   big["Q1"][:, dg].rearrange("p s c -> p (s c)"), True, True, [bigb["PT"], bigb["Q1"]], [pbk])
            pst.append((ps, pbk))
        for d, (ps, pbk) in enumerate(pst):
            tt(P, "dve", (tpa if d == 0 else tpb)[:], ps[:, 0:NGB * 128].rearrange("p (g n) -> p g n", g=NGB),
               masks[:, d, :].unsqueeze(1).broadcast_to([128, NGB, 128]), ALU.mult, [pbk, cb, tpab], [tpab])
        tt(P, "dve", tpa[:], tpa[:], tpb[:], ALU.add, [tpab], [tpab])
        tt(P, "dve", tpb[:], ident[:].unsqueeze(1).broadcast_to([128, NGB, 128]),
           dskp[:, g0:g0 + NGB].unsqueeze(2).broadcast_to([128, NGB, 128]), ALU.mult, [tpab, cb], [tpab])
        tt(P, "dve", Toep[:], tpa[:], tpb[:], ALU.add, [tpab], [toeb])
        for dg in range(NPB):
            d, g = dg // NGB, g0 + dg % NGB
            psG, pbG = P.psum_next()
            psS, pbS = P.psum_next()
            mm(P, psG[:, 0:NKK], Bmat[:, dg, 0, :], upk[:, g, :], True, True, [bmb, cb], [pbG])
            mm(P, psS[:, 0:NKK], Bmat[:, dg, 1, :], upk[:, g, :], True, True, [bmb, cb], [pbS])
            for (tdst, tb_, tab, ps, pbk) in ((t1, t1b, Ctab, psG, pbG), (t2, t2b, Stab, psS, pbS)):
                if d == 0:
                    tt(P, "dve", tdst[:, dg, :], tab[:, dg, :], ps[:, 0:NKK], ALU.mult, [l2b, pbk, tb_], [tb_])
                else:
                    tt(P, "dve", tdst[:, dg, 0:32], tab[:, dg, 0:32], ps[:, 31::-1], ALU.mult, [l2b, pbk, tb_], [tb_])
                    tt(P, "dve", tdst[:, dg, 32:NKK], tab[:, dg, 32:NKK], ps[:, NKK - 1:31:-1], ALU.mult,
                       [l2b, pbk, tb_], [tb_])
        tt(P, "dve", t1[:], t1[:], t2[:], ALU.add, [t1b, t2b], [t1b])
        P.op("dve", lambda e: e.tensor_tensor_scan(out=t2[:].rearrange("p g k -> p (g k)"),
                                                   data0=Dtab[:].rearrange("p g k -> p (g k)"),
                                                   data1=t1[:].rearrange("p g k -> p (g k)"),
                                                   initial=0.0, op0=ALU.mult, op1=ALU.add), [t1b, t2b, l2b, dtb], [t2b])
        for hdst, tab in ((hc, Ctab), (hs, Stab)):
            tt(P, "dve", hdst[:, 0:NGB, 1:NKK], tab[:, 0:NGB, 0:NKK - 1], t2[:, 0:NGB, 0:NKK - 1], ALU.mult,
               [l2b, t2b, hcb], [hcb])
            tt(P, "dve", hdst[:, NGB:NPB, 0:31], tab[:, NGB:NPB, 30::-1], t2[:, NGB:NPB, 30::-1], ALU.mult,
               [l2b, t2b, hcb], [hcb])
            tt(P, "dve", hdst[:, NGB:NPB, 32:NKK], tab[:, NGB:NPB, NKK - 2:30:-1], t2[:, NGB:NPB, NKK - 2:30:-1], ALU.mult,
               [l2b, t2b, hcb], [hcb])
        for g4 in range(NGB):
            g = g0 + g4
            ps, pbk = P.psum_next()
            mm(P, ps[:, 0:NKK], Toep[:, g4, :], upk[:, g, :], True, False, [toeb, cb], [pbk])
            for d in range(2):
                dg = d * NGB + g4
                mm(P, ps[:, 0:NKK], Q1b[:, dg, :], hc[:, dg, :], False, False, [qbb, hcb], [pbk])
                mm(P, ps[:, 0:NKK], Q2b[:, dg, :], hs[:, dg, :], False, d == 1, [qbb, hcb], [pbk])
            act(P, yst[:, g4, :], ps[:, 0:NKK], AF.Copy, [pbk], [ystb])
        if bg is not None:
            bg.pump(6)
        for s_ in range(8):
            dst = ys_dst.rows(s_ * 512 + g0 * 16, NGB * 16).rearrange("(g c) k -> c g k", c=16)
            P.dma("sp" if s_ % 2 else "act", dst, yst[s_ * 16:(s_ + 1) * 16, :, :], reads=[ystb], writes=[ys_buf])
    st.close()


class Pump:
    def __init__(self, jobs, depth):
        self.jobs, self.depth, self.i, self.h = jobs, depth, 0, {}

    def pump(self, n):
        for _ in range(n):
            if self.i >= len(self.jobs):
                return
            if self.i == 0:
                for k in range(min(self.depth, len(self.jobs))):
                    self.h[k] = self.jobs[k][0]()
            i = self.i
            self.jobs[i][1](self.h.pop(i))
            if i + self.depth < len(self.jobs):
                self.h[i + self.depth] = self.jobs[i + self.depth][0]()
            self.i += 1

    def drain(self):
        self.pump(len(self.jobs))


def k0_setup(P, condT, bmodT):
    ct = P.sbuf("ct", [128, 16, 2], F32)
    cbt = P.sbuf("cbt", [128, 16, 2], BF16)
    bt = P.sbuf("bt", [128, 2, 96], F32)
    bct, bcb, bbt = Buf(), Buf(), Buf()
    P.dma("sp", ct[:], condT, writes=[bct])
    P.dma("sp", bt[:], bmodT, writes=[bbt])
    act(P, cbt[:], ct[:], AF.Silu, [bct], [bcb])
    return {"cbt": cbt, "bcb": bcb, "bt": bt, "bbt": bbt}


def k0_jobs(P, ws, K, wmod_l, mdl, l):
    cbt, bcb, bt, bbt = K["cbt"], K["bcb"], K["bt"], K["bbt"]
    jobs = []
    for s_ in range(48):
        def load(s_=s_):
            return ws.load(wmod_l, 0, 16, s_ * 256, 256)

        def comp(h, s_=s_):
            wt, wb = h
            for jj in range(2):
                q = s_ * 2 + jj
                ps, pb = P.psum_next()
                for kc in range(16):
                    mm(P, ps[:, 0:2], wt[:, kc, jj * 128:(jj + 1) * 128], cbt[:, kc, :], kc == 0, kc == 15, [wb, bcb], [pb])
                act(P, mdl["t"][:, q, :], ps[:, 0:2], AF.Identity, [pb, bbt], [mdl["b"]], bias=bt[:, l, q:q + 1], scale=1.0)
        jobs.append((load, comp))
    return jobs


def k0_finish(P, mdl):
    mv = mdl["t"][:].rearrange("p (k c) w -> p k c w", k=6)
    ts(P, "dve", mdl["one"][:, 0], mv[:, 1], 1.0, None, ALU.add, None, [mdl["b"]], [mdl["b"]])
    ts(P, "dve", mdl["one"][:, 1], mv[:, 4], 1.0, None, ALU.add, None, [mdl["b"]], [mdl["b"]])


def k0_all(P, ws, nc, condT3, wmod, bmodT, oh_d, md, zcol):
    st = contextlib.ExitStack()
    NQ, NW = 24, 3
    NL = DEPTH * NQ * NW
    ct = P.sbuf("ct", [128, 16, NW], F32, st)
    cbt = P.sbuf("cbt", [128, 16, NW], BF16, st)
    bt = P.sbuf("bt", [128, DEPTH, NQ], F32, st)
    oh = P.sbuf("oh", [128, 2], F32, st)
    mloc = P.sbuf("mloc", [128, DEPTH, NQ, NW], F32, st)
    mall = P.sbuf("mall", [128, 4, NL], F32, st)
    bct, bcb, bbt, mlb, dlb, dgb, dgb2, mab = (Buf() for _ in range(8))
    P.dma("sp", ct[:], condT3, writes=[bct])
    P.dma("sp", bt[:], bmodT, writes=[bbt])
    P.dma("sp", oh[:], oh_d, writes=[bbt])
    act(P, cbt[:], ct[:], AF.Silu, [bct], [bcb])
    jobs = []
    for l in range(DEPTH):
        for s_ in range(NQ // 2):
            def load(s_=s_, l=l):
                return ws.load(wmod[l], 0, 16, s_ * 256, 256)

            def comp(h, s_=s_, l=l):
                wt, wb = h
                for jj in range(2):
                    q = s_ * 2 + jj
                    ps, pb = P.psum_next()
                    for kc in range(16):
                        mm(P, ps[:, 0:NW], wt[:, kc, jj * 128:(jj + 1) * 128], cbt[:, kc, :], kc == 0, kc == 15, [wb, bcb], [pb])
                    act(P, mloc[:, l, q, :], ps[:, 0:NW], AF.Identity, [pb, bbt], [mlb], bias=bt[:, l, q:q + 1], scale=1.0)
            jobs.append((load, comp))
    pipeline(jobs, 3)
    md_loc = nc.dram_tensor("md_loc", [128, NL], F32).ap()
    md_g1 = nc.dram_tensor("md_g1", [256, NL], F32).ap()
    md_g2 = nc.dram_tensor("md_g2", [512, NL], F32).ap()
    P.dma("sp", md_loc, mloc[:].rearrange("p l q w -> p (l q w)"), reads=[mlb], writes=[dlb])
    P.collective("AllGather", [md_loc], [md_g1], [[0, 1], [2, 3], [4, 5], [6, 7]], reads=[dlb], writes=[dgb])
    P.collective("AllGather", [md_g1], [md_g2], [[0, 4], [1, 5], [2, 6], [3, 7]], reads=[dgb], writes=[dgb2])
    P.dma("sp", mall[:], md_g2.rearrange("(k p) n -> p k n", p=128), reads=[dgb2], writes=[mab])
    for l in range(DEPTH):
        src = mall[:, :, l * NQ * NW:(l + 1) * NQ * NW].rearrange("p k (q w) -> p k q w", w=NW)
        dst = md[l]["t"][:].rearrange("p (k q) w -> p k q w", k=4)
        mb_ = md[l]["b"]
        act(P, dst[:, :, :, 0], src[:, :, :, 0], AF.Identity, [mab, bbt], [mb_], bias=zcol, scale=oh[:, 0:1])
        stt(P, "dve", dst[:, :, :, 0], src[:, :, :, 1], oh[:, 1:2], dst[:, :, :, 0], ALU.mult, ALU.add, [mab, bbt, mb_], [mb_])
        cp(P, "dve", dst[:, :, :, 1], src[:, :, :, 2], [mab], [mb_])
        k0_finish(P, md[l])
    P.phase()
    st.close()


def emit_ka(P, ws, C, x_src, xsrc_buf, pos_src, x0_dst, x0_buf, mdl, w_in, U_dst, u_buf, h_dst, h_buf, T=1152):
    st = contextlib.ExitStack()
    P.phase()
    tiles = token_tiles(T)
    nt = len(tiles)
    md = mdl["t"][:].rearrange("p (k c) w -> p k c w", k=6)
    onep = mdl["one"]
    x = P.sbuf("x", [128, 16, T], F32, st)
    xb = grid(16, nt, "x")
    h = P.sbuf("h", [128, 16, T], BF16, st)
    hb = grid(16, nt, "h")
    for c in range(16):
        P.dma("sp", x[:, c, :], x_src[:, c, :], reads=[xsrc_buf], writes=xb[c])
    if pos_src is not None:
        pos = P.sbuf("pos", [128, 4, 1024], F32, st)
        pbufs = [Buf() for _ in range(4)]
        for c in range(16):
            pbf = pbufs[c % 4]
            P.dma("act", pos[:, c % 4, :], pos_src[:, c, :], writes=[pbf])
            tt(P, "dve", x[:, c, 0:1024], x[:, c, 0:1024], pos[:, c % 4, :], ALU.add, [pbf] + xb[c], xb[c])
            P.dma("sp", x0_dst[:, c, :], x[:, c, :], reads=xb[c], writes=[x0_buf])
    S = ln_scratch(P, st)
    emit_ln(P, S, C, x, xb, h, hb, tiles, lambda c, mi: onep[:, 0, c, mi:mi + 1], lambda c, mi: md[:, 0, c, mi:mi + 1],
            extra=[mdl["b"]])
    for c in range(16):
        P.dma("act", h_dst[:, c, :], h[:, c, :], reads=hb[c], writes=[h_buf])
    stg = [(P.sbuf("stg", [128, T], BF16, st), Buf()) for _ in range(2)]
    jobs = []
    for s_ in range(8):
        def load(s_=s_):
            return ws.load(w_in, 0, 16, s_ * 256, 256)

        def comp(hd, s_=s_):
            wt, wb = hd
            for jj in range(2):
                j = s_ * 2 + jj
                stt_, stb = stg[j % 2]
                for ti, (t0, tn, _) in enumerate(tiles):
                    ps, pb = P.psum_next()
                    for kc in range(16):
                        mm(P, ps[:, 0:tn], wt[:, kc, jj * 128:(jj + 1) * 128], h[:, kc, t0:t0 + tn], kc == 0, kc == 15,
                           [wb, hb[kc][ti]], [pb])
                    if j < 8:
                        o = stt_[:].rearrange("p (s k) -> p s k", s=8)[:, :, t0 // 8:(t0 + tn) // 8]
                        act(P, o, ps[:, 0:tn].rearrange("p (k s) -> p s k", s=8), AF.Copy, [pb], [stb])
                    else:
                        act(P, stt_[:, t0:t0 + tn], ps[:, 0:tn], AF.Copy, [pb], [stb])
                P.dma("sp", U_dst.rows(j * 128, 128), stt_[:], reads=[stb], writes=[u_buf])
        jobs.append((load, comp))
    pipeline(jobs, 3)
    st.close()


def emit_kf(P, FC, U_all, ubufs, yf_dst, yf_buf, pch, need_ctx):
    st = contextlib.ExitStack()
    P.phase()
    uf = P.sbuf("uf", [128, 4, NTOK], BF16, st)
    ufb = Buf()
    for ph in range(2):
        src = U_all[ph][(8 + 4 * pch) * 128:(12 + 4 * pch) * 128, :].rearrange("(c p) t -> p c t", p=128)
        P.dma("sp", uf[:, :, CTX + ph * 1024: CTX + (ph + 1) * 1024], src[:, :, 0:1024], reads=[ubufs[ph]], writes=[ufb])
        P.dma("act", uf[:, :, ph * 128:(ph + 1) * 128], src[:, :, 1024:1152], reads=[ubufs[ph]], writes=[ufb])
    dftc, dpc, cb = FC["dftc"], FC["dpc"], FC["b"]
    dpl_d = FC["dpl_d"]
    V = P.sbuf("V", [128, 18, 2, 512], BF16, st)
    vb = [[Buf() for _ in range(2)] for _ in range(18)]
    yo = P.sbuf("yo", [128, 4, NTOK], BF16, st)
    yob = [Buf() for _ in range(4)]
    k = 0
    for tt_ in range(18):
        for grp in range(2):
            ps, pb = P.psum_next()
            for kc in range(2):
                mm(P, ps[:, :], uf[:, grp * 2 + kc, tt_ * 128:(tt_ + 1) * 128], dftc[:, kc, :], kc == 0, kc == 1, [ufb, cb], [pb])
            k += 1
            if k % 2:
                act(P, V[:, tt_, grp, :], ps[:, :], AF.Copy, [pb], [vb[tt_][grp]])
            else:
                cp(P, "dve", V[:, tt_, grp, :], ps[:, :], [pb], [vb[tt_][grp]])
    if need_ctx:
        for grp in range(2):
            for half in range(2):
                ps, pb = P.psum_next()
                n = 0
                for tt_ in range(2):
                    for cs in range(2):
                        mm(P, ps[:, 0:256], V[:, tt_, grp, cs * 256 + half * 128: cs * 256 + half * 128 + 128],
                           dpc[:, tt_, cs, :], n == 0, n == 3, [vb[tt_][grp], cb], [pb])
                        n += 1
                act(P, yo[:, grp * 2 + half, 0:256], ps[:, 0:256], AF.Copy, [pb], [yob[grp * 2 + half]])
    else:
        for c in range(4):
            P.op("dve", lambda e, c=c: e.memset(yo[:, c, 0:256], 0.0), [], [yob[c]])
    dts = [(P.sbuf("dpl", [128, 16, 2, 512], BF16, st), Buf()) for _ in range(2)]
    jobs = []
    for kb in range(4):
        def load(kb=kb):
            t, b = dts[kb % 2]
            for cs in range(2):
                src = dpl_d[:, cs, kb * 512:(kb + 1) * 512].rearrange("(tt p) k -> p tt k", p=128)
                P.dma("sp", t[:, :, cs, :], src, writes=[b])
            return t, b

        def comp(hd, kb=kb):
            t, b = hd
            for grp in range(2):
                for half in range(2):
                    ps, pb = P.psum_next()
                    n = 0
                    for tt_ in range(16):
                        for cs in range(2):
                            mm(P, ps[:, :], V[:, 2 + tt_, grp, cs * 256 + half * 128: cs * 256 + half * 128 + 128],
                               t[:, tt_, cs, :], n == 0, n == 31, [vb[2 + tt_][grp], b], [pb])
                            n += 1
                    o = yo[:, grp * 2 + half, 256 + kb * 512: 256 + (kb + 1) * 512]
                    if (grp + half) % 2:
                        act(P, o, ps[:, :], AF.Copy, [pb], [yob[grp * 2 + half]])
                    else:
                        cp(P, "dve", o, ps[:, :], [pb], [yob[grp * 2 + half]])
        jobs.append((load, comp))
    pipeline(jobs, 2)
    for c in range(4):
        P.dma("sp", yf_dst.rows(c * 128, 128), yo[:, c, :], reads=[yob[c]], writes=[yf_buf])
    st.close()


def emit_kc(P, ws, C, h_src, h_buf, x_src, xsrc_buf, mdl, lnp, lnpb, Ys_all, ysbufs, Yf_all, yfbufs, W, out_dst, out_buf, ph, T, blend=None):
    stp = contextlib.ExitStack()
    P.phase()
    tiles = token_tiles(T)
    nt = len(tiles)
    KL = T // 8
    md = mdl["t"][:].rearrange("p (k c) w -> p k c w", k=6)
    onep = mdl["one"]
    mdb = mdl["b"]
    h = P.sbuf("h", [128, 16, T], BF16, stp)
    hb = grid(16, nt, "h")
    mg = P.sbuf("mg", [128, 16, T], BF16, stp)
    mgb = grid(16, nt, "mg")
    S = ln_scratch(P, stp)
    sig = [(P.sbuf("sig", [128, 512], F32, stp), Buf()) for _ in range(2)]
    sigk = [0]

    def nsig():
        sigk[0] += 1
        return sig[sigk[0] % 2]

    bl = blend

    def load_h():
        for c in range(16):
            P.dma("sp" if c % 2 else "act", h[:, c, :], h_src[:, c, 0:T], reads=[h_buf], writes=hb[c])

    st2 = contextlib.ExitStack()
    sgl = P.sbuf("bufA", [128, 8, T], BF16, st2)
    g = P.sbuf("bufB", [128, 8, T], BF16, st2)
    yf = P.sbuf("bufC", [128, 8, T], BF16, st2)
    sgt = P.sbuf("sgt", [128, 2, T], BF16, st2)
    t1 = P.sbuf("t1", [128, 2, T], F32, st2)
    ab = [Buf("ysd") for _ in range(8)]
    gb = grid(8, nt, "g")
    sglb = grid(8, nt, "sgl")
    yfb = [Buf("yf") for _ in range(8)]
    sgtb = grid(2, nt, "sgt")
    t1b = grid(2, nt, "t1")
    ysd = sgl[:].rearrange("p j (s k) -> p j s k", s=8)
    ysdB = g[:].rearrange("p j (s k) -> p j s k", s=8)
    for pp in range(2):
        gw = [b_ for c in range(4 * pp, 4 * pp + 4) for b_ in gb[c]]
        for s_ in range(8):
            src = Ys_all.rows(s_ * 512, 512, slot=pp).rearrange("(j p) k -> p j k", p=128)
            P.dma("sp", ysd[:, 4 * pp:4 * pp + 4, s_, 0:128], src[:, :, 32:160], reads=[ysbufs[pp]], writes=ab[4 * pp:4 * pp + 4])
            P.dma("act", ysdB[:, 4 * pp:4 * pp + 4, s_, 0:128], src[:, :, 160:288], reads=[ysbufs[pp]], writes=gw)
            if T > 1024:
                P.dma("sp", ysd[:, 4 * pp:4 * pp + 4, s_, 128:144], src[:, :, 0:16], reads=[ysbufs[pp]], writes=ab[4 * pp:4 * pp + 4])
                P.dma("act", ysdB[:, 4 * pp:4 * pp + 4, s_, 128:144], src[:, :, 16:32], reads=[ysbufs[pp]], writes=gw)
    for pp in range(2):
        for c in range(4):
            cc_ = 4 * pp + c
            sf = Yf_all.rows(c * 128, 128, slot=pp)
            P.dma("sp", yf[:, cc_, 0:1024], sf[:, CTX:CTX + 1024], reads=[yfbufs[pp]], writes=[yfb[cc_]])
            P.dma("act", mg[:, cc_, 0:1024], sf[:, CTX + 1024:CTX + 2048], reads=[yfbufs[pp]], writes=mgb[cc_])
            if T > 1024:
                P.dma("sp", yf[:, cc_, 1024:1152], sf[:, 0:128], reads=[yfbufs[pp]], writes=[yfb[cc_]])
                P.dma("act", mg[:, cc_, 1024:1152], sf[:, 128:256], reads=[yfbufs[pp]], writes=mgb[cc_])
    load_h()
    for c in range(8):
        act(P, g[:, c, :], g[:, c, :], AF.Identity, gb[c] + [bl["b"]], gb[c], bias=bl["z"], scale=bl["m1"])
        stt(P, "dve", sgl[:, c, :], sgl[:, c, :], bl["m0"], g[:, c, :], ALU.mult, ALU.add, gb[c] + [ab[c], bl["b"]], [ab[c]])
    for c in range(8):
        act(P, mg[:, c, 0:T], mg[:, c, 0:T], AF.Identity, mgb[c] + [bl["b"]], mgb[c], bias=bl["z"], scale=bl["m1"])
        stt(P, "dve", yf[:, c, :], yf[:, c, :], bl["m0"], mg[:, c, 0:T], ALU.mult, ALU.add, mgb[c] + [yfb[c], bl["b"]], [yfb[c]])
    for c in range(8):
        for ti, (t0, tn, _) in enumerate(tiles):
            act(P, g[:, c, t0:t0 + tn].rearrange("p (k s) -> p k s", s=8),
                ysd[:, c, :, t0 // 8:(t0 + tn) // 8].rearrange("p s k -> p k s"), AF.Gelu, [ab[c]], [gb[c][ti]])
    for c in range(8):
        for ti in range(nt):
            sglb[c][ti].r.update({k: v for b_ in ab for k, v in b_.r.items()})
    jobs = []
    for s_ in range(4):
        def load(s_=s_):
            return ws.load(W["w_glu"], 0, 8, s_ * 256, 256)

        def comp(hd, s_=s_):
            wt, wb = hd
            for jj in range(2):
                j = s_ * 2 + jj
                for ti, (t0, tn, _) in enumerate(tiles):
                    ps, pb = P.psum_next()
                    for kc in range(8):
                        mm(P, ps[:, 0:tn], wt[:, kc, jj * 128:(jj + 1) * 128], g[:, kc, t0:t0 + tn], kc == 0, kc == 7,
                           [wb, gb[kc][ti]], [pb])
                    sg, sgb = nsig()
                    act(P, sg[:, 0:tn], ps[:, 0:tn], AF.Sigmoid, [pb], [sgb])
                    tt(P, "dve", sgl[:, j, t0:t0 + tn], g[:, j, t0:t0 + tn], sg[:, 0:tn], ALU.mult,
                       [gb[j][ti], sgb], [sglb[j][ti]])
        jobs.append((load, comp))
    for s_ in range(8):
        c0 = s_ * 256

        def mk(kind, s_=s_, c0=c0):
            def load():
                if kind == "gs":
                    return ws.load(W["w_in"], 0, 16, 2048 + c0, 256)
                if kind == "ps":
                    return ws.load(W["w_ps"], 0, 8, c0, 256)
                if kind == "gf":
                    return ws.load(W["w_in"], 0, 16, 4096 + c0, 256)
                return ws.load(W["w_pf"], 0, 8, c0, 256)

            def comp(hd):
                wt, wb = hd
                nk = 16 if kind in ("gs", "gf") else 8
                for jj in range(2):
                    j = s_ * 2 + jj
                    for ti, (t0, tn, _) in enumerate(tiles):
                        ps, pb = P.psum_next()
                        for kc in range(nk):
                            if kind in ("gs", "gf"):
                                rhs, rb = h[:, kc, t0:t0 + tn], hb[kc][ti]
                            elif kind == "ps":
                                rhs, rb = sgl[:, kc, t0:t0 + tn], sglb[kc][ti]
                            else:
                                rhs, rb = yf[:, kc, t0:t0 + tn], yfb[kc]
                            mm(P, ps[:, 0:tn], wt[:, kc, jj * 128:(jj + 1) * 128], rhs, kc == 0, kc == nk - 1, [wb, rb], [pb])
                        if kind in ("gs", "gf"):
                            act(P, sgt[:, jj, t0:t0 + tn], ps[:, 0:tn], AF.Sigmoid, [pb], [sgtb[jj][ti]])
                        elif kind == "ps":
                            tt(P, "dve", t1[:, jj, t0:t0 + tn], sgt[:, jj, t0:t0 + tn], ps[:, 0:tn], ALU.mult,
                               [pb, sgtb[jj][ti]], [t1b[jj][ti]])
                        else:
                            sg, sgb = nsig()
                            tt(P, "dve", sg[:, 0:tn], sgt[:, jj, t0:t0 + tn], ps[:, 0:tn], ALU.mult, [pb, sgtb[jj][ti]], [sgb])
                            tt(P, "dve", mg[:, j, t0:t0 + tn], sg[:, 0:tn], t1[:, jj, t0:t0 + tn], ALU.add,
                               [sgb, t1b[jj][ti]], [mgb[j][ti]])
            return (load, comp)
        for kind in ("gs", "ps", "gf", "pf"):
            jobs.append(mk(kind))
    pipeline(jobs, 3)
    st2.close()
    P.phase()

    x = P.sbuf("x2", [128, 16, T], F32, stp)
    xb = grid(16, nt, "x2")
    for c in range(16):
        P.dma("sp", x[:, c, :], x_src[:, c, 0:T], reads=[xsrc_buf], writes=xb[c])
        for ti, (t0, tn, _) in enumerate(tiles):
            act(P, x[:, c, t0:t0 + tn], x[:, c, t0:t0 + tn], AF.Copy, [xb[c][ti]], [xb[c][ti]], scale=ALPHA)
    jobs = []
    for s_ in range(8):
        def load(s_=s_):
            return ws.load(W["w_o"], 0, 16, s_ * 256, 256)

        def comp(hd, s_=s_):
            wt, wb = hd
            for jj in range(2):
                j = s_ * 2 + jj
                for ti, (t0, tn, segs) in enumerate(tiles):
                    ps, pb = P.psum_next()
                    for kc in range(16):
                        mm(P, ps[:, 0:tn], wt[:, kc, jj * 128:(jj + 1) * 128], mg[:, kc, t0:t0 + tn], kc == 0, kc == 15,
                           [wb, mgb[kc][ti]], [pb])
                    for (s0, sn, mi) in segs:
                        stt(P, "dve", x[:, j, s0:s0 + sn], ps[:, s0 - t0:s0 - t0 + sn], md[:, 2, j, mi:mi + 1],
                            x[:, j, s0:s0 + sn], ALU.mult, ALU.add, [pb, xb[j][ti], mdb], [xb[j][ti]])
        jobs.append((load, comp))
    pipeline(jobs, 3)
    emit_ln(P, S, C, x, xb, x, xb, tiles, lambda c, mi: lnp[:, 0, c:c + 1], lambda c, mi: lnp[:, 1, c:c + 1], extra=[lnpb])
    emit_ln(P, S, C, x, xb, h, hb, tiles, lambda c, mi: onep[:, 1, c, mi:mi + 1], lambda c, mi: md[:, 3, c, mi:mi + 1],
            extra=[mdb])
    for c in range(16):
        for ti, (t0, tn, _) in enumerate(tiles):
            act(P, x[:, c, t0:t0 + tn], x[:, c, t0:t0 + tn], AF.Copy, [xb[c][ti]], [xb[c][ti]], scale=ALPHA)
    jobs = []
    for hblk in range(4):
        for s_ in range(8):
            def load(s_=s_, hblk=hblk):
                return ws.load(W["w_up"], 0, 16, hblk * 2048 + s_ * 256, 256)

            def comp(hd, s_=s_):
                wt, wb = hd
                for jj in range(2):
                    j = s_ * 2 + jj
                    for ti, (t0, tn, _) in enumerate(tiles):
                        ps, pb = P.psum_next()
                        for kc in range(16):
                            mm(P, ps[:, 0:tn], wt[:, kc, jj * 128:(jj + 1) * 128], h[:, kc, t0:t0 + tn], kc == 0, kc == 15,
                               [wb, hb[kc][ti]], [pb])
                        sg, sgb = nsig()
                        act(P, sg[:, 0:tn], ps[:, 0:tn], AF.Relu, [pb], [sgb])
                        tt(P, "dve", mg[:, j, t0:t0 + tn], sg[:, 0:tn], sg[:, 0:tn], ALU.mult, [sgb], [mgb[j][ti]])
            jobs.append((load, comp))
        for s_ in range(8):
            def load(s_=s_, hblk=hblk):
                return ws.load(W["w_down"], hblk * 2048, 16, s_ * 256, 256)

            def comp(hd, s_=s_):
                wt, wb = hd
                for jj in range(2):
                    j = s_ * 2 + jj
                    for ti, (t0, tn, segs) in enumerate(tiles):
                        ps, pb = P.psum_next()
                        for kc in range(16):
                            mm(P, ps[:, 0:tn], wt[:, kc, jj * 128:(jj + 1) * 128], mg[:, kc, t0:t0 + tn], kc == 0, kc == 15,
                               [wb, mgb[kc][ti]], [pb])
                        for (s0, sn, mi) in segs:
                            stt(P, "dve", x[:, j, s0:s0 + sn], ps[:, s0 - t0:s0 - t0 + sn], md[:, 5, j, mi:mi + 1],
                                x[:, j, s0:s0 + sn], ALU.mult, ALU.add, [pb, xb[j][ti], mdb], [xb[j][ti]])
            jobs.append((load, comp))
    pipeline(jobs, 3)
    emit_ln(P, S, C, x, xb, x, xb, tiles, lambda c, mi: lnp[:, 2, c:c + 1], lambda c, mi: lnp[:, 3, c:c + 1], extra=[lnpb])
    for c in range(16):
        P.dma("sp", out_dst[:, c, 0:T], x[:, c, :], reads=xb[c], writes=[out_buf])
    stp.close()


class Chunked:
    def __init__(self, aps, rows_per):
        self.aps, self.rows_per = aps, rows_per

    def rows(self, r0, n, slot=0):
        k, o = r0 // self.rows_per, r0 % self.rows_per
        assert o + n <= self.rows_per
        return self.aps[k][slot * self.rows_per + o: slot * self.rows_per + o + n, :]


def emit_sel(P, BL, items):
    st = contextlib.ExitStack()
    P.phase()
    n = 1152
    ND = len(items)
    ta = [(P.sbuf("sela", [128, n], BF16, st), Buf()) for _ in range(ND)]
    tb = [(P.sbuf("selb", [128, n], BF16, st), Buf()) for _ in range(ND)]
    for k, (dst, srcA, srcB, rbuf, wbuf) in enumerate(items):
        P.dma("sp", ta[k][0][:], srcA, reads=[rbuf[0]], writes=[ta[k][1]])
        P.dma("act", tb[k][0][:], srcB, reads=[rbuf[1]], writes=[tb[k][1]])
    for k, (dst, srcA, srcB, rbuf, wbuf) in enumerate(items):
        a, ab_ = ta[k]
        b, bb_ = tb[k]
        act(P, b[:], b[:], AF.Identity, [bb_, BL["b"]], [bb_], bias=BL["z"], scale=BL["m1"])
        stt(P, "dve", a[:], a[:], BL["m0"], b[:], ALU.mult, ALU.add, [ab_, bb_, BL["b"]], [ab_])
        P.dma("sp", dst, a[:], reads=[ab_], writes=[wbuf])
    st.close()


S5P_SHAPES = {"lre2": [128, 64], "lim2": [128, 64], "lst2": [128, 64], "bb1": [128, 64, 16], "bb2": [128, 64, 16],
              "cc1": [128, 64, 16], "cc2": [128, 64, 16], "dskp": [128, 32]}
WNAMES = {"w_in": [2048, 6144], "w_glu": [1024, 1024], "w_ps": [1024, 2048], "w_pf": [1024, 2048], "w_o": [2048, 2048],
          "w_up": [2048, 8192], "w_down": [8192, 2048]}


def build_fused():
    nc = new_nc()
    condT = din(nc, "condT", [128, 16, 3])
    wmod = din(nc, "w_mod", [DEPTH, 2048, 3072])
    bmodT = din(nc, "bmodT", [128, DEPTH, 24])
    oh_d = din(nc, "onehot", [128, 2])
    xT = din(nc, "xT", [128, 16, 1152])
    posT = din(nc, "posT", [128, 16, 1024])
    lnp_d = din(nc, "lnp", [128, DEPTH, 4, 16])
    Wd = {n: din(nc, n, [DEPTH] + shp) for n, shp in WNAMES.items()}
    s5p = {n: din(nc, "s5_" + n, [DEPTH] + shp) for n, shp in S5P_SHAPES.items()}
    sc_d = {"evec": din(nc, "evec", [128, 3, 2, 8]), "pvec": din(nc, "pvec", [128, 4]), "masks": din(nc, "masks", [128, 2, 128]),
            "ident": din(nc, "ident", [128, 128]), "jv": din(nc, "jv", [128, NKK])}
    dftc_d = din(nc, "dftc", [128, 2, 512], BF16)
    dpl_d = din(nc, "dpl", [SEQ, 2, SEQ], BF16)
    dpc_d = din(nc, "dpc", [128, 2, 2, 256], BF16)
    msk_d = din(nc, "msk", [128, 3])
    outT = dout(nc, "outT", [128, 16, 1024])

    def dram(name, shape, dt=BF16):
        return nc.dram_tensor(name, shape, dt).ap()
    def chunked(name, nchunk, rows_per, cols, mult):
        return Chunked([dram(f"{name}_{k}", [mult * rows_per, cols]) for k in range(nchunk)], rows_per)
    U_loc = [chunked(f"U_loc{l}", 4, 512, 1152, 1) for l in range(DEPTH)]
    Ug = [chunked(f"Ug{l}", 4, 512, 1152, 2) for l in range(DEPTH)]
    Usel = [dram(f"Usel{i}", [2048, 1152]) for i in range(2)]
    Yf_loc = [chunked(f"Yf_loc{l}", 2, 256, NTOK, 1) for l in range(DEPTH)]
    Yfg = [chunked(f"Yfg{l}", 2, 256, NTOK, 2) for l in range(DEPTH)]
    Ys_loc = [chunked(f"Ys_loc{l}", 2, 2048, NKK, 1) for l in range(DEPTH)]
    Ysg = [chunked(f"Ysg{l}", 2, 2048, NKK, 2) for l in range(DEPTH)]

    def gather(loc, g, rb, wb):
        for k, (a, o) in enumerate(zip(loc.aps, g.aps)):
            P.collective("AllGather", [a], [o], GROUPS, reads=[rb], writes=[wb[k] if isinstance(wb, list) else wb])
    X0 = dram("X0", [128, 16, 1152], F32)
    X1 = dram("X1", [128, 16, 1152], F32)
    Hs = dram("Hs", [128, 16, 1152])
    P = Prog(nc)
    P.psum_init()
    C = make_consts(P, nc)
    md = [{"t": P.sbuf("md", [128, 96, 2], F32), "one": P.sbuf("onep", [128, 2, 16, 2], F32), "b": Buf("md")} for _ in range(DEPTH)]
    lnp = P.sbuf("lnp", [128, DEPTH, 4, 16], F32)
    lnpb = Buf("lnp")
    P.dma("sp", lnp[:], lnp_d, writes=[lnpb])
    SC = {"b": Buf("s5c")}
    for n, d_ in sc_d.items():
        SC[n] = P.sbuf(n, list(d_.shape), F32)
        P.dma("act", SC[n][:], d_, writes=[SC["b"]])
    FC = {"b": Buf("fc"), "dpl_d": dpl_d}
    FC["dftc"] = P.sbuf("dftc", [128, 2, 512], BF16)
    FC["dpc"] = P.sbuf("dpc", [128, 2, 2, 256], BF16)
    P.dma("act", FC["dftc"][:], dftc_d, writes=[FC["b"]])
    P.dma("act", FC["dpc"][:], dpc_d, writes=[FC["b"]])
    msk = P.sbuf("msk", [128, 3], F32)
    BL = {"m0": msk[:, 0:1], "m1": msk[:, 1:2], "z": msk[:, 2:3], "b": Buf("msk")}
    P.dma("sp", msk[:], msk_d, writes=[BL["b"]])
    ext = Buf("ext")
    ws = WStream(P, 3)
    k0_all(P, ws, nc, condT, wmod, bmodT, oh_d, md, BL["z"])
    bH, bX0, bX1, bout = Buf("H"), Buf("X0"), Buf("X1"), Buf("out")
    bU = [Buf("Usel0"), Buf("Usel1")]
    GROUPS = [[0, 1], [2, 3], [4, 5], [6, 7]]
    for l in range(DEPTH):
        last = l == DEPTH - 1
        W = {n: Wd[n][l] for n in WNAMES}
        bUl, bUg, bYfl, bYfg, bYsl, bYsg = (Buf(f"{n}{l}") for n in ("Ul", "Ug", "Yfl", "Yfg", "Ysl", "Ysg"))
        if l == 0:
            emit_ka(P, ws, C, xT, ext, posT, X0, bX0, md[l], W["w_in"], U_loc[l], bUl, Hs, bH)
        else:
            emit_ka(P, ws, C, X1, bX1, None, None, None, md[l], W["w_in"], U_loc[l], bUl, Hs, bH)
        bUgk = [Buf(f"Ug{l}_{k}") for k in range(4)]
        gather(U_loc[l], Ug[l], bUl, bUgk)
        items = []
        for ph in range(2):
            for base in (0, 1024):
                for j in range(4):
                    r0 = base + j * 128
                    items.append((Usel[ph][r0:r0 + 128, :], Ug[l].rows(r0, 128, slot=ph), Ug[l].rows(r0 + 512, 128, slot=ph),
                                  (bUgk[r0 // 512], bUgk[r0 // 512 + 1]), bU[ph]))
        emit_sel(P, BL, items)
        emit_kf(P, FC, Usel, bU, Yf_loc[l], bYfl, 0, not last)
        emit_ks(P, SC, Usel, bU, {n: s5p[n][l] for n in S5P_SHAPES}, Ys_loc[l], bYsl, 0, None,
                pre=lambda l=l, bYfl=bYfl, bYfg=bYfg: gather(Yf_loc[l], Yfg[l], bYfl, bYfg))
        gather(Ys_loc[l], Ysg[l], bYsl, bYsg)
        Yf_all, Ys_all = Yfg[l], Ysg[l]
        if not last:
            emit_kc(P, ws, C, Hs, bH, X0, bX0, md[l], lnp[:, l], lnpb, Ys_all, [bYsg, bYsg], Yf_all, [bYfg, bYfg], W,
                    X1, bX1, None, 1152, blend=BL)
        else:
            emit_kc(P, ws, C, Hs, bH, X1, bX1, md[l], lnp[:, l], lnpb, Ys_all, [bYsg, bYsg], Yf_all, [bYfg, bYfg], W,
                    outT, bout, None, 1024, blend=BL)
    P.finish()
    return nc


_NC = {}


def kernel(**inp):
    inp = {k: np.asarray(v) for k, v in inp.items()}
    x, ctx = inp["x"], inp["ctx"]
    pos = pos_table()
    fc = fnet_consts()
    sc = s5_consts()
    if "nc" not in _NC:
        _NC["nc"] = build_fused()
    posT = [fm(pos[p * 1024:(p + 1) * 1024]) for p in range(2)]
    lnp = np.stack([np.stack([inp["ln1_g"][l], inp["ln1_b"][l], inp["ln2_g"][l], inp["ln2_b"][l]]) for l in range(DEPTH)])
    lnp = np.ascontiguousarray(lnp.reshape(DEPTH, 4, 16, 128).transpose(3, 0, 1, 2))
    bmodT = np.ascontiguousarray(inp["b_mod"].reshape(DEPTH, 96, 128).transpose(2, 0, 1))
    s5 = []
    for p in range(2):
        hp = [host_s5_params(inp, l, p) for l in range(DEPTH)]
        s5.append({"s5_" + n: np.ascontiguousarray(np.stack([hp[l][n] for l in range(DEPTH)])) for n in S5P_SHAPES})
    shared = {"lnp": lnp, **{n: inp[n] for n in WNAMES}, **sc, **fc}
    maps = []
    for i in range(NCORES):
        b, r = i // 2, i % 2
        k, bl_ = (i // 4) * 2 + r, (i % 4) // 2
        onehot = np.zeros((128, 2), np.float32)
        onehot[:, i // 4] = 1.0
        cond = np.stack([inp["c"][bl_], inp["c"][bl_ + 2], inp["c_ctx"]])
        condT = np.ascontiguousarray(cond.reshape(3, 16, 128).transpose(2, 1, 0))
        wm = np.ascontiguousarray(inp["w_mod"][:, :, 3072 * k:3072 * (k + 1)])
        bm = np.ascontiguousarray(bmodT[:, :, 24 * k:24 * (k + 1)])
        xt = fm(np.concatenate([x[b, r * 1024:(r + 1) * 1024], ctx[b, r * 128:(r + 1) * 128]], axis=0))
        msk = np.ascontiguousarray(np.broadcast_to(np.array([1.0 - r, float(r), 0.0], np.float32)[None], (128, 3)))
        maps.append({"w_mod": wm, "bmodT": bm, "onehot": onehot, "condT": condT, "xT": xt, "posT": posT[r], "msk": msk, **s5[r], **shared})
    res = run(_NC["nc"], maps)
    out = np.zeros((NB, SEQ, D), np.float32)
    for i in range(NCORES):
        b, p = i // 2, i % 2
        out[b, p * 1024:(p + 1) * 1024] = unfm(res[i]["outT"])
    return out
```
